# Optimizing a Trainium2 kernel written in Bass

```python
import jax, jax.numpy as jnp
from jax import lax
import numpy as np

D_MODEL = 2048
BATCH = 2
SEQ = 4096
DEPTH = 4

MEM_LEN = 256
HEAD_DIM = 128
GRID_W = 64
NA_HEADS = 6
NA_ROWS = 8
NA_COLS = 16
RET_HEADS = 4
RET_DK = 128
RET_DV = 256
RET_CHUNK = 128
ROPE_BASE = 10000.0
DIL_HEADS = 6
DIL_PAIRS = ((128, 1), (512, 4), (2048, 16))
DIL_BLOCK = 128
T5_BUCKETS = 32
T5_MAX_DIST = 1024
CROSS_HEADS = 4
D_FF = 4 * D_MODEL
EPS = 1e-6
NEG = -1e30

NA_W = NA_HEADS * HEAD_DIM
RET_QK_W = RET_HEADS * RET_DK
RET_V_W = RET_HEADS * RET_DV
DIL_W = DIL_HEADS * HEAD_DIM
CROSS_W = CROSS_HEADS * HEAD_DIM
MIX_W = NA_W + RET_V_W + DIL_W
IN_SPLITS = (NA_W, NA_W, NA_W, RET_QK_W, RET_QK_W, RET_V_W, RET_V_W,
             DIL_W, DIL_W, DIL_W, D_MODEL, D_MODEL, D_MODEL)
IN_W = sum(IN_SPLITS)

kernel_name = "hybrid_natten_retention_dilated_encoder"


def rms_norm(x, g):
    xf = x.astype(jnp.float32)
    y = xf * lax.rsqrt(jnp.mean(jnp.square(xf), axis=-1, keepdims=True) + EPS)
    return (y * g.astype(jnp.float32)).astype(x.dtype)


def split_heads(t, h):
    b, s, w = t.shape
    return t.reshape(b, s, h, w // h).transpose(0, 2, 1, 3)


def merge_heads(t):
    b, h, s, d = t.shape
    return t.transpose(0, 2, 1, 3).reshape(b, s, h * d)


def t5_bucket(rel):
    nb = T5_BUCKETS // 2
    ret = (rel > 0).astype(np.int32) * nb
    n = np.abs(rel)
    max_exact = nb // 2
    large = max_exact + (np.log(np.maximum(n, 1) / max_exact) / np.log(T5_MAX_DIST / max_exact)
                         * (nb - max_exact)).astype(np.int32)
    large = np.minimum(large, nb - 1)
    return (ret + np.where(n < max_exact, n, large)).astype(np.int32)


def neighbourhood_attention(q, k, v, rpb):
    b, h, s, hd = q.shape
    rows = s // GRID_W
    wr = min(NA_ROWS, rows)
    qg = q.reshape(b, h, rows, GRID_W, hd)
    kg = k.reshape(b, h, rows, GRID_W, hd)
    vg = v.reshape(b, h, rows, GRID_W, hd)
    c = np.arange(GRID_W)
    c0 = np.clip(c - NA_COLS // 2, 0, GRID_W - NA_COLS)
    col_idx = c0[:, None] + np.arange(NA_COLS)[None, :]
    col_off = col_idx - c[:, None] + (NA_COLS - 1)
    scale = hd ** -0.5

    def row_block(r):
        r0 = jnp.clip(r - wr // 2, 0, rows - wr)
        q_r = lax.dynamic_index_in_dim(qg, r, axis=2, keepdims=False)
        k_r = lax.dynamic_slice_in_dim(kg, r0, wr, axis=2)[:, :, :, col_idx]
        v_r = lax.dynamic_slice_in_dim(vg, r0, wr, axis=2)[:, :, :, col_idx]
        row_off = r0 + jnp.arange(wr) - r + (NA_ROWS - 1)
        bias = rpb[:, row_off][:, :, col_off].transpose(0, 2, 1, 3)
        logits = (jnp.einsum('bhcd,bhrcjd->bhcrj', q_r, k_r).astype(jnp.float32) * scale
                  + bias[None].astype(jnp.float32))
        p = jax.nn.softmax(logits.reshape(b, h, GRID_W, wr * NA_COLS), axis=-1)
        p = p.reshape(logits.shape).astype(v.dtype)
        return jnp.einsum('bhcrj,bhrcjd->bhcd', p, v_r)

    out = lax.map(row_block, jnp.arange(rows))
    return out.transpose(1, 2, 0, 3, 4).reshape(b, h, s, hd)


def dilated_attention(q, k, v, t5_table):
    b, h, s, hd = q.shape
    scale = hd ** -0.5
    nblk = s // DIL_BLOCK
    groups = []
    for (w, d) in DIL_PAIRS:
        half = w // (2 * d)
        offs = d * np.arange(-half, half + 1)
        bias = t5_table[t5_bucket(offs)].T.astype(jnp.float32)
        groups.append((offs, bias))

    def q_block(i):
        t = i * DIL_BLOCK + jnp.arange(DIL_BLOCK)
        q_b = lax.dynamic_slice_in_dim(q, i * DIL_BLOCK, DIL_BLOCK, axis=2)
        outs, lses = [], []
        for offs, bias in groups:
            idx = t[:, None] + offs[None, :]
            valid = (idx >= 0) & (idx < s)
            idx = jnp.clip(idx, 0, s - 1)
            k_g = jnp.take(k, idx, axis=2)
            v_g = jnp.take(v, idx, axis=2)
            logits = (jnp.einsum('bhqd,bhqkd->bhqk', q_b, k_g).astype(jnp.float32) * scale
                      + bias[None, :, None, :])
            logits = jnp.where(valid[None, None], logits, NEG)
            lse = jax.nn.logsumexp(logits, axis=-1)
            p = jnp.exp(logits - lse[..., None]).astype(v.dtype)
            outs.append(jnp.einsum('bhqk,bhqkd->bhqd', p, v_g))
            lses.append(lse)
        alpha = jax.nn.softmax(jnp.stack(lses, axis=0), axis=0)
        return jnp.einsum('gbhq,gbhqd->bhqd', alpha.astype(v.dtype), jnp.stack(outs, axis=0))

    out = lax.map(q_block, jnp.arange(nblk))
    return out.transpose(1, 2, 0, 3, 4).reshape(b, h, s, hd)


def rotary(x):
    s, d = x.shape[2], x.shape[3]
    inv_freq = jnp.asarray((ROPE_BASE ** (-np.arange(0, d, 2, dtype=np.float32) / d)).astype(np.float32))
    ang = jnp.arange(s, dtype=jnp.float32)[:, None] * inv_freq[None, :]
    cos, sin = jnp.cos(ang), jnp.sin(ang)
    x1, x2 = x[..., : d // 2], x[..., d // 2:]
    return jnp.concatenate([x1 * cos - x2 * sin, x1 * sin + x2 * cos], axis=-1)


def retention_dir(q, k, v, log_g, include_diag):
    b, h, s, dk = q.shape
    dv = v.shape[-1]
    n = s // RET_CHUNK
    j = jnp.arange(RET_CHUNK, dtype=jnp.float32)
    diff = j[:, None] - j[None, :]
    keep = (diff >= 0) if include_diag else (diff > 0)
    dmask = jnp.where(keep, jnp.exp(jnp.where(keep, diff, 0.0) * log_g[:, None, None]), 0.0)
    xi = jnp.exp((j + 1.0) * log_g[:, None])
    zeta = jnp.exp((RET_CHUNK - 1.0 - j) * log_g[:, None])
    g_c = jnp.exp(RET_CHUNK * log_g)

    def chunks(t):
        return t.reshape(b, h, n, RET_CHUNK, t.shape[-1]).transpose(2, 0, 1, 3, 4)

    def step(state, inp):
        qi, ki, vi = inp
        inner = jnp.einsum('bhqd,bhkd->bhqk', qi, ki) * dmask[None]
        o = (jnp.einsum('bhqk,bhkv->bhqv', inner, vi)
             + jnp.einsum('bhqd,bhdv->bhqv', qi, state) * xi[None, :, :, None])
        state = (state * g_c[None, :, None, None]
                 + jnp.einsum('bhkd,bhkv->bhdv', ki, vi * zeta[None, :, :, None]))
        return state, o

    state0 = jnp.zeros((b, h, dk, dv), jnp.float32)
    _, o = lax.scan(step, state0, (chunks(q), chunks(k), chunks(v)))
    return o.transpose(1, 2, 0, 3, 4).reshape(b, h, s, dv)


def retention_mixer(q, k, v, gate, decay_w):
    qh = rotary(split_heads(q, RET_HEADS).astype(jnp.float32))
    kh = rotary(split_heads(k, RET_HEADS).astype(jnp.float32)) * (RET_DK ** -0.5)
    vh = split_heads(v, RET_HEADS).astype(jnp.float32)
    log_g = -jnp.exp(decay_w.astype(jnp.float32))
    fwd = retention_dir(qh, kh, vh, log_g[0], True)
    flip = lambda t: jnp.flip(t, axis=2)
    bwd = flip(retention_dir(flip(qh), flip(kh), flip(vh), log_g[1], False))
    y = fwd + bwd
    mu = jnp.mean(y, axis=-1, keepdims=True)
    var = jnp.mean(jnp.square(y - mu), axis=-1, keepdims=True)
    y = (y - mu) * lax.rsqrt(var + EPS)
    return jax.nn.silu(gate) * merge_heads(y).astype(gate.dtype)


def cross_attention(xn, mn, w_q, w_kv, w_o):
    q = split_heads(xn @ w_q, CROSS_HEADS)
    k, v = jnp.split(mn @ w_kv, 2, axis=-1)
    k, v = split_heads(k, CROSS_HEADS), split_heads(v, CROSS_HEADS)
    logits = jnp.einsum('bhqd,bhkd->bhqk', q, k).astype(jnp.float32) * (HEAD_DIM ** -0.5)
    p = jax.nn.softmax(logits, axis=-1).astype(v.dtype)
    return merge_heads(jnp.einsum('bhqk,bhkd->bhqd', p, v)) @ w_o


def setup_inputs(seed: int = 0) -> dict:
    key = jax.random.key(seed)
    ks = jax.random.split(key, 20)
    f32 = jnp.float32

    def dense(k, shape, fan_in):
        return jax.random.normal(k, shape, f32) * (fan_in ** -0.5)

    def gain(k, shape):
        return 1.0 + 0.02 * jax.random.normal(k, shape, f32)

    hh = np.arange(RET_HEADS, dtype=np.float32)
    w0 = np.log(-np.log(1.0 - 2.0 ** (-5.0 - hh))).astype(np.float32)
    ret_decay = jnp.asarray(w0)[None, None, :] + 0.05 * jax.random.normal(ks[5], (DEPTH, 2, RET_HEADS), f32)
    return {
        "x": jax.random.normal(ks[0], (BATCH, SEQ, D_MODEL), f32),
        "mem": jax.random.normal(ks[1], (BATCH, MEM_LEN, D_MODEL), f32),
        "t5_bias": 0.1 * jax.random.normal(ks[2], (T5_BUCKETS, DIL_HEADS), f32),
        "norm_mix_g": gain(ks[3], (DEPTH, D_MODEL)),
        "w_in": dense(ks[4], (DEPTH, D_MODEL, IN_W), D_MODEL),
        "na_rpb": 0.1 * jax.random.normal(ks[6], (DEPTH, NA_HEADS, 2 * NA_ROWS - 1, 2 * NA_COLS - 1), f32),
        "ret_decay": ret_decay,
        "w_branch": dense(ks[7], (DEPTH, MIX_W, D_MODEL), NA_W),
        "w_out": dense(ks[8], (DEPTH, D_MODEL, D_MODEL), D_MODEL),
        "norm_cross_g": gain(ks[9], (DEPTH, D_MODEL)),
        "norm_mem_g": gain(ks[10], (DEPTH, D_MODEL)),
        "w_cq": dense(ks[11], (DEPTH, D_MODEL, CROSS_W), D_MODEL),
        "w_ckv": dense(ks[12], (DEPTH, D_MODEL, 2 * CROSS_W), D_MODEL),
        "w_co": dense(ks[13], (DEPTH, CROSS_W, D_MODEL), CROSS_W),
        "norm_mlp_g": gain(ks[14], (DEPTH, D_MODEL)),
        "w_mlp1": dense(ks[15], (DEPTH, D_MODEL, D_FF), D_MODEL),
        "w_mlp2": dense(ks[16], (DEPTH, D_FF, D_MODEL), D_FF),
        "final_norm_g": gain(ks[17], (D_MODEL,)),
    }


def reference(x, mem, t5_bias, norm_mix_g, w_in, na_rpb, ret_decay, w_branch, w_out,
              norm_cross_g, norm_mem_g, w_cq, w_ckv, w_co, norm_mlp_g, w_mlp1, w_mlp2,
              final_norm_g):
    split_at = [int(o) for o in np.cumsum(IN_SPLITS)[:-1]]
    for l in range(DEPTH):
        xn = rms_norm(x, norm_mix_g[l])
        u = xn @ w_in[l]
        (qa, ka, va, qb, kb, vb, g_ret, qc, kc, vc,
         s_a, s_b, s_c) = jnp.split(u, split_at, axis=-1)
        o_a = merge_heads(neighbourhood_attention(split_heads(qa, NA_HEADS), split_heads(ka, NA_HEADS),
                                                  split_heads(va, NA_HEADS), na_rpb[l]))
        o_b = retention_mixer(qb, kb, vb, g_ret, ret_decay[l])
        o_c = merge_heads(dilated_attention(split_heads(qc, DIL_HEADS), split_heads(kc, DIL_HEADS),
                                            split_heads(vc, DIL_HEADS), t5_bias))
        wb = w_branch[l]
        merged = (jax.nn.sigmoid(s_a) * (o_a @ wb[:NA_W])
                  + jax.nn.sigmoid(s_b) * (o_b @ wb[NA_W:NA_W + RET_V_W])
                  + jax.nn.sigmoid(s_c) * (o_c @ wb[NA_W + RET_V_W:]))
        x = x + merged @ w_out[l]
        x = x + cross_attention(rms_norm(x, norm_cross_g[l]), rms_norm(mem, norm_mem_g[l]),
                                w_cq[l], w_ckv[l], w_co[l])
        xn = rms_norm(x, norm_mlp_g[l])
        x = x + jnp.square(jax.nn.relu(xn @ w_mlp1[l])) @ w_mlp2[l]
    return rms_norm(x, final_norm_g)
```

```python
import numpy as np
import ml_dtypes
from contextlib import ExitStack
import concourse.bass as bass
import concourse.mybir as mybir
from concourse.bass_utils import run_bass_kernel_spmd

F32 = mybir.dt.float32
BF16 = mybir.dt.bfloat16
AF = mybir.ActivationFunctionType
ALU = mybir.AluOpType

NCORES = 8
D = 2048
SEQ = 4096
T = 1024
NT = 8
KC = 16
DEPTH = 4
EPS = 1e-6
NEG = -30000.0
HD = 128
INV_SQRT_HD = HD ** -0.5
C_QA, C_KA, C_VA = 0, 768, 1536
C_QB, C_KB, C_VB, C_GR = 2304, 2816, 3328, 4352
C_QC, C_KC, C_VC = 5376, 6144, 6912
C_SA, C_SB, C_SC = 7680, 9728, 11776
SLOT_ELEMS = 8192
DIL_STRIP = 2944
OVERLAP_CC = True


class Buf:
    __slots__ = ("name", "lastw", "readers", "dsem", "wdma", "persistent")

    def __init__(self, name, dsem=None, persistent=True):
        self.name = name
        self.lastw = []
        self.readers = {}
        self.dsem = dsem
        self.wdma = False
        self.persistent = persistent


class DmaSem:
    def __init__(self, name, step=16):
        self.name = name
        self.count = 0
        self.handle = None
        self.step = step


class Op:
    __slots__ = ("eng", "fn", "waits", "inc", "val", "dsem", "noevent")

    def __init__(self, eng, fn):
        self.eng = eng
        self.fn = fn
        self.waits = []
        self.inc = False
        self.val = None
        self.dsem = None
        self.noevent = False


ENGS = ("pe", "act", "dve", "pool", "sp")


class Prog:
    def __init__(self):
        self.ops = {e: [] for e in ENGS}
        self.dsems = []
        self.bar_events = []
        self.bar_passed = {e: True for e in ENGS}

    def dsem(self, name, step=16):
        d = DmaSem(name, step)
        self.dsems.append(d)
        return d

    def barrier(self):
        ev = []
        for e in ("pe", "act", "dve", "pool"):
            for o in reversed(self.ops[e]):
                if o.dsem is None and not o.noevent:
                    ev.append(o)
                    break
        for d in self.dsems:
            if d.count > 0:
                ev.append((d, d.count))
        self.bar_events = ev
        self.bar_passed = {e: False for e in ENGS}

    def op(self, eng, fn, reads=(), writes=(), dma=False):
        o = Op(eng, fn)
        raw, other = [], []
        if not self.bar_passed[eng] and any(not b.persistent for b in list(reads) + list(writes)):
            self.bar_passed[eng] = True
            for d in self.bar_events:
                if isinstance(d, Op) and d.eng == eng and not dma:
                    continue
                raw.append(d)
        for b in reads:
            raw.extend(b.lastw)
        for b in writes:
            if not (dma and b.wdma and not b.readers):
                other.extend(b.lastw)
            other.extend(b.readers.values())
        for d in raw:
            if isinstance(d, Op):
                if d.eng == eng and eng == "pe":
                    continue
                d.inc = True
            o.waits.append(d)
        for d in other:
            if isinstance(d, Op):
                if d.eng == eng and not dma:
                    continue
                d.inc = True
            o.waits.append(d)
        if dma:
            assert len(writes) == 1
            b = writes[0]
            ds = b.dsem
            assert ds is not None, b.name
            ds.count += ds.step
            o.dsem = ds
            ev = (ds, ds.count)
            if b.wdma and not b.readers:
                b.lastw = b.lastw + [ev]
            else:
                b.lastw = [ev]
            b.readers = {}
            b.wdma = True
            for r in reads:
                r.readers[ds.name] = ev
        else:
            for b in writes:
                b.lastw = [o]
                b.readers = {}
                b.wdma = False
            for r in reads:
                if r not in writes:
                    r.readers[eng] = o
        self.ops[eng].append(o)
        return o

    def finalize(self):
        for e in ENGS:
            c = 0
            for o in self.ops[e]:
                if o.inc and o.dsem is None:
                    c += 1
                    o.val = c

    def emit_engine(self, e, eng, esems):
        waited = {}
        for o in self.ops[e]:
            for d in o.waits:
                if isinstance(d, Op):
                    key, val, sem = d.eng, d.val, esems[d.eng]
                else:
                    key, val, sem = d[0].name, d[1], d[0].handle
                if waited.get(key, -1) >= val:
                    continue
                waited[key] = val
                eng.wait_ge(sem, val)
            ins = o.fn(eng)
            if o.dsem is not None:
                ins.then_inc(o.dsem.handle, o.dsem.step)
            elif o.inc:
                ins.then_inc(esems[e], 1)
        if e == "sp":
            for d in self.dsems:
                if d.count > 0 and waited.get(d.name, -1) < d.count:
                    eng.wait_ge(d.handle, d.count)


def _dt_size(dt):
    return 4 if dt == F32 else 2


class Tn:
    __slots__ = ("ap", "b")

    def __init__(self, ap, b):
        self.ap = ap
        self.b = b

    def __getitem__(self, k):
        return self.ap[k]


class Builder:
    def __init__(self, mode, final_norm=False, dbg=None):
        self.mode = mode
        self.final_norm = final_norm
        self.dbg = dbg
        self.nc = bass.Bass("TRN2", target_bir_lowering=False)
        self.P = Prog()
        self.es = ExitStack()
        self.din = {}
        self.dout = {}
        self.ndsem = 0
        self.dsem_by_name = {}
        self.cc_n = 0
        self.rr = 0
        self.evt = 0

    def dram_in(self, name, shape, dtype=F32):
        ap = self.nc.dram_tensor(name, list(shape), dtype, kind="ExternalInput").ap()
        self.din[name] = ap
        return ap

    def dram_out(self, name, shape, dtype=F32):
        ap = self.nc.dram_tensor(name, list(shape), dtype, kind="ExternalOutput").ap()
        t = Tn(ap, Buf(name, self.new_dsem(name)))
        self.dout[name] = t
        return t

    def new_dsem(self, name):
        if name not in self.dsem_by_name:
            self.ndsem += 1
            self.dsem_by_name[name] = self.P.dsem("d%d_%s" % (self.ndsem, name))
        return self.dsem_by_name[name]

    def sb(self, name, shape, dtype, dma=False, sem=None):
        h = self.es.enter_context(self.nc.sbuf_tensor("sb_" + name, list(shape), dtype))
        ds = sem if sem is not None else (self.new_dsem(name) if dma else None)
        return Tn(h[:] if len(shape) == 2 else h[tuple([slice(None)] * len(shape))], Buf(name, ds, True))

    def arena_reset(self, keep=0):
        self.aoff = keep
        self.P.barrier()

    def ar_at(self, name, shape, dtype, off, dma=False):
        save = self.aoff
        self.aoff = off
        t = self.ar(name, shape, dtype, dma=dma)
        self.aoff = save
        return t

    def ar(self, name, shape, dtype, dma=False, sem=None):
        n = int(np.prod(shape[1:])) * _dt_size(dtype)
        n = (n + 63) // 64 * 64
        assert self.aoff + n <= self.arena_bytes, (name, self.aoff, n)
        ap = self.arena[:, self.aoff // 4:(self.aoff + n) // 4]
        self.aoff += n
        if dtype != F32:
            ap = ap.bitcast(dtype)
        ne = int(np.prod(shape[1:]))
        ap = ap[:, 0:ne]
        if len(shape) == 3:
            ap = ap.rearrange("p (a b) -> p a b", b=shape[2])
        elif len(shape) == 4:
            ap = ap.rearrange("p (a b c) -> p a b c", b=shape[2], c=shape[3])
        ds = sem if sem is not None else (self.new_dsem(name) if dma else None)
        return Tn(ap, Buf(name, ds, False))

    def dma(self, q, out_t, out_ap, in_ap, reads=()):
        self.P.op(q, lambda e: e.dma_start(out=out_ap, in_=in_ap), reads=[r.b for r in reads], writes=[out_t.b], dma=True)

    def mm(self, ps, ps_ap, lhsT, lhsT_ap, rhs, rhs_ap, start, stop, extra_reads=()):
        rd = [lhsT.b, rhs.b] + [r.b for r in extra_reads]
        self.P.op("pe", lambda e: e.matmul(ps_ap, lhsT=lhsT_ap, rhs=rhs_ap, start=start, stop=stop), reads=rd, writes=[ps.b])

    def tr(self, ps, ps_ap, src, src_ap):
        self.P.op("pe", lambda e: e.transpose(out=ps_ap, in_=src_ap, identity=self.ident.ap), reads=[src.b, self.ident.b], writes=[ps.b])

    def act(self, out_t, out_ap, in_t, in_ap, func, reads=(), writes=(), **kw):
        self.P.op("act", lambda e: e.activation(out=out_ap, in_=in_ap, func=func, **kw), reads=[in_t.b] + [r.b for r in reads],
                  writes=[out_t.b] + [w.b for w in writes])

    def vop(self, eng, method, out_t, reads, **kw):
        self.P.op(eng, lambda e: getattr(e, method)(**kw), reads=[r.b for r in reads], writes=[out_t.b])

    def evac(self, out_t, out_ap, ps, ps_ap, scale=None):
        self.evt += 1
        if self.evt % 2 == 0:
            if scale is None:
                self.act(out_t, out_ap, ps, ps_ap, AF.Copy)
            else:
                self.act(out_t, out_ap, ps, ps_ap, AF.Copy, scale=scale)
        else:
            if scale is None:
                self.vop("dve", "tensor_copy", out_t, [ps], out=out_ap, in_=ps_ap)
            else:
                self.vop("dve", "tensor_scalar", out_t, [ps], out=out_ap, in0=ps_ap, scalar1=scale, scalar2=None, op0=ALU.mult)

    def next_acc(self):
        self.rr = (self.rr + 1) % 4
        return self.pacc[self.rr]

    def next_slot(self):
        self.slot_i = (self.slot_i + 1) % len(self.wslots)
        return self.wslots[self.slot_i]

    def load_w(self, slot, w_ap, r0, nrows, c0, ncols, col_off=0, total_cols=None, elem_off=0):
        kc = nrows // 128
        tc_ = total_cols if total_cols is not None else ncols
        assert elem_off + kc * tc_ <= SLOT_ELEMS
        view = slot.ap[:, elem_off:elem_off + kc * tc_].rearrange("p (k c) -> p k c", c=tc_)
        src = w_ap[r0:r0 + nrows, c0:c0 + ncols].rearrange("(k p) c -> p k c", p=128)
        self.dma("pool", slot, view[:, :, col_off:col_off + ncols], src)
        return view

    def setup_common(self):
        nc = self.nc
        self.arena_bytes = 100 * 1024
        self.arena = self.es.enter_context(nc.sbuf_tensor("arena", [128, self.arena_bytes // 4], F32))
        self.aoff = 0
        self.xnT = self.sb("xnT", [128, KC, T], BF16)
        self.wslots = [self.sb("wslot%d" % i, [128, SLOT_ELEMS], BF16, dma=True) for i in range(4)]
        self.slot_i = -1
        self.ident = self.sb("ident", [128, 128], BF16, dma=True)
        self.ones = self.sb("ones", [128, 128], BF16)
        self.cst = self.sb("cst", [128, CST_N], F32, dma=True)
        self.lg = self.sb("lg", [128, 8], F32, dma=True)
        self.small = self.sb("small", [128, 64], F32)
        banks = [self.es.enter_context(nc.psum_tensor("pb%d" % i, [128, 512], F32)) for i in range(8)]
        self.pacc = [Tn(banks[i][:], Buf("pacc%d" % i)) for i in range(4)]
        self.patt = [Tn(banks[4 + i][:], Buf("patt%d" % i)) for i in range(2)]
        self.ptr = [Tn(banks[6 + i][:].bitcast(BF16), Buf("ptr%d" % i)) for i in range(2)]
        self.ptr_i = 0
        d_ident = self.dram_in("ident", [128, 128])
        d_cst = self.dram_in("cst", [128, CST_N])
        self.dma("pool", self.ident, self.ident.ap, d_ident)
        self.dma("sp", self.cst, self.cst.ap, d_cst)
        self.vop("dve", "memset", self.ones, [], ap=self.ones.ap, constant=1.0)

    def next_ptr(self):
        self.ptr_i ^= 1
        return self.ptr[self.ptr_i]

    def emit_norm(self, gain_dram, src_dram=None, src_tiles=None, ntiles=NT, dstT=None, gname="g", src_t=None):
        dstT = dstT or self.xnT
        g_bc = self.ar(gname + "_bc", [128, D], F32, dma=True)
        self.dma("sp", g_bc, g_bc.ap, gain_dram.partition_broadcast(128))
        junk = self.ar(gname + "_junk", [128, D], BF16)
        xnb = [self.ar(gname + "_xnb%d" % i, [128, D], BF16) for i in range(2)]
        st = self.ar(gname + "_st", [128, 4 * ntiles], F32)
        xs = None
        if src_dram is not None:
            xs = [self.ar(gname + "_xs%d" % i, [128, D], F32, dma=True) for i in range(2)]
        for tt in range(ntiles):
            if src_dram is not None:
                xt = xs[tt % 2]
                self.dma("sp", xt, xt.ap, src_dram[tt * 128:(tt + 1) * 128, :], reads=[src_t] if src_t is not None else ())
                x_ap = xt.ap
            else:
                xt = src_tiles[tt]
                x_ap = xt.ap
            c = 4 * tt
            self.act(junk, junk.ap, xt, x_ap, AF.Square, accum_out=st.ap[:, c:c + 1], writes=[st])
            self.vop("dve", "tensor_scalar", st, [st, junk], out=st.ap[:, c + 1:c + 2], in0=st.ap[:, c:c + 1],
                     scalar1=1.0 / D, scalar2=EPS, op0=ALU.mult, op1=ALU.add)
            self.act(st, st.ap[:, c + 2:c + 3], st, st.ap[:, c + 1:c + 2], AF.Sqrt)
            self.vop("dve", "reciprocal", st, [st], out=st.ap[:, c + 3:c + 4], in_=st.ap[:, c + 2:c + 3])
            xb = xnb[tt % 2]
            self.vop("dve", "scalar_tensor_tensor", xb, [xt, st, g_bc], out=xb.ap, in0=x_ap, scalar=st.ap[:, c + 3:c + 4],
                     in1=g_bc.ap, op0=ALU.mult, op1=ALU.mult)
            for rnd in range(2):
                pt = self.next_ptr()
                for i in range(8):
                    kc = rnd * 8 + i
                    self.tr(pt, pt.ap[:, i * 128:(i + 1) * 128], xb, xb.ap[:, kc * 128:(kc + 1) * 128])
                self.evac(dstT, dstT.ap[:, rnd * 8:(rnd + 1) * 8, tt * 128:(tt + 1) * 128],
                          pt, pt.ap.rearrange("p (a b) -> p a b", b=128))

    def emit_decay(self, d_decay):
        self.dma("sp", self.lg, self.lg.ap, d_decay.partition_broadcast(128))
        self.act(self.lg, self.lg.ap, self.lg, self.lg.ap, AF.Exp)
        self.vop("dve", "tensor_scalar", self.lg, [self.lg], out=self.lg.ap, in0=self.lg.ap, scalar1=-1.0, scalar2=None, op0=ALU.mult)

    def exp_scaled(self, out_t, out_ap, in_t, in_ap, col, mul=None):
        self.act(out_t, out_ap, in_t, in_ap, AF.Exp, scale=self.lg.ap[:, col:col + 1], reads=[self.lg])
        if mul is not None:
            self.vop("dve", "tensor_scalar", out_t, [out_t], out=out_ap, in0=out_ap, scalar1=mul, scalar2=None, op0=ALU.mult)

    def build_A(self):
        nc = self.nc
        self.setup_common()
        x_d = self.dram_in("x", [T, D])
        g_d = self.dram_in("g_mix", [D])
        w_in = self.dram_in("w_in", [D, 13824])
        dec_d = self.dram_in("ret_decay", [8])
        cs_d = self.dram_in("rot_cs", [T, 128])
        nsc_d = self.dram_in("rot_nsc", [T, 128])
        o_kaT = self.dram_out("kaT", [6, 128, T], BF16)
        o_kcT = self.dram_out("kcT", [6, 128, T], BF16)
        o_va = self.dram_out("va", [T, 768], BF16)
        o_vc = self.dram_out("vc", [T, 768], BF16)
        o_L = self.dram_out("L", [2, 4, 128, 256], F32)
        self.aoff = 0
        self.part_A(x_d, None, g_d, w_in, dec_d, cs_d, nsc_d, (o_kaT, o_kcT, o_va, o_vc, o_L))

    def part_A(self, x_d, x_t, g_d, w_in, dec_d, cs_d, nsc_d, outs, mid_cb=None):
        o_kaT, o_kcT, o_va, o_vc, o_L = outs
        self.emit_decay(dec_d)
        self.emit_norm(g_d, src_dram=x_d, src_t=x_t)
        self.arena_reset()
        kT = self.ar("kT", [128, 6, T], BF16)
        for (c0, o_t) in ((C_KA, o_kaT), (C_KC, o_kcT)):
            for blk, (cb, ncol) in enumerate(((0, 512), (512, 256))):
                slot = self.next_slot()
                wv = self.load_w(slot, w_in, 0, D, c0 + cb, ncol)
                for m in range(ncol // 128):
                    h = (cb // 128) + m
                    for half in range(2):
                        ps = self.next_acc()
                        for kc in range(KC):
                            self.mm(ps, ps.ap, slot, wv[:, kc, m * 128:(m + 1) * 128], self.xnT,
                                    self.xnT.ap[:, kc, half * 512:(half + 1) * 512], kc == 0, kc == KC - 1)
                        self.evac(kT, kT.ap[:, h, half * 512:(half + 1) * 512], ps, ps.ap)
            self.store_kT(o_t, kT)
        vtm = self.ar("vtm", [128, NT, 768], BF16)
        for (c0, o_t) in ((C_VA, o_va), (C_VC, o_vc)):
            for (cb, ncol) in ((0, 512), (512, 256)):
                slot = self.next_slot()
                wv = self.load_w(slot, w_in, 0, D, c0 + cb, ncol)
                for tt in range(NT):
                    ps = self.next_acc()
                    for kc in range(KC):
                        self.mm(ps, ps.ap[:, 0:ncol], self.xnT, self.xnT.ap[:, kc, tt * 128:(tt + 1) * 128], slot,
                                wv[:, kc, :], kc == 0, kc == KC - 1)
                    self.evac(vtm, vtm.ap[:, tt, cb:cb + ncol], ps, ps.ap[:, 0:ncol])
            self.store_v(o_t, vtm)
        rot = self.load_rot(cs_d, nsc_d)
        ZF = self.ar("ZF", [128, NT, 4], F32)
        ZB = self.ar("ZB", [128, NT, 4], F32)
        for h in range(4):
            self.exp_scaled(ZF, ZF.ap[:, :, h], self.cst, self.cst.ap[:, CST_ZE:CST_ZE + 8], h, mul=INV_SQRT_HD)
            self.exp_scaled(ZB, ZB.ap[:, :, h], self.cst, self.cst.ap[:, CST_ZB:CST_ZB + 8], 4 + h, mul=INV_SQRT_HD)
        krot = self.ar("krot", [128, NT, 512], F32)
        vb = self.ar("vb", [128, NT, 1024], BF16)
        kzf = self.ar("kzf", [128, NT, 512], BF16)
        kzb = self.ar("kzb", [128, NT, 512], BF16)
        pre = []
        for c0 in (C_KB, C_VB, C_VB + 512):
            slot = self.next_slot()
            pre.append((slot, self.load_w(slot, w_in, 0, D, c0, 512)))
        if mid_cb is not None:
            mid_cb()
        slot, wv = pre[0]
        for tt in range(NT):
            ps = self.next_acc()
            for kc in range(KC):
                self.mm(ps, ps.ap, self.xnT, self.xnT.ap[:, kc, tt * 128:(tt + 1) * 128], slot, wv[:, kc, :], kc == 0, kc == KC - 1)
            self.rotary(krot, krot.ap[:, tt, :], ps, ps.ap, rot, tt, 4)
            for (zt, kz) in ((ZF, kzf), (ZB, kzb)):
                self.vop("dve", "tensor_tensor", kz, [krot, zt], out=kz.ap[:, tt, :].rearrange("p (h d) -> p h d", d=128),
                         in0=krot.ap[:, tt, :].rearrange("p (h d) -> p h d", d=128),
                         in1=zt.ap[:, tt, :].unsqueeze(2).to_broadcast([128, 4, 128]), op=ALU.mult)
        for cb in range(2):
            slot, wv = pre[1 + cb]
            for tt in range(NT):
                ps = self.next_acc()
                for kc in range(KC):
                    self.mm(ps, ps.ap, self.xnT, self.xnT.ap[:, kc, tt * 128:(tt + 1) * 128], slot, wv[:, kc, :], kc == 0, kc == KC - 1)
                self.evac(vb, vb.ap[:, tt, cb * 512:(cb + 1) * 512], ps, ps.ap)
        Ls = self.ar("Ls", [128, 2, 4, 256], F32)
        for di, kz in enumerate((kzf, kzb)):
            for h in range(4):
                ps = self.next_acc()
                for tt in range(NT):
                    self.mm(ps, ps.ap[:, 0:256], kz, kz.ap[:, tt, h * 128:(h + 1) * 128], vb, vb.ap[:, tt, h * 256:(h + 1) * 256],
                            tt == 0, tt == NT - 1)
                self.evac(Ls, Ls.ap[:, di, h, :], ps, ps.ap[:, 0:256])
        self.dma("sp", o_L, o_L.ap.rearrange("d h p v -> p d h v"), Ls.ap, reads=[Ls])

    def load_rot(self, cs_d, nsc_d):
        cs = [self.ar("rot_cs%d" % i, [128, 128], F32, dma=True) for i in range(2)]
        nsc = [self.ar("rot_nsc%d" % i, [128, 128], F32, dma=True) for i in range(2)]
        t1 = self.ar("rot_t1", [128, 512], F32)
        t2 = self.ar("rot_t2", [128, 512], F32)
        q32 = self.ar("rot_q32", [128, 512], F32)
        return (cs, nsc, t1, t2, q32, cs_d, nsc_d)

    def rotary(self, out_t, out_ap, ps, ps_ap, rot, tt, ng):
        csl, nscl, t1, t2, q32, cs_d, nsc_d = rot
        cs, nsc = csl[tt % 2], nscl[tt % 2]
        self.dma("sp", cs, cs.ap, cs_d[tt * 128:(tt + 1) * 128, :])
        self.dma("sp", nsc, nsc.ap, nsc_d[tt * 128:(tt + 1) * 128, :])
        n = ng * 128
        self.act(q32, q32.ap[:, 0:n], ps, ps_ap, AF.Copy)
        q4 = q32.ap[:, 0:n].rearrange("p (g s d) -> p g s d", s=2, d=64)
        shp = [128, ng, 2, 64]
        csb = cs.ap.rearrange("p (s d) -> p s d", d=64).unsqueeze(1).to_broadcast(shp)
        nscb = nsc.ap.rearrange("p (s d) -> p s d", d=64).unsqueeze(1).to_broadcast(shp)
        t1v = t1.ap[:, 0:n].rearrange("p (g s d) -> p g s d", s=2, d=64)
        t2v = t2.ap[:, 0:n].rearrange("p (g s d) -> p g s d", s=2, d=64)
        self.vop("dve", "tensor_tensor", t1, [q32, cs], out=t1v, in0=q4[:, :, 0, :].unsqueeze(2).to_broadcast(shp), in1=csb, op=ALU.mult)
        self.vop("dve", "tensor_tensor", t2, [q32, nsc], out=t2v, in0=q4[:, :, 1, :].unsqueeze(2).to_broadcast(shp), in1=nscb, op=ALU.mult)
        self.vop("dve", "tensor_tensor", out_t, [t1, t2], out=out_ap, in0=t1.ap[:, 0:n], in1=t2.ap[:, 0:n], op=ALU.add)


    def build_B(self, stop=None):
        self.setup_common()
        KB = 1024
        x_d = self.dram_in("x", [T, D])
        g_mix = self.dram_in("g_mix", [D])
        g_cross = self.dram_in("g_cross", [D])
        g_mem = self.dram_in("g_mem", [D])
        g_mlp = self.dram_in("g_mlp", [D])
        w_in = self.dram_in("w_in", [D, 13824])
        w_branch = self.dram_in("w_branch", [2560, D])
        w_out = self.dram_in("w_out", [D, D])
        w_cq = self.dram_in("w_cq", [D, 512])
        w_ckv = self.dram_in("w_ckv", [D, 1024])
        w_co = self.dram_in("w_co", [512, D])
        w_mlp1 = self.dram_in("w_mlp1", [D, 8192])
        w_mlp2 = self.dram_in("w_mlp2", [8192, D])
        mem_d = self.dram_in("mem", [256, D])
        dec_d = self.dram_in("ret_decay", [8])
        cs_d = self.dram_in("rot_cs", [T, 128])
        nsc_d = self.dram_in("rot_nsc", [T, 128])
        expo_d = self.dram_in("expo", [128, 16])
        biasA_d = self.dram_in("biasA", [6, 16, 128, 512])
        biasC_d = self.dram_in("biasC", [6, 128, DIL_STRIP])
        lmC_d = self.dram_in("lmC", [128, DIL_STRIP])
        vcol_d = self.dram_in("vcolC", [128, 24])
        kaT_d = self.dram_in("kaT_h", [6, 128, 1536], BF16)
        va_d = self.dram_in("va_h", [1536, 768], BF16)
        kcT_d = self.dram_in("kcT_h", [6, 128, 3072], BF16)
        vc_d = self.dram_in("vc_h", [3072, 768], BF16)
        Lall_d = self.dram_in("Lall", [2, 4, 4, 128, 256])
        x_out = self.dram_out("x_out", [T, D])
        KmT = self.sb("KmT", [128, 4, 256], BF16)
        Vm = self.sb("Vm", [128, 2, 512], BF16)
        a = dict(x_d=x_d, g_mix=g_mix, g_cross=g_cross, g_mem=g_mem, g_mlp=g_mlp, w_in=w_in, w_branch=w_branch, w_out=w_out, w_cq=w_cq,
                 w_ckv=w_ckv, w_co=w_co, w_mlp1=w_mlp1, w_mlp2=w_mlp2, mem_d=mem_d, dec_d=dec_d, cs_d=cs_d, nsc_d=nsc_d, expo_d=expo_d,
                 biasA_d=biasA_d, biasC_d=biasC_d, lmC_d=lmC_d, vcol_d=vcol_d, kaT_d=kaT_d, va_d=va_d, kcT_d=kcT_d, vc_d=vc_d,
                 Lall_d=Lall_d, x_out=x_out, KmT=KmT, Vm=Vm)
        self.part_B(a, stop=stop)

    def part_B(self, a, stop=None, norm1=True, final_g=None):
        KB = 1024
        (x_d, g_mix, g_cross, g_mem, g_mlp, w_in, w_branch, w_out, w_cq, w_ckv, w_co, w_mlp1, w_mlp2, mem_d, dec_d, cs_d, nsc_d, expo_d,
         biasA_d, biasC_d, lmC_d, vcol_d, kaT_d, va_d, kcT_d, vc_d, Lall_d, x_out, KmT, Vm) = [a[k] for k in (
            "x_d", "g_mix", "g_cross", "g_mem", "g_mlp", "w_in", "w_branch", "w_out", "w_cq", "w_ckv", "w_co", "w_mlp1", "w_mlp2", "mem_d",
            "dec_d", "cs_d", "nsc_d", "expo_d", "biasA_d", "biasC_d", "lmC_d", "vcol_d", "kaT_d", "va_d", "kcT_d", "vc_d", "Lall_d",
            "x_out", "KmT", "Vm")]
        x_t = a.get("x_t")
        halo_t = a.get("halo_t", ())
        lq = a.get("L_queue", "sp")
        xnT = self.xnT
        OT_OFF, MG_OFF, X_BYTES = 0, 68 * KB, 64 * KB

        self.aoff = 0
        self.emit_decay(dec_d)
        mnT = self.ar("mnT", [128, KC, 256], BF16)
        self.emit_norm(g_mem, src_dram=mem_d, ntiles=2, dstT=mnT, gname="gm")
        slot = self.next_slot()
        wv = self.load_w(slot, w_ckv, 0, D, 0, 512)
        for h in range(4):
            ps = self.next_acc()
            for kc in range(KC):
                self.mm(ps, ps.ap[:, 0:256], slot, wv[:, kc, h * 128:(h + 1) * 128], mnT, mnT.ap[:, kc, :], kc == 0, kc == KC - 1)
            self.evac(KmT, KmT.ap[:, h, :], ps, ps.ap[:, 0:256])
        slot = self.next_slot()
        wv = self.load_w(slot, w_ckv, 0, D, 512, 512)
        for t in range(2):
            ps = self.next_acc()
            for kc in range(KC):
                self.mm(ps, ps.ap, mnT, mnT.ap[:, kc, t * 128:(t + 1) * 128], slot, wv[:, kc, :], kc == 0, kc == KC - 1)
            self.evac(Vm, Vm.ap[:, t, :], ps, ps.ap)
        if norm1:
            self.arena_reset()
            self.emit_norm(g_mix, src_dram=x_d, src_t=x_t)
        self.arena_reset()
        oT = self.ar_at("oT", [128, 20, T], BF16, OT_OFF)
        self.aoff = 40 * KB
        self.emit_retention(w_in, cs_d, nsc_d, expo_d, Lall_d, oT, lq=lq)
        self.arena_reset(keep=40 * KB)
        self.emit_attn(w_in, C_QA, kaT_d, va_d, biasA_d, oT, 0, nkt_halo=12, nr=8, name="na", halo_t=halo_t)
        self.arena_reset(keep=40 * KB)
        self.emit_attn(w_in, C_QC, kcT_d, vc_d, biasC_d, oT, 14, nkt_halo=24, nr=20, name="dil", lm_d=lmC_d, vcol_d=vcol_d, halo_t=halo_t)
        if stop == "mix":
            o = self.dram_out("oT_dbg", [20, 128, T], BF16)
            self.dma("sp", o, o.ap.rearrange("c p t -> p c t"), oT.ap, reads=[oT])
            return
        self.arena_reset(keep=40 * KB)
        mergedT = self.ar_at("mergedT", [128, KC, T], BF16, MG_OFF)
        SIG = [self.ar("sig%d" % g, [128, 512], F32) for g in range(3)]
        M = [self.ar("mrg%d" % g, [128, 512], F32) for g in range(3)]
        for fc in range(16):
            sw = self.next_slot()
            wbv = self.load_w(sw, w_branch, 0, 2560, fc * 128, 128)
            gsl = []
            for g in range(2):
                gsl.append((sw, self.load_w(sw, w_in, 0, D, C_SA + g * 2048 + fc * 128, 128, elem_off=2560 + g * 2048)))
            s2 = self.next_slot()
            gsl.append((s2, self.load_w(s2, w_in, 0, D, C_SA + 2 * 2048 + fc * 128, 128)))
            for half in range(2):
                hs = slice(half * 512, (half + 1) * 512)
                for g in range(3):
                    ps = self.next_acc()
                    gs_, gv = gsl[g]
                    for kc in range(KC):
                        self.mm(ps, ps.ap, gs_, gv[:, kc, :], xnT, xnT.ap[:, kc, hs], kc == 0, kc == KC - 1)
                    self.act(SIG[g], SIG[g].ap, ps, ps.ap, AF.Sigmoid)
                for g, (k0, nk) in enumerate(((0, 6), (6, 8), (14, 6))):
                    ps = self.next_acc()
                    for k in range(nk):
                        self.mm(ps, ps.ap, sw, wbv[:, k0 + k, :], oT, oT.ap[:, k0 + k, hs], k == 0, k == nk - 1)
                    self.vop("dve", "tensor_tensor", M[g], [ps, SIG[g]], out=M[g].ap, in0=ps.ap, in1=SIG[g].ap, op=ALU.mult)
                self.vop("dve", "tensor_tensor", M[0], [M[0], M[1]], out=M[0].ap, in0=M[0].ap, in1=M[1].ap, op=ALU.add)
                self.vop("dve", "tensor_tensor", mergedT, [M[0], M[2]], out=mergedT.ap[:, fc, hs], in0=M[0].ap, in1=M[2].ap, op=ALU.add)
        self.arena_reset(keep=X_BYTES)
        xs = []
        for tt in range(NT):
            xt = self.ar_at("x%d" % tt, [128, D], F32, tt * 8 * KB, dma=True)
            self.dma("sp", xt, xt.ap, x_d[tt * 128:(tt + 1) * 128, :], reads=[x_t] if x_t is not None else ())
            xs.append(xt)
        for cb in range(4):
            slot = self.next_slot()
            wv = self.load_w(slot, w_out, 0, D, cb * 512, 512)
            for tt in range(NT):
                ps = self.next_acc()
                for kc in range(KC):
                    self.mm(ps, ps.ap, mergedT, mergedT.ap[:, kc, tt * 128:(tt + 1) * 128], slot, wv[:, kc, :], kc == 0, kc == KC - 1)
                xa = xs[tt].ap[:, cb * 512:(cb + 1) * 512]
                self.vop("dve", "tensor_tensor", xs[tt], [xs[tt], ps], out=xa, in0=xa, in1=ps.ap, op=ALU.add)
        if stop == "x1":
            for tt in range(NT):
                self.dma("sp", x_out, x_out.ap[tt * 128:(tt + 1) * 128, :], xs[tt].ap, reads=[xs[tt]])
            return
        self.arena_reset(keep=X_BYTES)
        self.emit_norm(g_cross, src_tiles=xs, gname="gc")
        self.arena_reset(keep=X_BYTES)
        QxT = self.ar("QxT", [128, T], BF16)
        PT = [self.ar("cPT%d" % i, [128, 512], BF16) for i in range(2)]
        rc = self.ar("crc", [128, 512], F32)
        ocT = self.ar("ocT", [128, 4, T], BF16)
        for h in range(4):
            slot = self.next_slot()
            wv = self.load_w(slot, w_cq, 0, D, h * 128, 128)
            for half in range(2):
                ps = self.next_acc()
                for kc in range(KC):
                    self.mm(ps, ps.ap, slot, wv[:, kc, :], xnT, xnT.ap[:, kc, half * 512:(half + 1) * 512], kc == 0, kc == KC - 1)
                self.evac(QxT, QxT.ap[:, half * 512:(half + 1) * 512], ps, ps.ap, scale=INV_SQRT_HD)
            for half in range(2):
                hs = slice(half * 512, (half + 1) * 512)
                num, den = self.patt
                for kt in range(2):
                    sc = self.next_acc()
                    self.mm(sc, sc.ap, KmT, KmT.ap[:, h, kt * 128:(kt + 1) * 128], QxT, QxT.ap[:, hs], True, True)
                    p_ = PT[kt]
                    self.act(p_, p_.ap, sc, sc.ap, AF.Exp)
                    self.mm(num, num.ap, Vm, Vm.ap[:, kt, h * 128:(h + 1) * 128], p_, p_.ap, kt == 0, kt == 1)
                    self.mm(den, den.ap, self.ones, self.ones.ap, p_, p_.ap, kt == 0, kt == 1)
                self.vop("dve", "reciprocal", rc, [den], out=rc.ap, in_=den.ap)
                self.vop("dve", "tensor_tensor", ocT, [num, rc], out=ocT.ap[:, h, hs], in0=num.ap, in1=rc.ap, op=ALU.mult)
        for cb in range(4):
            slot = self.next_slot()
            wv = self.load_w(slot, w_co, 0, 512, cb * 512, 512)
            for tt in range(NT):
                ps = self.next_acc()
                for k in range(4):
                    self.mm(ps, ps.ap, ocT, ocT.ap[:, k, tt * 128:(tt + 1) * 128], slot, wv[:, k, :], k == 0, k == 3)
                xa = xs[tt].ap[:, cb * 512:(cb + 1) * 512]
                self.vop("dve", "tensor_tensor", xs[tt], [xs[tt], ps], out=xa, in0=xa, in1=ps.ap, op=ALU.add)
        if stop == "x2":
            for tt in range(NT):
                self.dma("sp", x_out, x_out.ap[tt * 128:(tt + 1) * 128, :], xs[tt].ap, reads=[xs[tt]])
            return
        self.arena_reset(keep=X_BYTES)
        self.emit_norm(g_mlp, src_tiles=xs, gname="gl")
        self.arena_reset(keep=X_BYTES)
        hT = self.ar("hT", [128, KC, T], BF16)
        r32 = [self.ar("r32_%d" % i, [128, 512], F32) for i in range(2)]
        ri = 0
        for q in range(4):
            for blk in range(4):
                slot = self.next_slot()
                wv = self.load_w(slot, w_mlp1, 0, D, q * 2048 + blk * 512, 512)
                for m in range(4):
                    hc = blk * 4 + m
                    for half in range(2):
                        ps = self.next_acc()
                        for kc in range(KC):
                            self.mm(ps, ps.ap, slot, wv[:, kc, m * 128:(m + 1) * 128], xnT, xnT.ap[:, kc, half * 512:(half + 1) * 512],
                                    kc == 0, kc == KC - 1)
                        ri ^= 1
                        r_ = r32[ri]
                        self.act(r_, r_.ap, ps, ps.ap, AF.Relu)
                        self.vop("dve", "tensor_tensor", hT, [r_], out=hT.ap[:, hc, half * 512:(half + 1) * 512], in0=r_.ap, in1=r_.ap, op=ALU.mult)
            for cb in range(4):
                slot = self.next_slot()
                wv = self.load_w(slot, w_mlp2, q * 2048, 2048, cb * 512, 512)
                for tt in range(NT):
                    ps = self.next_acc()
                    for k in range(KC):
                        self.mm(ps, ps.ap, hT, hT.ap[:, k, tt * 128:(tt + 1) * 128], slot, wv[:, k, :], k == 0, k == KC - 1)
                    xa = xs[tt].ap[:, cb * 512:(cb + 1) * 512]
                    self.vop("dve", "tensor_tensor", xs[tt], [xs[tt], ps], out=xa, in0=xa, in1=ps.ap, op=ALU.add)
        if final_g is None:
            for tt in range(NT):
                self.dma("sp", x_out, x_out.ap[tt * 128:(tt + 1) * 128, :], xs[tt].ap, reads=[xs[tt]])
            return
        self.arena_reset(keep=X_BYTES)
        g_bc = self.ar("gf_bc", [128, D], F32, dma=True)
        self.dma("sp", g_bc, g_bc.ap, final_g.partition_broadcast(128))
        junk = self.ar("gf_junk", [128, D], BF16)
        st = self.ar("gf_st", [128, 4 * NT], F32)
        ys = [self.ar("gf_ys%d" % i, [128, D], F32) for i in range(2)]
        youts = [Tn(x_out.ap, Buf("y_st%d" % i, self.new_dsem("y_st%d" % i))) for i in range(2)]
        for tt in range(NT):
            xt, yt = xs[tt], ys[tt % 2]
            c = 4 * tt
            self.act(junk, junk.ap, xt, xt.ap, AF.Square, accum_out=st.ap[:, c:c + 1], writes=[st])
            self.vop("dve", "tensor_scalar", st, [st], out=st.ap[:, c + 1:c + 2], in0=st.ap[:, c:c + 1], scalar1=1.0 / D, scalar2=EPS, op0=ALU.mult, op1=ALU.add)
            self.act(st, st.ap[:, c + 2:c + 3], st, st.ap[:, c + 1:c + 2], AF.Sqrt)
            self.vop("dve", "reciprocal", st, [st], out=st.ap[:, c + 3:c + 4], in_=st.ap[:, c + 2:c + 3])
            self.vop("dve", "scalar_tensor_tensor", yt, [xt, st, g_bc], out=yt.ap, in0=xt.ap, scalar=st.ap[:, c + 3:c + 4], in1=g_bc.ap, op0=ALU.mult, op1=ALU.mult)
            self.dma("sp", youts[tt % 2], x_out.ap[tt * 128:(tt + 1) * 128, :], yt.ap, reads=[yt])

    def emit_retention(self, w_in, cs_d, nsc_d, expo_d, Lall_d, oT, lq="sp"):
        cst, small = self.cst, self.small
        rot = self.load_rot(cs_d, nsc_d)
        DT = self.ar("DT", [128, 4, 128], F32)
        XIF = self.ar("XIF", [128, 4, 128], F32)
        ZBB = self.ar("ZBB", [128, 4, 128], F32)
        tmpD = self.ar("tmpD", [128, 128], F32)
        expo = self.ar("expo", [128, 16], F32, dma=True)
        self.dma("sp", expo, expo.ap, expo_d)
        wS = self.ar("wS", [128, 2, 4, 4], F32)
        for h in range(4):
            self.exp_scaled(DT, DT.ap[:, h, :], cst, cst.ap[:, CST_PF:CST_PF + 128], h)
            self.vop("dve", "tensor_tensor", DT, [DT, cst], out=DT.ap[:, h, :], in0=DT.ap[:, h, :], in1=cst.ap[:, CST_UF:CST_UF + 128], op=ALU.mult)
            self.exp_scaled(tmpD, tmpD.ap, cst, cst.ap[:, CST_PB:CST_PB + 128], 4 + h)
            self.vop("dve", "tensor_tensor", tmpD, [tmpD, cst], out=tmpD.ap, in0=tmpD.ap, in1=cst.ap[:, CST_UB:CST_UB + 128], op=ALU.mult)
            self.vop("dve", "tensor_tensor", DT, [DT, tmpD], out=DT.ap[:, h, :], in0=DT.ap[:, h, :], in1=tmpD.ap, op=ALU.add)
            self.vop("dve", "tensor_scalar", DT, [DT], out=DT.ap[:, h, :], in0=DT.ap[:, h, :], scalar1=INV_SQRT_HD, scalar2=None, op0=ALU.mult)
            self.exp_scaled(XIF, XIF.ap[:, h, :], cst, cst.ap[:, CST_N1:CST_N1 + 128], h)
            self.exp_scaled(ZBB, ZBB.ap[:, h, :], cst, cst.ap[:, CST_N2:CST_N2 + 128], 4 + h)
            self.exp_scaled(small, small.ap[:, h:h + 1], cst, cst.ap[:, CST_M2:CST_M2 + 1], h, mul=INV_SQRT_HD)
            self.exp_scaled(small, small.ap[:, 4 + h:5 + h], cst, cst.ap[:, CST_M1:CST_M1 + 1], 4 + h, mul=INV_SQRT_HD)
            for di in range(2):
                self.exp_scaled(wS, wS.ap[:, di, h, :], expo, expo.ap[:, 8 * di:8 * di + 4], 4 * di + h)
                self.vop("dve", "tensor_tensor", wS, [wS, expo], out=wS.ap[:, di, h, :], in0=wS.ap[:, di, h, :],
                         in1=expo.ap[:, 8 * di + 4:8 * di + 8], op=ALU.mult)
        self.act(small, small.ap[:, 8:16], self.lg, self.lg.ap, AF.Exp, scale=128.0)
        QKT = self.ar("QKT", [128, 2, T], BF16)
        Qxf = self.ar("Qxf", [128, NT, 128], BF16)
        Qzb = self.ar("Qzb", [128, NT, 128], BF16)
        Kzf = self.ar("Kzf", [128, NT, 128], BF16)
        Kxb = self.ar("Kxb", [128, NT, 128], BF16)
        vbh = self.ar("vbh", [128, NT, 256], BF16)
        sg = self.ar("sg", [128, NT, 256], BF16)
        rotbf = [self.ar("rotbf%d" % i, [128, 256], BF16) for i in range(2)]
        Sbf = [self.ar("Sbf%d" % d, [128, NT, 256], BF16) for d in range(2)]
        S32 = [[self.ar("S32_%d%d" % (d, i), [128, 256], F32) for i in range(2)] for d in range(2)]
        Lh = [self.ar("Lh%d" % d, [128, 4, 256], F32, dma=True) for d in range(2)]
        AT = [self.ar("AT%d" % i, [128, 128], BF16) for i in range(2)]
        junk = self.ar("rjunk", [128, 256], BF16)
        stt = [self.ar("stt%d" % i, [128, 8], F32) for i in range(2)]
        yn = [self.ar("yn%d" % i, [128, 256], F32) for i in range(2)]
        ob = [self.ar("ob%d" % i, [128, 256], BF16) for i in range(2)]
        xnT = self.xnT
        for h in range(4):
            sA = self.next_slot()
            vA = self.load_w(sA, w_in, 0, D, C_QB + 128 * h, 128, col_off=0, total_cols=512)
            self.load_w(sA, w_in, 0, D, C_KB + 128 * h, 128, col_off=128, total_cols=512)
            self.load_w(sA, w_in, 0, D, C_VB + 256 * h, 256, col_off=256, total_cols=512)
            sB = self.next_slot()
            vB = self.load_w(sB, w_in, 0, D, C_GR + 256 * h, 256)
            for di in range(2):
                self.dma(lq, Lh[di], Lh[di].ap, Lall_d[di, :, h].rearrange("j p v -> p j v"))
            for tt in range(NT):
                ts = slice(tt * 128, (tt + 1) * 128)
                ps = self.next_acc()
                for kc in range(KC):
                    self.mm(ps, ps.ap, xnT, xnT.ap[:, kc, ts], sA, vA[:, kc, :], kc == 0, kc == KC - 1)
                rb = rotbf[tt % 2]
                self.rotary(rb, rb.ap, ps, ps.ap[:, 0:256], rot, tt, 2)
                self.evac(vbh, vbh.ap[:, tt, :], ps, ps.ap[:, 256:512])
                pt = self.next_ptr()
                self.tr(pt, pt.ap[:, 0:128], rb, rb.ap[:, 0:128])
                self.tr(pt, pt.ap[:, 128:256], rb, rb.ap[:, 128:256])
                self.evac(QKT, QKT.ap[:, :, ts], pt, pt.ap[:, 0:256].rearrange("p (a b) -> p a b", b=128))
                self.vop("dve", "tensor_scalar", Kzf, [rb, small], out=Kzf.ap[:, tt, :], in0=rb.ap[:, 128:256], scalar1=small.ap[:, h:h + 1],
                         scalar2=None, op0=ALU.mult)
                self.vop("dve", "tensor_scalar", Kxb, [rb, small], out=Kxb.ap[:, tt, :], in0=rb.ap[:, 128:256], scalar1=small.ap[:, 4 + h:5 + h],
                         scalar2=None, op0=ALU.mult)
                ps2 = self.next_acc()
                for kc in range(KC):
                    self.mm(ps2, ps2.ap[:, 0:256], xnT, xnT.ap[:, kc, ts], sB, vB[:, kc, :], kc == 0, kc == KC - 1)
                self.act(sg, sg.ap[:, tt, :], ps2, ps2.ap[:, 0:256], AF.Silu)
            q3 = QKT.ap[:, 0, :].rearrange("p (c n) -> p c n", n=128)
            self.vop("dve", "tensor_tensor", Qxf, [QKT, XIF], out=Qxf.ap, in0=q3, in1=XIF.ap[:, h, :].unsqueeze(1).to_broadcast([128, NT, 128]), op=ALU.mult)
            self.vop("dve", "tensor_tensor", Qzb, [QKT, ZBB], out=Qzb.ap, in0=q3, in1=ZBB.ap[:, h, :].unsqueeze(1).to_broadcast([128, NT, 128]), op=ALU.mult)
            for di in range(2):
                s0 = S32[di][0]
                self.vop("dve", "tensor_scalar", s0, [Lh[di], wS], out=s0.ap, in0=Lh[di].ap[:, 0, :], scalar1=wS.ap[:, di, h, 0:1], scalar2=None, op0=ALU.mult)
                for j in range(1, 4):
                    self.vop("dve", "scalar_tensor_tensor", s0, [Lh[di], wS, s0], out=s0.ap, in0=Lh[di].ap[:, j, :], scalar=wS.ap[:, di, h, j:j + 1],
                             in1=s0.ap, op0=ALU.mult, op1=ALU.add)
            for di in range(2):
                order = list(range(NT)) if di == 0 else list(range(NT - 1, -1, -1))
                kz = Kzf if di == 0 else Kxb
                gcol = 8 + 4 * di + h
                cur = 0
                self.act(Sbf[di], Sbf[di].ap[:, order[0], :], S32[di][0], S32[di][0].ap, AF.Copy)
                for idx in range(NT - 1):
                    i = order[idx]
                    ps = self.next_acc()
                    self.mm(ps, ps.ap[:, 0:256], kz, kz.ap[:, i, :], vbh, vbh.ap[:, i, :], True, True)
                    nxt = cur ^ 1
                    self.vop("dve", "scalar_tensor_tensor", S32[di][nxt], [S32[di][cur], small, ps], out=S32[di][nxt].ap, in0=S32[di][cur].ap,
                             scalar=small.ap[:, gcol:gcol + 1], in1=ps.ap[:, 0:256], op0=ALU.mult, op1=ALU.add)
                    self.act(Sbf[di], Sbf[di].ap[:, order[idx + 1], :], S32[di][nxt], S32[di][nxt].ap, AF.Copy)
                    cur = nxt
            for i in range(NT):
                ts = slice(i * 128, (i + 1) * 128)
                sc = self.next_acc()
                self.mm(sc, sc.ap[:, 0:128], QKT, QKT.ap[:, 1, ts], QKT, QKT.ap[:, 0, ts], True, True)
                at = AT[i % 2]
                self.vop("dve", "tensor_tensor", at, [sc, DT], out=at.ap, in0=sc.ap[:, 0:128], in1=DT.ap[:, h, :], op=ALU.mult)
                o = self.patt[i % 2]
                oa = o.ap[:, 0:256]
                self.mm(o, oa, at, at.ap, vbh, vbh.ap[:, i, :], True, False)
                self.mm(o, oa, Qxf, Qxf.ap[:, i, :], Sbf[0], Sbf[0].ap[:, i, :], False, False)
                self.mm(o, oa, Qzb, Qzb.ap[:, i, :], Sbf[1], Sbf[1].ap[:, i, :], False, True)
                st = stt[i % 2]
                self.act(junk, junk.ap, o, oa, AF.Copy, accum_out=st.ap[:, 0:1], writes=[st])
                self.act(junk, junk.ap, o, oa, AF.Square, accum_out=st.ap[:, 1:2], writes=[st])
                self.vop("dve", "tensor_scalar", st, [st], out=st.ap[:, 2:4], in0=st.ap[:, 0:2], scalar1=1.0 / 256, scalar2=None, op0=ALU.mult)
                self.vop("dve", "tensor_tensor", st, [st], out=st.ap[:, 4:5], in0=st.ap[:, 2:3], in1=st.ap[:, 2:3], op=ALU.mult)
                self.vop("dve", "tensor_tensor", st, [st], out=st.ap[:, 5:6], in0=st.ap[:, 3:4], in1=st.ap[:, 4:5], op=ALU.subtract)
                self.vop("dve", "tensor_scalar", st, [st], out=st.ap[:, 5:6], in0=st.ap[:, 5:6], scalar1=EPS, scalar2=None, op0=ALU.add)
                self.act(st, st.ap[:, 6:7], st, st.ap[:, 5:6], AF.Sqrt)
                self.vop("dve", "reciprocal", st, [st], out=st.ap[:, 7:8], in_=st.ap[:, 6:7])
                y_ = yn[i % 2]
                self.vop("dve", "tensor_scalar", y_, [o, st], out=y_.ap, in0=oa, scalar1=st.ap[:, 2:3], scalar2=st.ap[:, 7:8],
                         op0=ALU.subtract, op1=ALU.mult)
                ob_ = ob[i % 2]
                self.vop("dve", "tensor_tensor", ob_, [y_, sg], out=ob_.ap, in0=y_.ap, in1=sg.ap[:, i, :], op=ALU.mult)
                pt = self.next_ptr()
                self.tr(pt, pt.ap[:, 0:128], ob_, ob_.ap[:, 0:128])
                self.tr(pt, pt.ap[:, 128:256], ob_, ob_.ap[:, 128:256])
                self.evac(oT, oT.ap[:, 6 + 2 * h:8 + 2 * h, ts], pt, pt.ap[:, 0:256].rearrange("p (a b) -> p a b", b=128))

    def emit_attn(self, w_in, c_q, kT_d, v_d, bias_d, oT, o_chunk0, nkt_halo, nr, name, lm_d=None, vcol_d=None, halo_t=()):
        xnT = self.xnT
        QT = self.ar(name + "QT", [128, T], BF16)
        KT = [self.ar(name + "KT%d" % i, [128, nkt_halo * 128], BF16, dma=True) for i in range(2)]
        V = [self.ar(name + "V%d" % i, [128, nkt_halo, 128], BF16, dma=True) for i in range(2)]
        strip = lm_d is not None
        if strip:
            bias2 = [self.ar(name + "bias%d" % i, [128, DIL_STRIP], BF16, dma=True) for i in range(2)]
        else:
            nb = bias_d.shape[1]
            bias2 = [self.ar(name + "bias%d" % i, [128, nb, 512], BF16, dma=True) for i in range(2)]
        PT = [self.ar(name + "PT%d" % i, [128, 512], BF16) for i in range(3)]
        rc = self.ar(name + "rc", [128, 512], F32)
        lm = vcol = None
        if strip:
            lm = self.ar(name + "lm", [128, DIL_STRIP], BF16, dma=True)
            self.dma("pool", lm, lm.ap, lm_d)
            vcol = self.ar(name + "vcol", [128, 24], F32, dma=True)
            self.dma("sp", vcol, vcol.ap, vcol_d)
        for h in range(6):
            slot = self.next_slot()
            wv = self.load_w(slot, w_in, 0, D, c_q + 128 * h, 128)
            kt_, v_ = KT[h % 2], V[h % 2]
            self.dma("sp", kt_, kt_.ap, kT_d[h], reads=halo_t)
            self.dma("sp", v_, v_.ap, v_d[:, h * 128:(h + 1) * 128].rearrange("(t p) d -> p t d", p=128), reads=halo_t)
            bias = bias2[h % 2]
            if strip:
                self.dma("pool", bias, bias.ap, bias_d[h])
                self.vop("dve", "tensor_tensor", bias, [bias, lm], out=bias.ap, in0=bias.ap, in1=lm.ap, op=ALU.add)
            else:
                self.dma("pool", bias, bias.ap, bias_d[h].rearrange("r k n -> k r n"))
            for half in range(2):
                ps = self.next_acc()
                for kc in range(KC):
                    self.mm(ps, ps.ap, slot, wv[:, kc, :], xnT, xnT.ap[:, kc, half * 512:(half + 1) * 512], kc == 0, kc == KC - 1)
                self.evac(QT, QT.ap[:, half * 512:(half + 1) * 512], ps, ps.ap, scale=INV_SQRT_HD)
            for qg in range(2):
                qs = slice(qg * 512, (qg + 1) * 512)
                num, den = self.patt
                for r in range(nr):
                    kk = 4 * qg + r
                    sc = self.next_acc()
                    self.mm(sc, sc.ap, kt_, kt_.ap[:, kk * 128:(kk + 1) * 128], QT, QT.ap[:, qs], True, False)
                    if strip:
                        b_ap = bias.ap[:, 128 * (19 - r):128 * (19 - r) + 512]
                    else:
                        b_ap = bias.ap[:, qg * nr + r, :]
                    self.mm(sc, sc.ap, self.ident, self.ident.ap, bias, b_ap, False, True)
                    p_ = PT[r % 3]
                    if vcol is None:
                        self.act(p_, p_.ap, sc, sc.ap, AF.Exp)
                    else:
                        self.act(p_, p_.ap, sc, sc.ap, AF.Exp, bias=vcol.ap[:, kk:kk + 1], reads=[vcol])
                    self.mm(num, num.ap, v_, v_.ap[:, kk, :], p_, p_.ap, r == 0, r == nr - 1)
                    self.mm(den, den.ap, self.ones, self.ones.ap, p_, p_.ap, r == 0, r == nr - 1)
                self.vop("dve", "reciprocal", rc, [den], out=rc.ap, in_=den.ap)
                self.vop("dve", "tensor_tensor", oT, [num, rc], out=oT.ap[:, o_chunk0 + h, qs], in0=num.ap, in1=rc.ap, op=ALU.mult)

    def build_F(self):
        self.setup_common()
        x_d = self.dram_in("x", [T, D])
        g_d = self.dram_in("g_final", [D])
        y = self.dram_out("y", [T, D])
        self.aoff = 0
        g_bc = self.ar("g_bc", [128, D], F32, dma=True)
        self.dma("sp", g_bc, g_bc.ap, g_d.partition_broadcast(128))
        junk = self.ar("junk", [128, D], BF16)
        st = self.ar("st", [128, 4 * NT], F32)
        xs = [self.ar("xs%d" % i, [128, D], F32, dma=True) for i in range(2)]
        ys = [self.ar("ys%d" % i, [128, D], F32) for i in range(2)]
        for tt in range(NT):
            xt, yt = xs[tt % 2], ys[tt % 2]
            c = 4 * tt
            self.dma("sp", xt, xt.ap, x_d[tt * 128:(tt + 1) * 128, :])
            self.act(junk, junk.ap, xt, xt.ap, AF.Square, accum_out=st.ap[:, c:c + 1], writes=[st])
            self.vop("dve", "tensor_scalar", st, [st], out=st.ap[:, c + 1:c + 2], in0=st.ap[:, c:c + 1], scalar1=1.0 / D, scalar2=EPS, op0=ALU.mult, op1=ALU.add)
            self.act(st, st.ap[:, c + 2:c + 3], st, st.ap[:, c + 1:c + 2], AF.Sqrt)
            self.vop("dve", "reciprocal", st, [st], out=st.ap[:, c + 3:c + 4], in_=st.ap[:, c + 2:c + 3])
            self.vop("dve", "scalar_tensor_tensor", yt, [xt, st, g_bc], out=yt.ap, in0=xt.ap, scalar=st.ap[:, c + 3:c + 4], in1=g_bc.ap, op0=ALU.mult, op1=ALU.mult)
            self.dma("sp", y, y.ap[tt * 128:(tt + 1) * 128, :], yt.ap, reads=[yt])


    def internal(self, name, shape, dtype):
        ap = self.nc.dram_tensor(name, list(shape), dtype).ap()
        return Tn(ap, Buf(name, self.new_dsem(name)))

    def collective(self, src_ts, send_ap, recv_ap):
        self.cc_n += 1
        n = self.cc_n

        def fn(g):
            g.collective_compute("AllGather", ALU.bypass, replica_groups=[[0, 1, 2, 3], [4, 5, 6, 7]],
                                 ins=[send_ap.opt()], outs=[recv_ap.opt()]).then_inc(self.cc_sem, 1)
            return g.wait_ge(self.cc_sem, n)
        o = self.P.op("pool", fn, reads=[t.b for t in src_ts], writes=[])
        o.noevent = True

    def store_kT(self, o, kT):
        if isinstance(o, Tn):
            self.dma("sp", o, o.ap.rearrange("h p t -> p h t"), kT.ap, reads=[kT])
        else:
            self.dma("sp", o[0], o[0].ap.rearrange("(h p) t -> p h t", p=128), kT.ap[:, 0:4, :], reads=[kT])
            self.dma("sp", o[1], o[1].ap.rearrange("(h p) t -> p h t", p=128), kT.ap[:, 4:6, :], reads=[kT])

    def store_v(self, o, vtm):
        if isinstance(o, Tn):
            self.dma("sp", o, o.ap.rearrange("(t p) c -> p t c", p=128), vtm.ap, reads=[vtm])
        else:
            self.dma("sp", o[0], o[0].ap.rearrange("(t p) c -> p t c", p=128), vtm.ap[:, :, 0:384], reads=[vtm])
            self.dma("sp", o[1], o[1].ap.rearrange("(t p) c -> p t c", p=128), vtm.ap[:, :, 384:768], reads=[vtm])

    def assemble_halos(self, l, pieces):
        kaT_h = self.internal("kaT_h%d" % l, [6, 128, 1536], BF16)
        va_h = self.internal("va_h%d" % l, [1536, 768], BF16)
        kcT_h = self.internal("kcT_h%d" % l, [6, 128, 3072], BF16)
        vc_h = self.internal("vc_h%d" % l, [3072, 768], BF16)
        NE = 4096
        Rs = [[self.ar("hR%d_%d" % (st, i), [128, NE], BF16, dma=True) for i in range(4)] for st in range(2)]
        accs = [self.ar("hacc%d" % st, [128, NE], BF16) for st in range(2)]
        wsel = self.wsel
        specs = (("ka", "k", 256, kaT_h), ("kc", "k", 1024, kcT_h), ("va", "v", 256, va_h), ("vc", "v", 1024, vc_h))
        step = 0
        alias = []
        for (nm, kind, hw, out) in specs:
            (s0, r0), (s1, r1) = pieces[nm]
            outs2 = [Tn(out.ap, Buf(nm + "_hs%d" % st, self.new_dsem(nm + "_hs%d" % st))) for st in range(2)]
            alias.extend(outs2)
            if kind == "k":
                self.dma("sp", out, out.ap[0:4, :, hw:hw + T], s0.ap.rearrange("(h p) t -> h p t", p=128), reads=[s0])
                self.dma("sp", out, out.ap[4:6, :, hw:hw + T], s1.ap.rearrange("(h p) t -> h p t", p=128), reads=[s1])
            else:
                self.dma("sp", out, out.ap[hw:hw + T, 0:384], s0.ap, reads=[s0])
                self.dma("sp", out, out.ap[hw:hw + T, 384:768], s1.ap, reads=[s1])
            for side in range(2):
                ranks = (0, 1, 2) if side == 0 else (1, 2, 3)
                for pi, rp in enumerate((r0, r1)):
                    R, acc = Rs[step % 2], accs[step % 2]
                    out_s = outs2[step % 2]
                    step += 1
                    if kind == "k":
                        nh, h0 = (4, 0) if pi == 0 else (2, 4)
                        n = nh * hw
                    else:
                        c0 = 384 * pi
                        n = (hw // 128) * 384
                    for r in ranks:
                        if kind == "k":
                            full = rp[r * nh * 128:(r + 1) * nh * 128, :].rearrange("(h p) t -> p h t", p=128)
                            src = full[:, :, T - hw:T] if side == 0 else full[:, :, 0:hw]
                            dst = R[r].ap[:, 0:n].rearrange("p (h t) -> p h t", t=hw)
                        else:
                            full = rp[r * T:(r + 1) * T, :]
                            src = (full[T - hw:T, :] if side == 0 else full[0:hw, :]).rearrange("(t p) c -> p t c", p=128)
                            dst = R[r].ap[:, 0:n].rearrange("p (t c) -> p t c", c=384)
                        self.dma("pool", R[r], dst, src)
                    c = 4 * side
                    ra = ranks[0]
                    self.vop("dve", "tensor_scalar", acc, [R[ra], wsel], out=acc.ap[:, 0:n], in0=R[ra].ap[:, 0:n], scalar1=wsel.ap[:, c + ra:c + ra + 1],
                             scalar2=None, op0=ALU.mult)
                    for r in ranks[1:]:
                        self.vop("dve", "scalar_tensor_tensor", acc, [R[r], wsel, acc], out=acc.ap[:, 0:n], in0=R[r].ap[:, 0:n],
                                 scalar=wsel.ap[:, c + r:c + r + 1], in1=acc.ap[:, 0:n], op0=ALU.mult, op1=ALU.add)
                    if kind == "k":
                        reg = out.ap[h0:h0 + nh, :, 0:hw] if side == 0 else out.ap[h0:h0 + nh, :, hw + T:hw + T + hw]
                        self.dma("sp", out_s, reg.rearrange("h p t -> p h t"), acc.ap[:, 0:n].rearrange("p (h t) -> p h t", t=hw), reads=[acc])
                    else:
                        reg = out.ap[0:hw, c0:c0 + 384] if side == 0 else out.ap[hw + T:hw + T + hw, c0:c0 + 384]
                        self.dma("sp", out_s, reg.rearrange("(t p) c -> p t c", p=128), acc.ap[:, 0:n].rearrange("p (t c) -> p t c", c=384), reads=[acc])
        self.halo_alias = tuple(alias)
        return kaT_h, va_h, kcT_h, vc_h

    def build_fused(self, nlayers=DEPTH):
        nc = self.nc
        self.setup_common()
        x_in = self.dram_in("x", [T, D])
        mem_d = self.dram_in("mem", [256, D])
        g_mix = self.dram_in("norm_mix_g", [DEPTH, D])
        g_cross = self.dram_in("norm_cross_g", [DEPTH, D])
        g_mem = self.dram_in("norm_mem_g", [DEPTH, D])
        g_mlp = self.dram_in("norm_mlp_g", [DEPTH, D])
        g_final = self.dram_in("final_norm_g", [D])
        w_in = [self.dram_in("w_in_%d" % l, [D, 13824]) for l in range(nlayers)]
        w_branch = [self.dram_in("w_branch_%d" % l, [2560, D]) for l in range(nlayers)]
        w_out = [self.dram_in("w_out_%d" % l, [D, D]) for l in range(nlayers)]
        w_cq = [self.dram_in("w_cq_%d" % l, [D, 512]) for l in range(nlayers)]
        w_ckv = [self.dram_in("w_ckv_%d" % l, [D, 1024]) for l in range(nlayers)]
        w_co = [self.dram_in("w_co_%d" % l, [512, D]) for l in range(nlayers)]
        w_mlp1 = [self.dram_in("w_mlp1_%d" % l, [D, 8192]) for l in range(nlayers)]
        w_mlp2 = [self.dram_in("w_mlp2_%d" % l, [8192, D]) for l in range(nlayers)]
        biasA_d = [self.dram_in("biasA_%d" % l, [6, 16, 128, 512]) for l in range(nlayers)]
        dec = self.dram_in("ret_decay", [DEPTH, 8])
        cs_d = self.dram_in("rot_cs", [T, 128])
        nsc_d = self.dram_in("rot_nsc", [T, 128])
        expo_d = self.dram_in("expo", [128, 16])
        wsel_d = self.dram_in("wsel", [128, 8])
        biasC_d = self.dram_in("biasC", [6, 128, DIL_STRIP])
        lmC_d = self.dram_in("lmC", [128, DIL_STRIP])
        vcol_d = self.dram_in("vcolC", [128, 24])
        y = self.dram_out("y", [T, D])
        KmT = self.sb("KmT", [128, 4, 256], BF16)
        Vm = self.sb("Vm", [128, 2, 512], BF16)
        self.wsel = self.sb("wsel", [128, 8], F32, dma=True)
        self.dma("sp", self.wsel, self.wsel.ap, wsel_d)
        xscr = self.internal("xscr", [T, D], F32)
        x_d, x_t = x_in, None
        for l in range(nlayers):
            pieces = {}
            for nm, shapes in (("ka", ((512, T), (256, T))), ("kc", ((512, T), (256, T))), ("va", ((T, 384), (T, 384))), ("vc", ((T, 384), (T, 384)))):
                pp = []
                for i, (rws, cls) in enumerate(shapes):
                    sname = "s_%s%d" % (nm, i)
                    sap = nc.dram_tensor("%s_%d" % (sname, l), [rws, cls], BF16).ap()
                    rap = nc.dram_tensor("r_%s%d_%d" % (nm, i, l), [4 * rws, cls], BF16).ap()
                    pp.append((Tn(sap, Buf(sname, self.new_dsem(sname))), rap))
                pieces[nm] = tuple(pp)
            send_L = nc.dram_tensor("send_L%d" % l, [1024, 256], F32).ap()
            recv_L = nc.dram_tensor("recv_L%d" % l, [4096, 256], F32).ap()
            o_L = Tn(send_L.rearrange("(d h p) v -> d h p v", d=2, h=4), Buf("s_L", self.new_dsem("s_L")))
            self.arena_reset()
            outs = ((pieces["ka"][0][0], pieces["ka"][1][0]), (pieces["kc"][0][0], pieces["kc"][1][0]),
                    (pieces["va"][0][0], pieces["va"][1][0]), (pieces["vc"][0][0], pieces["vc"][1][0]), o_L)
            def mid_cb(pieces=pieces):
                for nm in ("ka", "kc", "va", "vc"):
                    for (st_, rap) in pieces[nm]:
                        self.collective([st_], st_.ap, rap)
            if OVERLAP_CC:
                self.part_A(x_d, x_t, g_mix[l], w_in[l], dec[l], cs_d, nsc_d, outs, mid_cb=mid_cb)
            else:
                self.part_A(x_d, x_t, g_mix[l], w_in[l], dec[l], cs_d, nsc_d, outs)
                mid_cb()
            self.collective([o_L], send_L, recv_L)
            self.arena_reset()
            halos = self.assemble_halos(l, pieces)
            last = (l == nlayers - 1)
            a = dict(x_d=x_d, x_t=x_t, g_mix=g_mix[l], g_cross=g_cross[l], g_mem=g_mem[l], g_mlp=g_mlp[l], w_in=w_in[l], w_branch=w_branch[l],
                     w_out=w_out[l], w_cq=w_cq[l], w_ckv=w_ckv[l], w_co=w_co[l], w_mlp1=w_mlp1[l], w_mlp2=w_mlp2[l], mem_d=mem_d, dec_d=dec[l],
                     cs_d=cs_d, nsc_d=nsc_d, expo_d=expo_d, biasA_d=biasA_d[l], biasC_d=biasC_d, lmC_d=lmC_d, vcol_d=vcol_d,
                     kaT_d=halos[0].ap, va_d=halos[1].ap, kcT_d=halos[2].ap, vc_d=halos[3].ap, halo_t=tuple(halos) + self.halo_alias,
                     Lall_d=recv_L.rearrange("(j d h p) v -> d j h p v", j=4, d=2, h=4), L_queue="pool",
                     x_out=(y if last else xscr), KmT=KmT, Vm=Vm)
            self.arena_reset()
            self.part_B(a, norm1=False, final_g=(g_final if last else None))
            x_d, x_t = xscr.ap, xscr

    def finish(self):
        nc = self.nc
        P = self.P
        P.finalize()
        esems = {en: self.es.enter_context(nc.semaphore("s_" + en)) for en in ENGS}
        self.cc_sem = self.es.enter_context(nc.semaphore("s_cc"))
        for d in P.dsems:
            if d.count > 0:
                d.handle = self.es.enter_context(nc.semaphore(d.name))
        with nc.Block() as block:
            @block.tensor
            def _(t):
                P.emit_engine("pe", t, esems)

            @block.scalar
            def _(a):
                P.emit_engine("act", a, esems)

            @block.vector
            def _(v):
                P.emit_engine("dve", v, esems)

            @block.gpsimd
            def _(g):
                P.emit_engine("pool", g, esems)

            @block.sync
            def _(s):
                P.emit_engine("sp", s, esems)
        self.es.close()
        return nc


CST_PF, CST_PB, CST_UF, CST_UB, CST_N1, CST_N2 = 0, 128, 256, 384, 512, 640
CST_M1, CST_M2, CST_ZE, CST_ZB = 768, 769, 770, 778
CST_N = 786


def make_cst():
    c = np.zeros((128, CST_N), np.float32)
    m = np.arange(128)[:, None].astype(np.float32)
    n = np.arange(128)[None, :].astype(np.float32)
    c[:, CST_PF:CST_PF + 128] = np.maximum(n - m, 0)
    c[:, CST_PB:CST_PB + 128] = np.maximum(m - n, 0)
    c[:, CST_UF:CST_UF + 128] = (n >= m)
    c[:, CST_UB:CST_UB + 128] = (m > n)
    c[:, CST_N1:CST_N1 + 128] = n + 1 + 0 * m
    c[:, CST_N2:CST_N2 + 128] = 127 - n + 0 * m
    c[:, CST_M1] = m[:, 0] + 1
    c[:, CST_M2] = 127 - m[:, 0]
    tt = np.arange(8)[None, :].astype(np.float32)
    c[:, CST_ZE:CST_ZE + 8] = 1023 - (tt * 128 + m)
    c[:, CST_ZB:CST_ZB + 8] = tt * 128 + m + 1
    return c


def rot_tables(j):
    d = 128
    inv_freq = (10000.0 ** (-np.arange(0, d, 2, dtype=np.float32) / d)).astype(np.float32)
    pos = (np.arange(T, dtype=np.float32) + np.float32(j * T))
    ang = pos[:, None] * inv_freq[None, :]
    cos, sin = np.cos(ang).astype(np.float32), np.sin(ang).astype(np.float32)
    cs = np.concatenate([cos, sin], axis=1)
    nsc = np.concatenate([-sin, cos], axis=1)
    return np.ascontiguousarray(cs), np.ascontiguousarray(nsc)


_PROG_CACHE = {}


def get_prog(mode, stop=None):
    key = (mode, stop)
    if key not in _PROG_CACHE:
        b = Builder(mode)
        if mode == "FUSED":
            b.build_fused(nlayers=(stop if stop is not None else DEPTH))
        elif mode == "A":
            b.build_A()
        elif mode == "B":
            b.build_B(stop=stop)
        else:
            b.build_F()
        _PROG_CACHE[key] = b.finish()
        _PROG_INPUTS[id(_PROG_CACHE[key])] = list(b.din.keys())
    return _PROG_CACHE[key]


def t5_bucket(rel):
    nb = 16
    ret = (rel > 0).astype(np.int32) * nb
    n = np.abs(rel)
    max_exact = nb // 2
    large = max_exact + (np.log(np.maximum(n, 1) / max_exact) / np.log(1024 / max_exact) * (nb - max_exact)).astype(np.int32)
    large = np.minimum(large, nb - 1)
    return (ret + np.where(n < max_exact, n, large)).astype(np.int32)


_IDX_CACHE = {}


def na_index(j):
    if ("na", j) not in _IDX_CACHE:
        qg = np.arange(2)[:, None, None, None]
        r = np.arange(8)[None, :, None, None]
        k = np.arange(128)[None, None, :, None]
        q = np.arange(512)[None, None, None, :]
        tk = 1024 * j - 256 + 128 * (4 * qg + r) + k
        tq = 1024 * j + 512 * qg + q + 0 * k
        inseq = (tk >= 0) & (tk < SEQ)
        rk, ck = tk // 64, tk % 64
        rq, cq = tq // 64, tq % 64
        r0 = np.clip(rq - 4, 0, 56)
        c0 = np.clip(cq - 8, 0, 48)
        valid = inseq & (rk >= r0) & (rk < r0 + 8) & (ck >= c0) & (ck < c0 + 16)
        ri = np.clip(rk - rq + 7, 0, 14)
        ci = np.clip(ck - cq + 15, 0, 30)
        _IDX_CACHE[("na", j)] = (ri.reshape(16, 128, 512), ci.reshape(16, 128, 512), valid.reshape(16, 128, 512))
    return _IDX_CACHE[("na", j)]


def dil_index():
    if "dil" not in _IDX_CACHE:
        k = np.arange(128)[:, None]
        c = np.arange(DIL_STRIP)[None, :]
        off = 1408 + k - c
        a = np.abs(off)
        mult = (a <= 64).astype(np.int32) + ((a <= 256) & (off % 4 == 0)) + ((a <= 1024) & (off % 16 == 0))
        bucket = t5_bucket(off)
        lm = np.where(mult > 0, np.log(np.maximum(mult, 1)), 0.0).astype(np.float32)
        _IDX_CACHE["dil"] = (bucket, mult > 0, np.ascontiguousarray(lm))
    return _IDX_CACHE["dil"]


def dil_bias(t5):
    bucket, dvalid, lmC = dil_index()
    biasC = np.where(dvalid[None], np.transpose(t5[bucket], (2, 0, 1)), np.float32(NEG)).astype(np.float32)
    return np.ascontiguousarray(biasC), lmC


def halo(arr, axis, start, length):
    n = arr.shape[axis]
    lo, hi = max(start, 0), min(start + length, n)
    shp = list(arr.shape)
    shp[axis] = length
    out = np.zeros(shp, arr.dtype)
    sl_o = [slice(None)] * arr.ndim
    sl_i = [slice(None)] * arr.ndim
    sl_o[axis] = slice(lo - start, hi - start)
    sl_i[axis] = slice(lo, hi)
    out[tuple(sl_o)] = arr[tuple(sl_i)]
    return out


def run_B(x_chunks, resA, inputs, l, stop=None):
    nc = get_prog("B", stop)
    cst = make_cst()
    ident = np.eye(128, dtype=np.float32)
    biasC, lmC = dil_bias(inputs["t5_bias"])
    W = {k: np.ascontiguousarray(inputs[k][l]) for k in ("w_in", "w_branch", "w_out", "w_cq", "w_ckv", "w_co", "w_mlp1", "w_mlp2",
                                                          "norm_mix_g", "norm_cross_g", "norm_mem_g", "norm_mlp_g")}
    rpb = inputs["na_rpb"][l]
    maps = []
    for c in range(NCORES):
        b, j = c // 4, c % 4
        grp = [resA[b * 4 + jj] for jj in range(4)]
        kaT = np.concatenate([g["kaT"] for g in grp], axis=2)
        kcT = np.concatenate([g["kcT"] for g in grp], axis=2)
        va = np.concatenate([g["va"] for g in grp], axis=0)
        vc = np.concatenate([g["vc"] for g in grp], axis=0)
        Lall = np.ascontiguousarray(np.stack([g["L"] for g in grp], axis=1))
        ri, ci, valid = na_index(j)
        biasA = np.where(valid[None], rpb[:, ri, ci], np.float32(NEG)).astype(np.float32)
        cs, nsc = rot_tables(j)
        expo = np.zeros((128, 16), np.float32)
        for jj in range(4):
            if jj < j:
                expo[:, jj] = 1024.0 * (j - 1 - jj)
                expo[:, 4 + jj] = 1.0
            if jj > j:
                expo[:, 8 + jj] = 1024.0 * (jj - j - 1)
                expo[:, 12 + jj] = 1.0
        vcol = np.zeros((128, 24), np.float32)
        tok = 1024 * j - 1024 + 128 * np.arange(24)[None, :] + np.arange(128)[:, None]
        vcol[(tok < 0) | (tok >= SEQ)] = NEG
        maps.append({
            "x": x_chunks[c], "g_mix": W["norm_mix_g"], "g_cross": W["norm_cross_g"], "g_mem": W["norm_mem_g"], "g_mlp": W["norm_mlp_g"],
            "w_in": W["w_in"], "w_branch": W["w_branch"], "w_out": W["w_out"], "w_cq": W["w_cq"], "w_ckv": W["w_ckv"], "w_co": W["w_co"],
            "w_mlp1": W["w_mlp1"], "w_mlp2": W["w_mlp2"], "mem": np.ascontiguousarray(inputs["mem"][b]),
            "ret_decay": np.ascontiguousarray(inputs["ret_decay"][l].reshape(8)), "rot_cs": cs, "rot_nsc": nsc, "expo": expo,
            "biasA": biasA, "biasC": biasC, "lmC": lmC, "vcolC": vcol,
            "kaT_h": halo(kaT, 2, 1024 * j - 256, 1536), "va_h": halo(va, 0, 1024 * j - 256, 1536),
            "kcT_h": halo(kcT, 2, 1024 * j - 1024, 3072), "vc_h": halo(vc, 0, 1024 * j - 1024, 3072),
            "Lall": Lall, "ident": ident, "cst": cst,
        })
    needed = set(nc_input_names(nc))
    maps = [{k: v for k, v in m.items() if k in needed} for m in maps]
    res = run_bass_kernel_spmd(nc, maps, core_ids=list(range(NCORES)))
    return res.results


def nc_input_names(nc):
    return _PROG_INPUTS[id(nc)]


_PROG_INPUTS = {}


def run_A(x_chunks, inputs, l):
    nc = get_prog("A")
    cst = make_cst()
    ident = np.eye(128, dtype=np.float32)
    maps = []
    for c in range(NCORES):
        cs, nsc = rot_tables(c % 4)
        maps.append({"x": x_chunks[c], "g_mix": np.ascontiguousarray(inputs["norm_mix_g"][l]),
                     "w_in": np.ascontiguousarray(inputs["w_in"][l]),
                     "ret_decay": np.ascontiguousarray(inputs["ret_decay"][l].reshape(8)),
                     "rot_cs": cs, "rot_nsc": nsc, "ident": ident, "cst": cst})
    res = run_bass_kernel_spmd(nc, maps, core_ids=list(range(NCORES)))
    return res.results


def run_F(x_chunks, inputs):
    nc = get_prog("F")
    cst = make_cst()
    ident = np.eye(128, dtype=np.float32)
    g = np.ascontiguousarray(inputs["final_norm_g"])
    maps = [{"x": x_chunks[c], "g_final": g, "ident": ident, "cst": cst} for c in range(NCORES)]
    res = run_bass_kernel_spmd(nc, maps, core_ids=list(range(NCORES)))
    return res.results


def fused_maps(inputs, nlayers=DEPTH):
    cst = make_cst()
    ident = np.eye(128, dtype=np.float32)
    biasC, lmC = dil_bias(inputs["t5_bias"])
    shared = {k: np.ascontiguousarray(inputs[k]) for k in ("norm_mix_g", "norm_cross_g", "norm_mem_g", "norm_mlp_g", "final_norm_g")}
    for k in ("w_in", "w_branch", "w_out", "w_cq", "w_ckv", "w_co", "w_mlp1", "w_mlp2"):
        for l in range(nlayers):
            shared["%s_%d" % (k, l)] = np.ascontiguousarray(inputs[k][l])
    shared["ret_decay"] = np.ascontiguousarray(inputs["ret_decay"].reshape(DEPTH, 8))
    rpb = inputs["na_rpb"]
    biasA_j = []
    for j in range(4):
        ri, ci, valid = na_index(j)
        biasA_j.append(np.where(valid[None, None], rpb[:, :, ri, ci], np.float32(NEG)).astype(np.float32))
    x = inputs["x"]
    maps = []
    for c in range(NCORES):
        b, j = c // 4, c % 4
        cs, nsc = rot_tables(j)
        expo = np.zeros((128, 16), np.float32)
        wsel = np.zeros((128, 8), np.float32)
        for jj in range(4):
            if jj < j:
                expo[:, jj] = 1024.0 * (j - 1 - jj)
                expo[:, 4 + jj] = 1.0
            if jj > j:
                expo[:, 8 + jj] = 1024.0 * (jj - j - 1)
                expo[:, 12 + jj] = 1.0
        if j > 0:
            wsel[:, j - 1] = 1.0
        if j < 3:
            wsel[:, 4 + j + 1] = 1.0
        vcol = np.zeros((128, 24), np.float32)
        tok = 1024 * j - 1024 + 128 * np.arange(24)[None, :] + np.arange(128)[:, None]
        vcol[(tok < 0) | (tok >= SEQ)] = NEG
        m = dict(shared)
        m.update({"x": np.ascontiguousarray(x[b, j * T:(j + 1) * T]), "mem": np.ascontiguousarray(inputs["mem"][b]),
                  "rot_cs": cs, "rot_nsc": nsc, "expo": expo, "wsel": wsel, "biasC": biasC, "lmC": lmC,
                  "vcolC": vcol, "ident": ident, "cst": cst})
        for l in range(nlayers):
            m["biasA_%d" % l] = np.ascontiguousarray(biasA_j[j][l])
        maps.append(m)
    return maps


def kernel_fused(inputs, nlayers=None):
    nc = get_prog("FUSED", nlayers)
    maps = fused_maps(inputs, nlayers if nlayers is not None else DEPTH)
    res = run_bass_kernel_spmd(nc, maps, core_ids=list(range(NCORES)))
    out = np.zeros((2, SEQ, D), np.float32)
    for c in range(NCORES):
        out[c // 4, (c % 4) * T:(c % 4 + 1) * T] = np.asarray(res.results[c]["y"])
    return out


def kernel(**inputs):
    inputs = {k: np.asarray(v) for k, v in inputs.items()}
    return kernel_fused(inputs)


def kernel_unfused(**inputs):
    inputs = {k: np.asarray(v) for k, v in inputs.items()}
    x = inputs["x"].astype(np.float32, copy=False)
    xch = [np.ascontiguousarray(x[c // 4, (c % 4) * T:(c % 4 + 1) * T]) for c in range(NCORES)]
    for l in range(DEPTH):
        resA = run_A(xch, inputs, l)
        resA = [{k: np.asarray(v) for k, v in r.items()} for r in resA]
        resB = run_B(xch, resA, inputs, l)
        xch = [np.ascontiguousarray(np.asarray(r["x_out"])) for r in resB]
    resF = run_F(xch, inputs)
    out = np.zeros((2, SEQ, D), np.float32)
    for c in range(NCORES):
        out[c // 4, (c % 4) * T:(c % 4 + 1) * T] = np.asarray(resF[c]["y"])
    return out
```

```python
import numpy as np
import ml_dtypes
from contextlib import ExitStack
import concourse.bass as bass
import concourse.mybir as mybir
from concourse.bass_utils import run_bass_kernel_spmd

F32 = mybir.dt.float32
BF16 = mybir.dt.bfloat16
AF = mybir.ActivationFunctionType
ALU = mybir.AluOpType

NCORES = 8
D = 2048
SEQ = 4096
T = 1024
NT = 8
KC = 16
DEPTH = 4
EPS = 1e-6
NEG = -30000.0
HD = 128
INV_SQRT_HD = HD ** -0.5
C_QA, C_KA, C_VA = 0, 768, 1536
C_QB, C_KB, C_VB, C_GR = 2304, 2816, 3328, 4352
C_QC, C_KC, C_VC = 5376, 6144, 6912
C_SA, C_SB, C_SC = 7680, 9728, 11776
SLOT_ELEMS = 8192
DIL_STRIP = 2944
OVERLAP_CC = True


class Buf:
    __slots__ = ("name", "lastw", "readers", "dsem", "wdma", "persistent")

    def __init__(self, name, dsem=None, persistent=True):
        self.name = name
        self.lastw = []
        self.readers = {}
        self.dsem = dsem
        self.wdma = False
        self.persistent = persistent


class DmaSem:
    def __init__(self, name, step=16):
        self.name = name
        self.count = 0
        self.handle = None
        self.step = step


class Op:
    __slots__ = ("eng", "fn", "waits", "inc", "val", "dsem", "noevent")

    def __init__(self, eng, fn):
        self.eng = eng
        self.fn = fn
        self.waits = []
        self.inc = False
        self.val = None
        self.dsem = None
        self.noevent = False


ENGS = ("pe", "act", "dve", "pool", "sp")


class Prog:
    def __init__(self):
        self.ops = {e: [] for e in ENGS}
        self.dsems = []
        self.bar_events = []
        self.bar_passed = {e: True for e in ENGS}

    def dsem(self, name, step=16):
        d = DmaSem(name, step)
        self.dsems.append(d)
        return d

    def barrier(self):
        ev = []
        for e in ("pe", "act", "dve", "pool"):
            for o in reversed(self.ops[e]):
                if o.dsem is None and not o.noevent:
                    ev.append(o)
                    break
        for d in self.dsems:
            if d.count > 0:
                ev.append((d, d.count))
        self.bar_events = ev
        self.bar_passed = {e: False for e in ENGS}

    def op(self, eng, fn, reads=(), writes=(), dma=False):
        o = Op(eng, fn)
        raw, other = [], []
        if not self.bar_passed[eng] and any(not b.persistent for b in list(reads) + list(writes)):
            self.bar_passed[eng] = True
            for d in self.bar_events:
                if isinstance(d, Op) and d.eng == eng and not dma:
                    continue
                raw.append(d)
        for b in reads:
            raw.extend(b.lastw)
        for b in writes:
            if not (dma and b.wdma and not b.readers):
                other.extend(b.lastw)
            other.extend(b.readers.values())
        for d in raw:
            if isinstance(d, Op):
                if d.eng == eng and eng == "pe":
                    continue
                d.inc = True
            o.waits.append(d)
        for d in other:
            if isinstance(d, Op):
                if d.eng == eng and not dma:
                    continue
                d.inc = True
            o.waits.append(d)
        if dma:
            assert len(writes) == 1
            b = writes[0]
            ds = b.dsem
            assert ds is not None, b.name
            ds.count += ds.step
            o.dsem = ds
            ev = (ds, ds.count)
            if b.wdma and not b.readers:
                b.lastw = b.lastw + [ev]
            else:
                b.lastw = [ev]
            b.readers = {}
            b.wdma = True
            for r in reads:
                r.readers[ds.name] = ev
        else:
            for b in writes:
                b.lastw = [o]
                b.readers = {}
                b.wdma = False
            for r in reads:
                if r not in writes:
                    r.readers[eng] = o
        self.ops[eng].append(o)
        return o

    def finalize(self):
        for e in ENGS:
            c = 0
            for o in self.ops[e]:
                if o.inc and o.dsem is None:
                    c += 1
                    o.val = c

    def emit_engine(self, e, eng, esems):
        waited = {}
        for o in self.ops[e]:
            for d in o.waits:
                if isinstance(d, Op):
                    key, val, sem = d.eng, d.val, esems[d.eng]
                else:
                    key, val, sem = d[0].name, d[1], d[0].handle
                if waited.get(key, -1) >= val:
                    continue
                waited[key] = val
                eng.wait_ge(sem, val)
            ins = o.fn(eng)
            if o.dsem is not None:
                ins.then_inc(o.dsem.handle, o.dsem.step)
            elif o.inc:
                ins.then_inc(esems[e], 1)
        if e == "sp":
            for d in self.dsems:
                if d.count > 0 and waited.get(d.name, -1) < d.count:
                    eng.wait_ge(d.handle, d.count)


def _dt_size(dt):
    return 4 if dt == F32 else 2


class Tn:
    __slots__ = ("ap", "b")

    def __init__(self, ap, b):
        self.ap = ap
        self.b = b

    def __getitem__(self, k):
        return self.ap[k]


class Builder:
    def __init__(self, mode, final_norm=False, dbg=None):
        self.mode = mode
        self.final_norm = final_norm
        self.dbg = dbg
        self.nc = bass.Bass("TRN2", target_bir_lowering=False)
        self.P = Prog()
        self.es = ExitStack()
        self.din = {}
        self.dout = {}
        self.ndsem = 0
        self.dsem_by_name = {}
        self.cc_n = 0
        self.rr = 0
        self.evt = 0

    def dram_in(self, name, shape, dtype=F32):
        ap = self.nc.dram_tensor(name, list(shape), dtype, kind="ExternalInput").ap()
        self.din[name] = ap
        return ap

    def dram_out(self, name, shape, dtype=F32):
        ap = self.nc.dram_tensor(name, list(shape), dtype, kind="ExternalOutput").ap()
        t = Tn(ap, Buf(name, self.new_dsem(name)))
        self.dout[name] = t
        return t

    def new_dsem(self, name):
        if name not in self.dsem_by_name:
            self.ndsem += 1
            self.dsem_by_name[name] = self.P.dsem("d%d_%s" % (self.ndsem, name))
        return self.dsem_by_name[name]

    def sb(self, name, shape, dtype, dma=False, sem=None):
        h = self.es.enter_context(self.nc.sbuf_tensor("sb_" + name, list(shape), dtype))
        ds = sem if sem is not None else (self.new_dsem(name) if dma else None)
        return Tn(h[:] if len(shape) == 2 else h[tuple([slice(None)] * len(shape))], Buf(name, ds, True))

    def arena_reset(self, keep=0):
        self.aoff = keep
        self.P.barrier()

    def ar_at(self, name, shape, dtype, off, dma=False):
        save = self.aoff
        self.aoff = off
        t = self.ar(name, shape, dtype, dma=dma)
        self.aoff = save
        return t

    def ar(self, name, shape, dtype, dma=False, sem=None):
        n = int(np.prod(shape[1:])) * _dt_size(dtype)
        n = (n + 63) // 64 * 64
        assert self.aoff + n <= self.arena_bytes, (name, self.aoff, n)
        ap = self.arena[:, self.aoff // 4:(self.aoff + n) // 4]
        self.aoff += n
        if dtype != F32:
            ap = ap.bitcast(dtype)
        ne = int(np.prod(shape[1:]))
        ap = ap[:, 0:ne]
        if len(shape) == 3:
            ap = ap.rearrange("p (a b) -> p a b", b=shape[2])
        elif len(shape) == 4:
            ap = ap.rearrange("p (a b c) -> p a b c", b=shape[2], c=shape[3])
        ds = sem if sem is not None else (self.new_dsem(name) if dma else None)
        return Tn(ap, Buf(name, ds, False))

    def dma(self, q, out_t, out_ap, in_ap, reads=()):
        self.P.op(q, lambda e: e.dma_start(out=out_ap, in_=in_ap), reads=[r.b for r in reads], writes=[out_t.b], dma=True)

    def mm(self, ps, ps_ap, lhsT, lhsT_ap, rhs, rhs_ap, start, stop, extra_reads=()):
        rd = [lhsT.b, rhs.b] + [r.b for r in extra_reads]
        self.P.op("pe", lambda e: e.matmul(ps_ap, lhsT=lhsT_ap, rhs=rhs_ap, start=start, stop=stop), reads=rd, writes=[ps.b])

    def tr(self, ps, ps_ap, src, src_ap):
        self.P.op("pe", lambda e: e.transpose(out=ps_ap, in_=src_ap, identity=self.ident.ap), reads=[src.b, self.ident.b], writes=[ps.b])

    def act(self, out_t, out_ap, in_t, in_ap, func, reads=(), writes=(), **kw):
        self.P.op("act", lambda e: e.activation(out=out_ap, in_=in_ap, func=func, **kw), reads=[in_t.b] + [r.b for r in reads],
                  writes=[out_t.b] + [w.b for w in writes])

    def vop(self, eng, method, out_t, reads, **kw):
        self.P.op(eng, lambda e: getattr(e, method)(**kw), reads=[r.b for r in reads], writes=[out_t.b])

    def evac(self, out_t, out_ap, ps, ps_ap, scale=None):
        self.evt += 1
        if self.evt % 2 == 0:
            if scale is None:
                self.act(out_t, out_ap, ps, ps_ap, AF.Copy)
            else:
                self.act(out_t, out_ap, ps, ps_ap, AF.Copy, scale=scale)
        else:
            if scale is None:
                self.vop("dve", "tensor_copy", out_t, [ps], out=out_ap, in_=ps_ap)
            else:
                self.vop("dve", "tensor_scalar", out_t, [ps], out=out_ap, in0=ps_ap, scalar1=scale, scalar2=None, op0=ALU.mult)

    def next_acc(self):
        self.rr = (self.rr + 1) % 4
        return self.pacc[self.rr]

    def next_slot(self):
        self.slot_i = (self.slot_i + 1) % len(self.wslots)
        return self.wslots[self.slot_i]

    def load_w(self, slot, w_ap, r0, nrows, c0, ncols, col_off=0, total_cols=None, elem_off=0):
        kc = nrows // 128
        tc_ = total_cols if total_cols is not None else ncols
        assert elem_off + kc * tc_ <= SLOT_ELEMS
        view = slot.ap[:, elem_off:elem_off + kc * tc_].rearrange("p (k c) -> p k c", c=tc_)
        src = w_ap[r0:r0 + nrows, c0:c0 + ncols].rearrange("(k p) c -> p k c", p=128)
        self.dma("pool", slot, view[:, :, col_off:col_off + ncols], src)
        return view

    def setup_common(self):
        nc = self.nc
        self.arena_bytes = 100 * 1024
        self.arena = self.es.enter_context(nc.sbuf_tensor("arena", [128, self.arena_bytes // 4], F32))
        self.aoff = 0
        self.xnT = self.sb("xnT", [128, KC, T], BF16)
        self.wslots = [self.sb("wslot%d" % i, [128, SLOT_ELEMS], BF16, dma=True) for i in range(4)]
        self.slot_i = -1
        self.ident = self.sb("ident", [128, 128], BF16, dma=True)
        self.ones = self.sb("ones", [128, 128], BF16)
        self.cst = self.sb("cst", [128, CST_N], F32, dma=True)
        self.lg = self.sb("lg", [128, 8], F32, dma=True)
        self.small = self.sb("small", [128, 64], F32)
        banks = [self.es.enter_context(nc.psum_tensor("pb%d" % i, [128, 512], F32)) for i in range(8)]
        self.pacc = [Tn(banks[i][:], Buf("pacc%d" % i)) for i in range(4)]
        self.patt = [Tn(banks[4 + i][:], Buf("patt%d" % i)) for i in range(2)]
        self.ptr = [Tn(banks[6 + i][:].bitcast(BF16), Buf("ptr%d" % i)) for i in range(2)]
        self.ptr_i = 0
        d_ident = self.dram_in("ident", [128, 128])
        d_cst = self.dram_in("cst", [128, CST_N])
        self.dma("pool", self.ident, self.ident.ap, d_ident)
        self.dma("sp", self.cst, self.cst.ap, d_cst)
        self.vop("dve", "memset", self.ones, [], ap=self.ones.ap, constant=1.0)

    def next_ptr(self):
        self.ptr_i ^= 1
        return self.ptr[self.ptr_i]

    def emit_norm(self, gain_dram, src_dram=None, src_tiles=None, ntiles=NT, dstT=None, gname="g", src_t=None):
        dstT = dstT or self.xnT
        g_bc = self.ar(gname + "_bc", [128, D], F32, dma=True)
        self.dma("sp", g_bc, g_bc.ap, gain_dram.partition_broadcast(128))
        junk = self.ar(gname + "_junk", [128, D], BF16)
        xnb = [self.ar(gname + "_xnb%d" % i, [128, D], BF16) for i in range(2)]
        st = self.ar(gname + "_st", [128, 4 * ntiles], F32)
        xs = None
        if src_dram is not None:
            xs = [self.ar(gname + "_xs%d" % i, [128, D], F32, dma=True) for i in range(2)]
        for tt in range(ntiles):
            if src_dram is not None:
                xt = xs[tt % 2]
                self.dma("sp", xt, xt.ap, src_dram[tt * 128:(tt + 1) * 128, :], reads=[src_t] if src_t is not None else ())
                x_ap = xt.ap
            else:
                xt = src_tiles[tt]
                x_ap = xt.ap
            c = 4 * tt
            self.act(junk, junk.ap, xt, x_ap, AF.Square, accum_out=st.ap[:, c:c + 1], writes=[st])
            self.vop("dve", "tensor_scalar", st, [st, junk], out=st.ap[:, c + 1:c + 2], in0=st.ap[:, c:c + 1],
                     scalar1=1.0 / D, scalar2=EPS, op0=ALU.mult, op1=ALU.add)
            self.act(st, st.ap[:, c + 2:c + 3], st, st.ap[:, c + 1:c + 2], AF.Sqrt)
            self.vop("dve", "reciprocal", st, [st], out=st.ap[:, c + 3:c + 4], in_=st.ap[:, c + 2:c + 3])
            xb = xnb[tt % 2]
            self.vop("dve", "scalar_tensor_tensor", xb, [xt, st, g_bc], out=xb.ap, in0=x_ap, scalar=st.ap[:, c + 3:c + 4],
                     in1=g_bc.ap, op0=ALU.mult, op1=ALU.mult)
            for rnd in range(2):
                pt = self.next_ptr()
                for i in range(8):
                    kc = rnd * 8 + i
                    self.tr(pt, pt.ap[:, i * 128:(i + 1) * 128], xb, xb.ap[:, kc * 128:(kc + 1) * 128])
                self.evac(dstT, dstT.ap[:, rnd * 8:(rnd + 1) * 8, tt * 128:(tt + 1) * 128],
                          pt, pt.ap.rearrange("p (a b) -> p a b", b=128))

    def emit_decay(self, d_decay):
        self.dma("sp", self.lg, self.lg.ap, d_decay.partition_broadcast(128))
        self.act(self.lg, self.lg.ap, self.lg, self.lg.ap, AF.Exp)
        self.vop("dve", "tensor_scalar", self.lg, [self.lg], out=self.lg.ap, in0=self.lg.ap, scalar1=-1.0, scalar2=None, op0=ALU.mult)

    def exp_scaled(self, out_t, out_ap, in_t, in_ap, col, mul=None):
        self.act(out_t, out_ap, in_t, in_ap, AF.Exp, scale=self.lg.ap[:, col:col + 1], reads=[self.lg])
        if mul is not None:
            self.vop("dve", "tensor_scalar", out_t, [out_t], out=out_ap, in0=out_ap, scalar1=mul, scalar2=None, op0=ALU.mult)

    def build_A(self):
        nc = self.nc
        self.setup_common()
        x_d = self.dram_in("x", [T, D])
        g_d = self.dram_in("g_mix", [D])
        w_in = self.dram_in("w_in", [D, 13824])
        dec_d = self.dram_in("ret_decay", [8])
        cs_d = self.dram_in("rot_cs", [T, 128])
        nsc_d = self.dram_in("rot_nsc", [T, 128])
        o_kaT = self.dram_out("kaT", [6, 128, T], BF16)
        o_kcT = self.dram_out("kcT", [6, 128, T], BF16)
        o_va = self.dram_out("va", [T, 768], BF16)
        o_vc = self.dram_out("vc", [T, 768], BF16)
        o_L = self.dram_out("L", [2, 4, 128, 256], F32)
        self.aoff = 0
        self.part_A(x_d, None, g_d, w_in, dec_d, cs_d, nsc_d, (o_kaT, o_kcT, o_va, o_vc, o_L))

    def part_A(self, x_d, x_t, g_d, w_in, dec_d, cs_d, nsc_d, outs, mid_cb=None):
        o_kaT, o_kcT, o_va, o_vc, o_L = outs
        self.emit_decay(dec_d)
        self.emit_norm(g_d, src_dram=x_d, src_t=x_t)
        self.arena_reset()
        kT = self.ar("kT", [128, 6, T], BF16)
        vtm = self.ar("vtm", [128, NT, 768], BF16)

        def load_blocks(c0):
            blks = []
            for (cb, ncol) in ((0, 512), (512, 256)):
                slot = self.next_slot()
                blks.append((slot, self.load_w(slot, w_in, 0, D, c0 + cb, ncol), cb, ncol))
            return blks

        def proj_k(blks, o_t):
            for (slot, wv, cb, ncol) in blks:
                for m in range(ncol // 128):
                    h = (cb // 128) + m
                    for half in range(2):
                        ps = self.next_acc()
                        for kc in range(KC):
                            self.mm(ps, ps.ap, slot, wv[:, kc, m * 128:(m + 1) * 128], self.xnT,
                                    self.xnT.ap[:, kc, half * 512:(half + 1) * 512], kc == 0, kc == KC - 1)
                        self.evac(kT, kT.ap[:, h, half * 512:(half + 1) * 512], ps, ps.ap)
            self.store_kT(o_t, kT)

        def proj_v(blks, o_t):
            for (slot, wv, cb, ncol) in blks:
                for tt in range(NT):
                    ps = self.next_acc()
                    for kc in range(KC):
                        self.mm(ps, ps.ap[:, 0:ncol], self.xnT, self.xnT.ap[:, kc, tt * 128:(tt + 1) * 128], slot,
                                wv[:, kc, :], kc == 0, kc == KC - 1)
                    self.evac(vtm, vtm.ap[:, tt, cb:cb + ncol], ps, ps.ap[:, 0:ncol])
            self.store_v(o_t, vtm)
        proj_k(load_blocks(C_KA), o_kaT)
        proj_v(load_blocks(C_VA), o_va)
        kc_blks = load_blocks(C_KC)
        vc_blks = load_blocks(C_VC)
        if mid_cb is not None:
            mid_cb(0)
        proj_k(kc_blks, o_kcT)
        proj_v(vc_blks, o_vc)
        rot = self.load_rot(cs_d, nsc_d)
        ZF = self.ar("ZF", [128, NT, 4], F32)
        ZB = self.ar("ZB", [128, NT, 4], F32)
        for h in range(4):
            self.exp_scaled(ZF, ZF.ap[:, :, h], self.cst, self.cst.ap[:, CST_ZE:CST_ZE + 8], h, mul=INV_SQRT_HD)
            self.exp_scaled(ZB, ZB.ap[:, :, h], self.cst, self.cst.ap[:, CST_ZB:CST_ZB + 8], 4 + h, mul=INV_SQRT_HD)
        krot = self.ar("krot", [128, NT, 512], F32)
        vb = self.ar("vb", [128, NT, 1024], BF16)
        kzf = self.ar("kzf", [128, NT, 512], BF16)
        kzb = self.ar("kzb", [128, NT, 512], BF16)
        pre = []
        for c0 in (C_KB, C_VB, C_VB + 512):
            slot = self.next_slot()
            pre.append((slot, self.load_w(slot, w_in, 0, D, c0, 512)))
        if mid_cb is not None:
            mid_cb(1)
        slot, wv = pre[0]
        for tt in range(NT):
            ps = self.next_acc()
            for kc in range(KC):
                self.mm(ps, ps.ap, self.xnT, self.xnT.ap[:, kc, tt * 128:(tt + 1) * 128], slot, wv[:, kc, :], kc == 0, kc == KC - 1)
            self.rotary(krot, krot.ap[:, tt, :], ps, ps.ap, rot, tt, 4)
            for (zt, kz) in ((ZF, kzf), (ZB, kzb)):
                self.vop("dve", "tensor_tensor", kz, [krot, zt], out=kz.ap[:, tt, :].rearrange("p (h d) -> p h d", d=128),
                         in0=krot.ap[:, tt, :].rearrange("p (h d) -> p h d", d=128),
                         in1=zt.ap[:, tt, :].unsqueeze(2).to_broadcast([128, 4, 128]), op=ALU.mult)
        for cb in range(2):
            slot, wv = pre[1 + cb]
            for tt in range(NT):
                ps = self.next_acc()
                for kc in range(KC):
                    self.mm(ps, ps.ap, self.xnT, self.xnT.ap[:, kc, tt * 128:(tt + 1) * 128], slot, wv[:, kc, :], kc == 0, kc == KC - 1)
                self.evac(vb, vb.ap[:, tt, cb * 512:(cb + 1) * 512], ps, ps.ap)
        Ls = self.ar("Ls", [128, 2, 4, 256], F32)
        for di, kz in enumerate((kzf, kzb)):
            for h in range(4):
                ps = self.next_acc()
                for tt in range(NT):
                    self.mm(ps, ps.ap[:, 0:256], kz, kz.ap[:, tt, h * 128:(h + 1) * 128], vb, vb.ap[:, tt, h * 256:(h + 1) * 256],
                            tt == 0, tt == NT - 1)
                self.evac(Ls, Ls.ap[:, di, h, :], ps, ps.ap[:, 0:256])
        self.dma("sp", o_L, o_L.ap.rearrange("d h p v -> p d h v"), Ls.ap, reads=[Ls])

    def load_rot(self, cs_d, nsc_d):
        cs = [self.ar("rot_cs%d" % i, [128, 128], F32, dma=True) for i in range(2)]
        nsc = [self.ar("rot_nsc%d" % i, [128, 128], F32, dma=True) for i in range(2)]
        t1 = self.ar("rot_t1", [128, 512], F32)
        t2 = self.ar("rot_t2", [128, 512], F32)
        q32 = self.ar("rot_q32", [128, 512], F32)
        return (cs, nsc, t1, t2, q32, cs_d, nsc_d)

    def rotary(self, out_t, out_ap, ps, ps_ap, rot, tt, ng):
        csl, nscl, t1, t2, q32, cs_d, nsc_d = rot
        cs, nsc = csl[tt % 2], nscl[tt % 2]
        self.dma("sp", cs, cs.ap, cs_d[tt * 128:(tt + 1) * 128, :])
        self.dma("sp", nsc, nsc.ap, nsc_d[tt * 128:(tt + 1) * 128, :])
        n = ng * 128
        self.act(q32, q32.ap[:, 0:n], ps, ps_ap, AF.Copy)
        q4 = q32.ap[:, 0:n].rearrange("p (g s d) -> p g s d", s=2, d=64)
        shp = [128, ng, 2, 64]
        csb = cs.ap.rearrange("p (s d) -> p s d", d=64).unsqueeze(1).to_broadcast(shp)
        nscb = nsc.ap.rearrange("p (s d) -> p s d", d=64).unsqueeze(1).to_broadcast(shp)
        t1v = t1.ap[:, 0:n].rearrange("p (g s d) -> p g s d", s=2, d=64)
        t2v = t2.ap[:, 0:n].rearrange("p (g s d) -> p g s d", s=2, d=64)
        self.vop("dve", "tensor_tensor", t1, [q32, cs], out=t1v, in0=q4[:, :, 0, :].unsqueeze(2).to_broadcast(shp), in1=csb, op=ALU.mult)
        self.vop("dve", "tensor_tensor", t2, [q32, nsc], out=t2v, in0=q4[:, :, 1, :].unsqueeze(2).to_broadcast(shp), in1=nscb, op=ALU.mult)
        self.vop("dve", "tensor_tensor", out_t, [t1, t2], out=out_ap, in0=t1.ap[:, 0:n], in1=t2.ap[:, 0:n], op=ALU.add)


    def build_B(self, stop=None):
        self.setup_common()
        KB = 1024
        x_d = self.dram_in("x", [T, D])
        g_mix = self.dram_in("g_mix", [D])
        g_cross = self.dram_in("g_cross", [D])
        g_mem = self.dram_in("g_mem", [D])
        g_mlp = self.dram_in("g_mlp", [D])
        w_in = self.dram_in("w_in", [D, 13824])
        w_branch = self.dram_in("w_branch", [2560, D])
        w_out = self.dram_in("w_out", [D, D])
        w_cq = self.dram_in("w_cq", [D, 512])
        w_ckv = self.dram_in("w_ckv", [D, 1024])
        w_co = self.dram_in("w_co", [512, D])
        w_mlp1 = self.dram_in("w_mlp1", [D, 8192])
        w_mlp2 = self.dram_in("w_mlp2", [8192, D])
        mem_d = self.dram_in("mem", [256, D])
        dec_d = self.dram_in("ret_decay", [8])
        cs_d = self.dram_in("rot_cs", [T, 128])
        nsc_d = self.dram_in("rot_nsc", [T, 128])
        expo_d = self.dram_in("expo", [128, 16])
        biasA_d = self.dram_in("biasA", [6, 16, 128, 512])
        biasC_d = self.dram_in("biasC", [6, 128, DIL_STRIP])
        lmC_d = self.dram_in("lmC", [128, DIL_STRIP])
        vcol_d = self.dram_in("vcolC", [128, 24])
        kaT_d = self.dram_in("kaT_h", [6, 128, 1536], BF16)
        va_d = self.dram_in("va_h", [1536, 768], BF16)
        kcT_d = self.dram_in("kcT_h", [6, 128, 3072], BF16)
        vc_d = self.dram_in("vc_h", [3072, 768], BF16)
        Lall_d = self.dram_in("Lall", [2, 4, 4, 128, 256])
        x_out = self.dram_out("x_out", [T, D])
        KmT = self.sb("KmT", [128, 4, 256], BF16)
        Vm = self.sb("Vm", [128, 2, 512], BF16)
        a = dict(x_d=x_d, g_mix=g_mix, g_cross=g_cross, g_mem=g_mem, g_mlp=g_mlp, w_in=w_in, w_branch=w_branch, w_out=w_out, w_cq=w_cq,
                 w_ckv=w_ckv, w_co=w_co, w_mlp1=w_mlp1, w_mlp2=w_mlp2, mem_d=mem_d, dec_d=dec_d, cs_d=cs_d, nsc_d=nsc_d, expo_d=expo_d,
                 biasA_d=biasA_d, biasC_d=biasC_d, lmC_d=lmC_d, vcol_d=vcol_d, kaT_d=kaT_d, va_d=va_d, kcT_d=kcT_d, vc_d=vc_d,
                 Lall_d=Lall_d, x_out=x_out, KmT=KmT, Vm=Vm)
        self.part_B(a, stop=stop)

    def part_B(self, a, stop=None, norm1=True, final_g=None):
        KB = 1024
        (x_d, g_mix, g_cross, g_mem, g_mlp, w_in, w_branch, w_out, w_cq, w_ckv, w_co, w_mlp1, w_mlp2, mem_d, dec_d, cs_d, nsc_d, expo_d,
         biasA_d, biasC_d, lmC_d, vcol_d, kaT_d, va_d, kcT_d, vc_d, Lall_d, x_out, KmT, Vm) = [a[k] for k in (
            "x_d", "g_mix", "g_cross", "g_mem", "g_mlp", "w_in", "w_branch", "w_out", "w_cq", "w_ckv", "w_co", "w_mlp1", "w_mlp2", "mem_d",
            "dec_d", "cs_d", "nsc_d", "expo_d", "biasA_d", "biasC_d", "lmC_d", "vcol_d", "kaT_d", "va_d", "kcT_d", "vc_d", "Lall_d",
            "x_out", "KmT", "Vm")]
        x_t = a.get("x_t")
        halo_t = a.get("halo_t", ())
        lq = a.get("L_queue", "sp")
        xnT = self.xnT
        OT_OFF, MG_OFF, X_BYTES = 0, 68 * KB, 64 * KB

        self.aoff = 0
        self.emit_decay(dec_d)
        mnT = self.ar("mnT", [128, KC, 256], BF16)
        self.emit_norm(g_mem, src_dram=mem_d, ntiles=2, dstT=mnT, gname="gm")
        slot = self.next_slot()
        wv = self.load_w(slot, w_ckv, 0, D, 0, 512)
        for h in range(4):
            ps = self.next_acc()
            for kc in range(KC):
                self.mm(ps, ps.ap[:, 0:256], slot, wv[:, kc, h * 128:(h + 1) * 128], mnT, mnT.ap[:, kc, :], kc == 0, kc == KC - 1)
            self.evac(KmT, KmT.ap[:, h, :], ps, ps.ap[:, 0:256])
        slot = self.next_slot()
        wv = self.load_w(slot, w_ckv, 0, D, 512, 512)
        for t in range(2):
            ps = self.next_acc()
            for kc in range(KC):
                self.mm(ps, ps.ap, mnT, mnT.ap[:, kc, t * 128:(t + 1) * 128], slot, wv[:, kc, :], kc == 0, kc == KC - 1)
            self.evac(Vm, Vm.ap[:, t, :], ps, ps.ap)
        if norm1:
            self.arena_reset()
            self.emit_norm(g_mix, src_dram=x_d, src_t=x_t)
        self.arena_reset()
        oT = self.ar_at("oT", [128, 20, T], BF16, OT_OFF)
        self.aoff = 40 * KB
        self.emit_retention(w_in, cs_d, nsc_d, expo_d, Lall_d, oT, lq=lq)
        self.arena_reset(keep=40 * KB)
        self.emit_attn(w_in, C_QA, kaT_d, va_d, biasA_d, oT, 0, nkt_halo=12, nr=8, name="na", halo_t=halo_t)
        self.arena_reset(keep=40 * KB)
        self.emit_attn(w_in, C_QC, kcT_d, vc_d, biasC_d, oT, 14, nkt_halo=24, nr=20, name="dil", lm_d=lmC_d, vcol_d=vcol_d, halo_t=halo_t)
        if stop == "mix":
            o = self.dram_out("oT_dbg", [20, 128, T], BF16)
            self.dma("sp", o, o.ap.rearrange("c p t -> p c t"), oT.ap, reads=[oT])
            return
        self.arena_reset(keep=40 * KB)
        mergedT = self.ar_at("mergedT", [128, KC, T], BF16, MG_OFF)
        SIG = [self.ar("sig%d" % g, [128, 512], F32) for g in range(3)]
        M = [self.ar("mrg%d" % g, [128, 512], F32) for g in range(3)]
        for fc in range(16):
            sw = self.next_slot()
            wbv = self.load_w(sw, w_branch, 0, 2560, fc * 128, 128)
            gsl = []
            for g in range(2):
                gsl.append((sw, self.load_w(sw, w_in, 0, D, C_SA + g * 2048 + fc * 128, 128, elem_off=2560 + g * 2048)))
            s2 = self.next_slot()
            gsl.append((s2, self.load_w(s2, w_in, 0, D, C_SA + 2 * 2048 + fc * 128, 128)))
            for half in range(2):
                hs = slice(half * 512, (half + 1) * 512)
                for g in range(3):
                    ps = self.next_acc()
                    gs_, gv = gsl[g]
                    for kc in range(KC):
                        self.mm(ps, ps.ap, gs_, gv[:, kc, :], xnT, xnT.ap[:, kc, hs], kc == 0, kc == KC - 1)
                    self.act(SIG[g], SIG[g].ap, ps, ps.ap, AF.Sigmoid)
                for g, (k0, nk) in enumerate(((0, 6), (6, 8), (14, 6))):
                    ps = self.next_acc()
                    for k in range(nk):
                        self.mm(ps, ps.ap, sw, wbv[:, k0 + k, :], oT, oT.ap[:, k0 + k, hs], k == 0, k == nk - 1)
                    self.vop("dve", "tensor_tensor", M[g], [ps, SIG[g]], out=M[g].ap, in0=ps.ap, in1=SIG[g].ap, op=ALU.mult)
                self.vop("dve", "tensor_tensor", M[0], [M[0], M[1]], out=M[0].ap, in0=M[0].ap, in1=M[1].ap, op=ALU.add)
                self.vop("dve", "tensor_tensor", mergedT, [M[0], M[2]], out=mergedT.ap[:, fc, hs], in0=M[0].ap, in1=M[2].ap, op=ALU.add)
        self.arena_reset(keep=X_BYTES)
        xs = []
        for tt in range(NT):
            xt = self.ar_at("x%d" % tt, [128, D], F32, tt * 8 * KB, dma=True)
            self.dma("sp", xt, xt.ap, x_d[tt * 128:(tt + 1) * 128, :], reads=[x_t] if x_t is not None else ())
            xs.append(xt)
        for cb in range(4):
            slot = self.next_slot()
            wv = self.load_w(slot, w_out, 0, D, cb * 512, 512)
            for tt in range(NT):
                ps = self.next_acc()
                for kc in range(KC):
                    self.mm(ps, ps.ap, mergedT, mergedT.ap[:, kc, tt * 128:(tt + 1) * 128], slot, wv[:, kc, :], kc == 0, kc == KC - 1)
                xa = xs[tt].ap[:, cb * 512:(cb + 1) * 512]
                self.vop("dve", "tensor_tensor", xs[tt], [xs[tt], ps], out=xa, in0=xa, in1=ps.ap, op=ALU.add)
        if stop == "x1":
            for tt in range(NT):
                self.dma("sp", x_out, x_out.ap[tt * 128:(tt + 1) * 128, :], xs[tt].ap, reads=[xs[tt]])
            return
        self.arena_reset(keep=X_BYTES)
        self.emit_norm(g_cross, src_tiles=xs, gname="gc")
        self.arena_reset(keep=X_BYTES)
        QxT = self.ar("QxT", [128, T], BF16)
        PT = [self.ar("cPT%d" % i, [128, 512], BF16) for i in range(2)]
        rc = self.ar("crc", [128, 512], F32)
        ocT = self.ar("ocT", [128, 4, T], BF16)
        for h in range(4):
            slot = self.next_slot()
            wv = self.load_w(slot, w_cq, 0, D, h * 128, 128)
            for half in range(2):
                ps = self.next_acc()
                for kc in range(KC):
                    self.mm(ps, ps.ap, slot, wv[:, kc, :], xnT, xnT.ap[:, kc, half * 512:(half + 1) * 512], kc == 0, kc == KC - 1)
                self.evac(QxT, QxT.ap[:, half * 512:(half + 1) * 512], ps, ps.ap, scale=INV_SQRT_HD)
            for half in range(2):
                hs = slice(half * 512, (half + 1) * 512)
                num, den = self.patt
                for kt in range(2):
                    sc = self.next_acc()
                    self.mm(sc, sc.ap, KmT, KmT.ap[:, h, kt * 128:(kt + 1) * 128], QxT, QxT.ap[:, hs], True, True)
                    p_ = PT[kt]
                    self.act(p_, p_.ap, sc, sc.ap, AF.Exp)
                    self.mm(num, num.ap, Vm, Vm.ap[:, kt, h * 128:(h + 1) * 128], p_, p_.ap, kt == 0, kt == 1)
                    self.mm(den, den.ap, self.ones, self.ones.ap, p_, p_.ap, kt == 0, kt == 1)
                self.vop("dve", "reciprocal", rc, [den], out=rc.ap, in_=den.ap)
                self.vop("dve", "tensor_tensor", ocT, [num, rc], out=ocT.ap[:, h, hs], in0=num.ap, in1=rc.ap, op=ALU.mult)
        for cb in range(4):
            slot = self.next_slot()
            wv = self.load_w(slot, w_co, 0, 512, cb * 512, 512)
            for tt in range(NT):
                ps = self.next_acc()
                for k in range(4):
                    self.mm(ps, ps.ap, ocT, ocT.ap[:, k, tt * 128:(tt + 1) * 128], slot, wv[:, k, :], k == 0, k == 3)
                xa = xs[tt].ap[:, cb * 512:(cb + 1) * 512]
                self.vop("dve", "tensor_tensor", xs[tt], [xs[tt], ps], out=xa, in0=xa, in1=ps.ap, op=ALU.add)
        if stop == "x2":
            for tt in range(NT):
                self.dma("sp", x_out, x_out.ap[tt * 128:(tt + 1) * 128, :], xs[tt].ap, reads=[xs[tt]])
            return
        self.arena_reset(keep=X_BYTES)
        self.emit_norm(g_mlp, src_tiles=xs, gname="gl")
        self.arena_reset(keep=X_BYTES)
        hT = self.ar("hT", [128, KC, T], BF16)
        r32 = [self.ar("r32_%d" % i, [128, 512], F32) for i in range(2)]
        ri = 0
        for q in range(4):
            for blk in range(4):
                slot = self.next_slot()
                wv = self.load_w(slot, w_mlp1, 0, D, q * 2048 + blk * 512, 512)
                for m in range(4):
                    hc = blk * 4 + m
                    for half in range(2):
                        ps = self.next_acc()
                        for kc in range(KC):
                            self.mm(ps, ps.ap, slot, wv[:, kc, m * 128:(m + 1) * 128], xnT, xnT.ap[:, kc, half * 512:(half + 1) * 512],
                                    kc == 0, kc == KC - 1)
                        ri ^= 1
                        r_ = r32[ri]
                        self.act(r_, r_.ap, ps, ps.ap, AF.Relu)
                        self.vop("dve", "tensor_tensor", hT, [r_], out=hT.ap[:, hc, half * 512:(half + 1) * 512], in0=r_.ap, in1=r_.ap, op=ALU.mult)
            for cb in range(4):
                slot = self.next_slot()
                wv = self.load_w(slot, w_mlp2, q * 2048, 2048, cb * 512, 512)
                for tt in range(NT):
                    ps = self.next_acc()
                    for k in range(KC):
                        self.mm(ps, ps.ap, hT, hT.ap[:, k, tt * 128:(tt + 1) * 128], slot, wv[:, k, :], k == 0, k == KC - 1)
                    xa = xs[tt].ap[:, cb * 512:(cb + 1) * 512]
                    self.vop("dve", "tensor_tensor", xs[tt], [xs[tt], ps], out=xa, in0=xa, in1=ps.ap, op=ALU.add)
        if final_g is None:
            for tt in range(NT):
                self.dma("sp", x_out, x_out.ap[tt * 128:(tt + 1) * 128, :], xs[tt].ap, reads=[xs[tt]])
            return
        self.arena_reset(keep=X_BYTES)
        g_bc = self.ar("gf_bc", [128, D], F32, dma=True)
        self.dma("sp", g_bc, g_bc.ap, final_g.partition_broadcast(128))
        junk = self.ar("gf_junk", [128, D], BF16)
        st = self.ar("gf_st", [128, 4 * NT], F32)
        ys = [self.ar("gf_ys%d" % i, [128, D], F32) for i in range(2)]
        youts = [Tn(x_out.ap, Buf("y_st%d" % i, self.new_dsem("y_st%d" % i))) for i in range(2)]
        for tt in range(NT):
            xt, yt = xs[tt], ys[tt % 2]
            c = 4 * tt
            self.act(junk, junk.ap, xt, xt.ap, AF.Square, accum_out=st.ap[:, c:c + 1], writes=[st])
            self.vop("dve", "tensor_scalar", st, [st], out=st.ap[:, c + 1:c + 2], in0=st.ap[:, c:c + 1], scalar1=1.0 / D, scalar2=EPS, op0=ALU.mult, op1=ALU.add)
            self.act(st, st.ap[:, c + 2:c + 3], st, st.ap[:, c + 1:c + 2], AF.Sqrt)
            self.vop("dve", "reciprocal", st, [st], out=st.ap[:, c + 3:c + 4], in_=st.ap[:, c + 2:c + 3])
            self.vop("dve", "scalar_tensor_tensor", yt, [xt, st, g_bc], out=yt.ap, in0=xt.ap, scalar=st.ap[:, c + 3:c + 4], in1=g_bc.ap, op0=ALU.mult, op1=ALU.mult)
            self.dma("sp", youts[tt % 2], x_out.ap[tt * 128:(tt + 1) * 128, :], yt.ap, reads=[yt])

    def emit_retention(self, w_in, cs_d, nsc_d, expo_d, Lall_d, oT, lq="sp"):
        cst, small = self.cst, self.small
        rot = self.load_rot(cs_d, nsc_d)
        DT = self.ar("DT", [128, 4, 128], F32)
        XIF = self.ar("XIF", [128, 4, 128], F32)
        ZBB = self.ar("ZBB", [128, 4, 128], F32)
        tmpD = self.ar("tmpD", [128, 128], F32)
        expo = self.ar("expo", [128, 16], F32, dma=True)
        self.dma("sp", expo, expo.ap, expo_d)
        wS = self.ar("wS", [128, 2, 4, 4], F32)
        for h in range(4):
            self.exp_scaled(DT, DT.ap[:, h, :], cst, cst.ap[:, CST_PF:CST_PF + 128], h)
            self.vop("dve", "tensor_tensor", DT, [DT, cst], out=DT.ap[:, h, :], in0=DT.ap[:, h, :], in1=cst.ap[:, CST_UF:CST_UF + 128], op=ALU.mult)
            self.exp_scaled(tmpD, tmpD.ap, cst, cst.ap[:, CST_PB:CST_PB + 128], 4 + h)
            self.vop("dve", "tensor_tensor", tmpD, [tmpD, cst], out=tmpD.ap, in0=tmpD.ap, in1=cst.ap[:, CST_UB:CST_UB + 128], op=ALU.mult)
            self.vop("dve", "tensor_tensor", DT, [DT, tmpD], out=DT.ap[:, h, :], in0=DT.ap[:, h, :], in1=tmpD.ap, op=ALU.add)
            self.vop("dve", "tensor_scalar", DT, [DT], out=DT.ap[:, h, :], in0=DT.ap[:, h, :], scalar1=INV_SQRT_HD, scalar2=None, op0=ALU.mult)
            self.exp_scaled(XIF, XIF.ap[:, h, :], cst, cst.ap[:, CST_N1:CST_N1 + 128], h)
            self.exp_scaled(ZBB, ZBB.ap[:, h, :], cst, cst.ap[:, CST_N2:CST_N2 + 128], 4 + h)
            self.exp_scaled(small, small.ap[:, h:h + 1], cst, cst.ap[:, CST_M2:CST_M2 + 1], h, mul=INV_SQRT_HD)
            self.exp_scaled(small, small.ap[:, 4 + h:5 + h], cst, cst.ap[:, CST_M1:CST_M1 + 1], 4 + h, mul=INV_SQRT_HD)
            for di in range(2):
                self.exp_scaled(wS, wS.ap[:, di, h, :], expo, expo.ap[:, 8 * di:8 * di + 4], 4 * di + h)
                self.vop("dve", "tensor_tensor", wS, [wS, expo], out=wS.ap[:, di, h, :], in0=wS.ap[:, di, h, :],
                         in1=expo.ap[:, 8 * di + 4:8 * di + 8], op=ALU.mult)
        self.act(small, small.ap[:, 8:16], self.lg, self.lg.ap, AF.Exp, scale=128.0)
        QKT = self.ar("QKT", [128, 2, T], BF16)
        Qxf = self.ar("Qxf", [128, NT, 128], BF16)
        Qzb = self.ar("Qzb", [128, NT, 128], BF16)
        Kzf = self.ar("Kzf", [128, NT, 128], BF16)
        Kxb = self.ar("Kxb", [128, NT, 128], BF16)
        vbh = self.ar("vbh", [128, NT, 256], BF16)
        sg = self.ar("sg", [128, NT, 256], BF16)
        rotbf = [self.ar("rotbf%d" % i, [128, 256], BF16) for i in range(2)]
        Sbf = [self.ar("Sbf%d" % d, [128, NT, 256], BF16) for d in range(2)]
        S32 = [[self.ar("S32_%d%d" % (d, i), [128, 256], F32) for i in range(2)] for d in range(2)]
        Lh = [self.ar("Lh%d" % d, [128, 4, 256], F32, dma=True) for d in range(2)]
        AT = [self.ar("AT%d" % i, [128, 128], BF16) for i in range(2)]
        junk = self.ar("rjunk", [128, 256], BF16)
        stt = [self.ar("stt%d" % i, [128, 8], F32) for i in range(2)]
        yn = [self.ar("yn%d" % i, [128, 256], F32) for i in range(2)]
        ob = [self.ar("ob%d" % i, [128, 256], BF16) for i in range(2)]
        xnT = self.xnT
        for h in range(4):
            sA = self.next_slot()
            vA = self.load_w(sA, w_in, 0, D, C_QB + 128 * h, 128, col_off=0, total_cols=512)
            self.load_w(sA, w_in, 0, D, C_KB + 128 * h, 128, col_off=128, total_cols=512)
            self.load_w(sA, w_in, 0, D, C_VB + 256 * h, 256, col_off=256, total_cols=512)
            sB = self.next_slot()
            vB = self.load_w(sB, w_in, 0, D, C_GR + 256 * h, 256)
            for di in range(2):
                self.dma(lq, Lh[di], Lh[di].ap, Lall_d[di, :, h].rearrange("j p v -> p j v"))
            for tt in range(NT):
                ts = slice(tt * 128, (tt + 1) * 128)
                ps = self.next_acc()
                for kc in range(KC):
                    self.mm(ps, ps.ap, xnT, xnT.ap[:, kc, ts], sA, vA[:, kc, :], kc == 0, kc == KC - 1)
                rb = rotbf[tt % 2]
                self.rotary(rb, rb.ap, ps, ps.ap[:, 0:256], rot, tt, 2)
                self.evac(vbh, vbh.ap[:, tt, :], ps, ps.ap[:, 256:512])
                pt = self.next_ptr()
                self.tr(pt, pt.ap[:, 0:128], rb, rb.ap[:, 0:128])
                self.tr(pt, pt.ap[:, 128:256], rb, rb.ap[:, 128:256])
                self.evac(QKT, QKT.ap[:, :, ts], pt, pt.ap[:, 0:256].rearrange("p (a b) -> p a b", b=128))
                self.vop("dve", "tensor_scalar", Kzf, [rb, small], out=Kzf.ap[:, tt, :], in0=rb.ap[:, 128:256], scalar1=small.ap[:, h:h + 1],
                         scalar2=None, op0=ALU.mult)
                self.vop("dve", "tensor_scalar", Kxb, [rb, small], out=Kxb.ap[:, tt, :], in0=rb.ap[:, 128:256], scalar1=small.ap[:, 4 + h:5 + h],
                         scalar2=None, op0=ALU.mult)
                ps2 = self.next_acc()
                for kc in range(KC):
                    self.mm(ps2, ps2.ap[:, 0:256], xnT, xnT.ap[:, kc, ts], sB, vB[:, kc, :], kc == 0, kc == KC - 1)
                self.act(sg, sg.ap[:, tt, :], ps2, ps2.ap[:, 0:256], AF.Silu)
            q3 = QKT.ap[:, 0, :].rearrange("p (c n) -> p c n", n=128)
            self.vop("dve", "tensor_tensor", Qxf, [QKT, XIF], out=Qxf.ap, in0=q3, in1=XIF.ap[:, h, :].unsqueeze(1).to_broadcast([128, NT, 128]), op=ALU.mult)
            self.vop("dve", "tensor_tensor", Qzb, [QKT, ZBB], out=Qzb.ap, in0=q3, in1=ZBB.ap[:, h, :].unsqueeze(1).to_broadcast([128, NT, 128]), op=ALU.mult)
            for di in range(2):
                s0 = S32[di][0]
                self.vop("dve", "tensor_scalar", s0, [Lh[di], wS], out=s0.ap, in0=Lh[di].ap[:, 0, :], scalar1=wS.ap[:, di, h, 0:1], scalar2=None, op0=ALU.mult)
                for j in range(1, 4):
                    self.vop("dve", "scalar_tensor_tensor", s0, [Lh[di], wS, s0], out=s0.ap, in0=Lh[di].ap[:, j, :], scalar=wS.ap[:, di, h, j:j + 1],
                             in1=s0.ap, op0=ALU.mult, op1=ALU.add)
            for di in range(2):
                order = list(range(NT)) if di == 0 else list(range(NT - 1, -1, -1))
                kz = Kzf if di == 0 else Kxb
                gcol = 8 + 4 * di + h
                cur = 0
                self.act(Sbf[di], Sbf[di].ap[:, order[0], :], S32[di][0], S32[di][0].ap, AF.Copy)
                for idx in range(NT - 1):
                    i = order[idx]
                    ps = self.next_acc()
                    self.mm(ps, ps.ap[:, 0:256], kz, kz.ap[:, i, :], vbh, vbh.ap[:, i, :], True, True)
                    nxt = cur ^ 1
                    self.vop("dve", "scalar_tensor_tensor", S32[di][nxt], [S32[di][cur], small, ps], out=S32[di][nxt].ap, in0=S32[di][cur].ap,
                             scalar=small.ap[:, gcol:gcol + 1], in1=ps.ap[:, 0:256], op0=ALU.mult, op1=ALU.add)
                    self.act(Sbf[di], Sbf[di].ap[:, order[idx + 1], :], S32[di][nxt], S32[di][nxt].ap, AF.Copy)
                    cur = nxt
            for i in range(NT):
                ts = slice(i * 128, (i + 1) * 128)
                sc = self.next_acc()
                self.mm(sc, sc.ap[:, 0:128], QKT, QKT.ap[:, 1, ts], QKT, QKT.ap[:, 0, ts], True, True)
                at = AT[i % 2]
                self.vop("dve", "tensor_tensor", at, [sc, DT], out=at.ap, in0=sc.ap[:, 0:128], in1=DT.ap[:, h, :], op=ALU.mult)
                o = self.patt[i % 2]
                oa = o.ap[:, 0:256]
                self.mm(o, oa, at, at.ap, vbh, vbh.ap[:, i, :], True, False)
                self.mm(o, oa, Qxf, Qxf.ap[:, i, :], Sbf[0], Sbf[0].ap[:, i, :], False, False)
                self.mm(o, oa, Qzb, Qzb.ap[:, i, :], Sbf[1], Sbf[1].ap[:, i, :], False, True)
                st = stt[i % 2]
                self.act(junk, junk.ap, o, oa, AF.Copy, accum_out=st.ap[:, 0:1], writes=[st])
                self.act(junk, junk.ap, o, oa, AF.Square, accum_out=st.ap[:, 1:2], writes=[st])
                self.vop("dve", "tensor_scalar", st, [st], out=st.ap[:, 2:4], in0=st.ap[:, 0:2], scalar1=1.0 / 256, scalar2=None, op0=ALU.mult)
                self.vop("dve", "tensor_tensor", st, [st], out=st.ap[:, 4:5], in0=st.ap[:, 2:3], in1=st.ap[:, 2:3], op=ALU.mult)
                self.vop("dve", "tensor_tensor", st, [st], out=st.ap[:, 5:6], in0=st.ap[:, 3:4], in1=st.ap[:, 4:5], op=ALU.subtract)
                self.vop("dve", "tensor_scalar", st, [st], out=st.ap[:, 5:6], in0=st.ap[:, 5:6], scalar1=EPS, scalar2=None, op0=ALU.add)
                self.act(st, st.ap[:, 6:7], st, st.ap[:, 5:6], AF.Sqrt)
                self.vop("dve", "reciprocal", st, [st], out=st.ap[:, 7:8], in_=st.ap[:, 6:7])
                y_ = yn[i % 2]
                self.vop("dve", "tensor_scalar", y_, [o, st], out=y_.ap, in0=oa, scalar1=st.ap[:, 2:3], scalar2=st.ap[:, 7:8],
                         op0=ALU.subtract, op1=ALU.mult)
                ob_ = ob[i % 2]
                self.vop("dve", "tensor_tensor", ob_, [y_, sg], out=ob_.ap, in0=y_.ap, in1=sg.ap[:, i, :], op=ALU.mult)
                pt = self.next_ptr()
                self.tr(pt, pt.ap[:, 0:128], ob_, ob_.ap[:, 0:128])
                self.tr(pt, pt.ap[:, 128:256], ob_, ob_.ap[:, 128:256])
                self.evac(oT, oT.ap[:, 6 + 2 * h:8 + 2 * h, ts], pt, pt.ap[:, 0:256].rearrange("p (a b) -> p a b", b=128))

    def emit_attn(self, w_in, c_q, kT_d, v_d, bias_d, oT, o_chunk0, nkt_halo, nr, name, lm_d=None, vcol_d=None, halo_t=()):
        xnT = self.xnT
        QT = self.ar(name + "QT", [128, T], BF16)
        KT = [self.ar(name + "KT%d" % i, [128, nkt_halo * 128], BF16, dma=True) for i in range(2)]
        V = [self.ar(name + "V%d" % i, [128, nkt_halo, 128], BF16, dma=True) for i in range(2)]
        strip = lm_d is not None
        if strip:
            bias2 = [self.ar(name + "bias%d" % i, [128, DIL_STRIP], BF16, dma=True) for i in range(2)]
        else:
            nb = bias_d.shape[1]
            bias2 = [self.ar(name + "bias%d" % i, [128, nb, 512], BF16, dma=True) for i in range(2)]
        PT = [self.ar(name + "PT%d" % i, [128, 512], BF16) for i in range(3)]
        rc = self.ar(name + "rc", [128, 512], F32)
        lm = vcol = None
        if strip:
            lm = self.ar(name + "lm", [128, DIL_STRIP], BF16, dma=True)
            self.dma("pool", lm, lm.ap, lm_d)
            vcol = self.ar(name + "vcol", [128, 24], F32, dma=True)
            self.dma("sp", vcol, vcol.ap, vcol_d)
        for h in range(6):
            slot = self.next_slot()
            wv = self.load_w(slot, w_in, 0, D, c_q + 128 * h, 128)
            kt_, v_ = KT[h % 2], V[h % 2]
            self.dma("sp", kt_, kt_.ap, kT_d[h], reads=halo_t)
            self.dma("sp", v_, v_.ap, v_d[:, h * 128:(h + 1) * 128].rearrange("(t p) d -> p t d", p=128), reads=halo_t)
            bias = bias2[h % 2]
            if strip:
                self.dma("pool", bias, bias.ap, bias_d[h])
                self.vop("dve", "tensor_tensor", bias, [bias, lm], out=bias.ap, in0=bias.ap, in1=lm.ap, op=ALU.add)
            else:
                self.dma("pool", bias, bias.ap, bias_d[h].rearrange("r k n -> k r n"))
            for half in range(2):
                ps = self.next_acc()
                for kc in range(KC):
                    self.mm(ps, ps.ap, slot, wv[:, kc, :], xnT, xnT.ap[:, kc, half * 512:(half + 1) * 512], kc == 0, kc == KC - 1)
                self.evac(QT, QT.ap[:, half * 512:(half + 1) * 512], ps, ps.ap, scale=INV_SQRT_HD)
            for qg in range(2):
                qs = slice(qg * 512, (qg + 1) * 512)
                num, den = self.patt
                for r in range(nr):
                    kk = 4 * qg + r
                    sc = self.next_acc()
                    self.mm(sc, sc.ap, kt_, kt_.ap[:, kk * 128:(kk + 1) * 128], QT, QT.ap[:, qs], True, False)
                    if strip:
                        b_ap = bias.ap[:, 128 * (19 - r):128 * (19 - r) + 512]
                    else:
                        b_ap = bias.ap[:, qg * nr + r, :]
                    self.mm(sc, sc.ap, self.ident, self.ident.ap, bias, b_ap, False, True)
                    p_ = PT[r % 3]
                    if vcol is None:
                        self.act(p_, p_.ap, sc, sc.ap, AF.Exp)
                    else:
                        self.act(p_, p_.ap, sc, sc.ap, AF.Exp, bias=vcol.ap[:, kk:kk + 1], reads=[vcol])
                    self.mm(num, num.ap, v_, v_.ap[:, kk, :], p_, p_.ap, r == 0, r == nr - 1)
                    self.mm(den, den.ap, self.ones, self.ones.ap, p_, p_.ap, r == 0, r == nr - 1)
                self.vop("dve", "reciprocal", rc, [den], out=rc.ap, in_=den.ap)
                self.vop("dve", "tensor_tensor", oT, [num, rc], out=oT.ap[:, o_chunk0 + h, qs], in0=num.ap, in1=rc.ap, op=ALU.mult)

    def build_F(self):
        self.setup_common()
        x_d = self.dram_in("x", [T, D])
        g_d = self.dram_in("g_final", [D])
        y = self.dram_out("y", [T, D])
        self.aoff = 0
        g_bc = self.ar("g_bc", [128, D], F32, dma=True)
        self.dma("sp", g_bc, g_bc.ap, g_d.partition_broadcast(128))
        junk = self.ar("junk", [128, D], BF16)
        st = self.ar("st", [128, 4 * NT], F32)
        xs = [self.ar("xs%d" % i, [128, D], F32, dma=True) for i in range(2)]
        ys = [self.ar("ys%d" % i, [128, D], F32) for i in range(2)]
        for tt in range(NT):
            xt, yt = xs[tt % 2], ys[tt % 2]
            c = 4 * tt
            self.dma("sp", xt, xt.ap, x_d[tt * 128:(tt + 1) * 128, :])
            self.act(junk, junk.ap, xt, xt.ap, AF.Square, accum_out=st.ap[:, c:c + 1], writes=[st])
            self.vop("dve", "tensor_scalar", st, [st], out=st.ap[:, c + 1:c + 2], in0=st.ap[:, c:c + 1], scalar1=1.0 / D, scalar2=EPS, op0=ALU.mult, op1=ALU.add)
            self.act(st, st.ap[:, c + 2:c + 3], st, st.ap[:, c + 1:c + 2], AF.Sqrt)
            self.vop("dve", "reciprocal", st, [st], out=st.ap[:, c + 3:c + 4], in_=st.ap[:, c + 2:c + 3])
            self.vop("dve", "scalar_tensor_tensor", yt, [xt, st, g_bc], out=yt.ap, in0=xt.ap, scalar=st.ap[:, c + 3:c + 4], in1=g_bc.ap, op0=ALU.mult, op1=ALU.mult)
            self.dma("sp", y, y.ap[tt * 128:(tt + 1) * 128, :], yt.ap, reads=[yt])


    def internal(self, name, shape, dtype):
        ap = self.nc.dram_tensor(name, list(shape), dtype).ap()
        return Tn(ap, Buf(name, self.new_dsem(name)))

    def collective(self, src_ts, send_ap, recv_ap):
        self.cc_n += 1
        n = self.cc_n

        def fn(g):
            g.collective_compute("AllGather", ALU.bypass, replica_groups=[[0, 1, 2, 3], [4, 5, 6, 7]],
                                 ins=[send_ap.opt()], outs=[recv_ap.opt()]).then_inc(self.cc_sem, 1)
            return g.wait_ge(self.cc_sem, n)
        o = self.P.op("pool", fn, reads=[t.b for t in src_ts], writes=[])
        o.noevent = True

    def store_kT(self, o, kT):
        if isinstance(o, Tn):
            self.dma("sp", o, o.ap.rearrange("h p t -> p h t"), kT.ap, reads=[kT])
        else:
            self.dma("sp", o[0], o[0].ap.rearrange("(h p) t -> p h t", p=128), kT.ap[:, 0:4, :], reads=[kT])
            self.dma("sp", o[1], o[1].ap.rearrange("(h p) t -> p h t", p=128), kT.ap[:, 4:6, :], reads=[kT])

    def store_v(self, o, vtm):
        if isinstance(o, Tn):
            self.dma("sp", o, o.ap.rearrange("(t p) c -> p t c", p=128), vtm.ap, reads=[vtm])
        else:
            self.dma("sp", o[0], o[0].ap.rearrange("(t p) c -> p t c", p=128), vtm.ap[:, :, 0:384], reads=[vtm])
            self.dma("sp", o[1], o[1].ap.rearrange("(t p) c -> p t c", p=128), vtm.ap[:, :, 384:768], reads=[vtm])

    def assemble_halos(self, l, pieces):
        kaT_h = self.internal("kaT_h%d" % l, [6, 128, 1536], BF16)
        va_h = self.internal("va_h%d" % l, [1536, 768], BF16)
        kcT_h = self.internal("kcT_h%d" % l, [6, 128, 3072], BF16)
        vc_h = self.internal("vc_h%d" % l, [3072, 768], BF16)
        NE = 4096
        Rs = [[self.ar("hR%d_%d" % (st, i), [128, NE], BF16, dma=True) for i in range(4)] for st in range(2)]
        accs = [self.ar("hacc%d" % st, [128, NE], BF16) for st in range(2)]
        wsel = self.wsel
        specs = (("ka", "k", 256, kaT_h), ("kc", "k", 1024, kcT_h), ("va", "v", 256, va_h), ("vc", "v", 1024, vc_h))
        step = 0
        alias = []
        for (nm, kind, hw, out) in specs:
            (s0, r0), (s1, r1) = pieces[nm]
            outs2 = [Tn(out.ap, Buf(nm + "_hs%d" % st, self.new_dsem(nm + "_hs%d" % st))) for st in range(2)]
            alias.extend(outs2)
            if kind == "k":
                self.dma("sp", out, out.ap[0:4, :, hw:hw + T], s0.ap.rearrange("(h p) t -> h p t", p=128), reads=[s0])
                self.dma("sp", out, out.ap[4:6, :, hw:hw + T], s1.ap.rearrange("(h p) t -> h p t", p=128), reads=[s1])
            else:
                self.dma("sp", out, out.ap[hw:hw + T, 0:384], s0.ap, reads=[s0])
                self.dma("sp", out, out.ap[hw:hw + T, 384:768], s1.ap, reads=[s1])
            for side in range(2):
                ranks = (0, 1, 2) if side == 0 else (1, 2, 3)
                for pi, rp in enumerate((r0, r1)):
                    R, acc = Rs[step % 2], accs[step % 2]
                    out_s = outs2[step % 2]
                    step += 1
                    if kind == "k":
                        nh, h0 = (4, 0) if pi == 0 else (2, 4)
                        n = nh * hw
                    else:
                        c0 = 384 * pi
                        n = (hw // 128) * 384
                    for r in ranks:
                        if kind == "k":
                            full = rp[r * nh * 128:(r + 1) * nh * 128, :].rearrange("(h p) t -> p h t", p=128)
                            src = full[:, :, T - hw:T] if side == 0 else full[:, :, 0:hw]
                            dst = R[r].ap[:, 0:n].rearrange("p (h t) -> p h t", t=hw)
                        else:
                            full = rp[r * T:(r + 1) * T, :]
                            src = (full[T - hw:T, :] if side == 0 else full[0:hw, :]).rearrange("(t p) c -> p t c", p=128)
                            dst = R[r].ap[:, 0:n].rearrange("p (t c) -> p t c", c=384)
                        self.dma("pool", R[r], dst, src)
                    c = 4 * side
                    ra = ranks[0]
                    self.vop("dve", "tensor_scalar", acc, [R[ra], wsel], out=acc.ap[:, 0:n], in0=R[ra].ap[:, 0:n], scalar1=wsel.ap[:, c + ra:c + ra + 1],
                             scalar2=None, op0=ALU.mult)
                    for r in ranks[1:]:
                        self.vop("dve", "scalar_tensor_tensor", acc, [R[r], wsel, acc], out=acc.ap[:, 0:n], in0=R[r].ap[:, 0:n],
                                 scalar=wsel.ap[:, c + r:c + r + 1], in1=acc.ap[:, 0:n], op0=ALU.mult, op1=ALU.add)
                    if kind == "k":
                        reg = out.ap[h0:h0 + nh, :, 0:hw] if side == 0 else out.ap[h0:h0 + nh, :, hw + T:hw + T + hw]
                        self.dma("sp", out_s, reg.rearrange("h p t -> p h t"), acc.ap[:, 0:n].rearrange("p (h t) -> p h t", t=hw), reads=[acc])
                    else:
                        reg = out.ap[0:hw, c0:c0 + 384] if side == 0 else out.ap[hw + T:hw + T + hw, c0:c0 + 384]
                        self.dma("sp", out_s, reg.rearrange("(t p) c -> p t c", p=128), acc.ap[:, 0:n].rearrange("p (t c) -> p t c", c=384), reads=[acc])
        self.halo_alias = tuple(alias)
        return kaT_h, va_h, kcT_h, vc_h

    def build_fused(self, nlayers=DEPTH):
        nc = self.nc
        self.setup_common()
        x_in = self.dram_in("x", [T, D])
        mem_d = self.dram_in("mem", [256, D])
        g_mix = self.dram_in("norm_mix_g", [DEPTH, D])
        g_cross = self.dram_in("norm_cross_g", [DEPTH, D])
        g_mem = self.dram_in("norm_mem_g", [DEPTH, D])
        g_mlp = self.dram_in("norm_mlp_g", [DEPTH, D])
        g_final = self.dram_in("final_norm_g", [D])
        w_in = [self.dram_in("w_in_%d" % l, [D, 13824]) for l in range(nlayers)]
        w_branch = [self.dram_in("w_branch_%d" % l, [2560, D]) for l in range(nlayers)]
        w_out = [self.dram_in("w_out_%d" % l, [D, D]) for l in range(nlayers)]
        w_cq = [self.dram_in("w_cq_%d" % l, [D, 512]) for l in range(nlayers)]
        w_ckv = [self.dram_in("w_ckv_%d" % l, [D, 1024]) for l in range(nlayers)]
        w_co = [self.dram_in("w_co_%d" % l, [512, D]) for l in range(nlayers)]
        w_mlp1 = [self.dram_in("w_mlp1_%d" % l, [D, 8192]) for l in range(nlayers)]
        w_mlp2 = [self.dram_in("w_mlp2_%d" % l, [8192, D]) for l in range(nlayers)]
        biasA_d = [self.dram_in("biasA_%d" % l, [6, 16, 128, 512]) for l in range(nlayers)]
        dec = self.dram_in("ret_decay", [DEPTH, 8])
        cs_d = self.dram_in("rot_cs", [T, 128])
        nsc_d = self.dram_in("rot_nsc", [T, 128])
        expo_d = self.dram_in("expo", [128, 16])
        wsel_d = self.dram_in("wsel", [128, 8])
        biasC_d = self.dram_in("biasC", [6, 128, DIL_STRIP])
        lmC_d = self.dram_in("lmC", [128, DIL_STRIP])
        vcol_d = self.dram_in("vcolC", [128, 24])
        y = self.dram_out("y", [T, D])
        KmT = self.sb("KmT", [128, 4, 256], BF16)
        Vm = self.sb("Vm", [128, 2, 512], BF16)
        self.wsel = self.sb("wsel", [128, 8], F32, dma=True)
        self.dma("sp", self.wsel, self.wsel.ap, wsel_d)
        xscr = self.internal("xscr", [T, D], F32)
        x_d, x_t = x_in, None
        for l in range(nlayers):
            pieces = {}
            for nm, shapes in (("ka", ((512, T), (256, T))), ("kc", ((512, T), (256, T))), ("va", ((T, 384), (T, 384))), ("vc", ((T, 384), (T, 384)))):
                pp = []
                for i, (rws, cls) in enumerate(shapes):
                    sname = "s_%s%d" % (nm, i)
                    sap = nc.dram_tensor("%s_%d" % (sname, l), [rws, cls], BF16).ap()
                    rap = nc.dram_tensor("r_%s%d_%d" % (nm, i, l), [4 * rws, cls], BF16).ap()
                    pp.append((Tn(sap, Buf(sname, self.new_dsem(sname))), rap))
                pieces[nm] = tuple(pp)
            send_L = nc.dram_tensor("send_L%d" % l, [1024, 256], F32).ap()
            recv_L = nc.dram_tensor("recv_L%d" % l, [4096, 256], F32).ap()
            o_L = Tn(send_L.rearrange("(d h p) v -> d h p v", d=2, h=4), Buf("s_L", self.new_dsem("s_L")))
            self.arena_reset()
            outs = ((pieces["ka"][0][0], pieces["ka"][1][0]), (pieces["kc"][0][0], pieces["kc"][1][0]),
                    (pieces["va"][0][0], pieces["va"][1][0]), (pieces["vc"][0][0], pieces["vc"][1][0]), o_L)
            def mid_cb(stage, pieces=pieces):
                for nm in (("ka", "va") if stage == 0 else ("kc", "vc")):
                    for (st_, rap) in pieces[nm]:
                        self.collective([st_], st_.ap, rap)
            if OVERLAP_CC:
                self.part_A(x_d, x_t, g_mix[l], w_in[l], dec[l], cs_d, nsc_d, outs, mid_cb=mid_cb)
            else:
                self.part_A(x_d, x_t, g_mix[l], w_in[l], dec[l], cs_d, nsc_d, outs)
                mid_cb(0)
                mid_cb(1)
            self.collective([o_L], send_L, recv_L)
            self.arena_reset()
            halos = self.assemble_halos(l, pieces)
            last = (l == nlayers - 1)
            a = dict(x_d=x_d, x_t=x_t, g_mix=g_mix[l], g_cross=g_cross[l], g_mem=g_mem[l], g_mlp=g_mlp[l], w_in=w_in[l], w_branch=w_branch[l],
                     w_out=w_out[l], w_cq=w_cq[l], w_ckv=w_ckv[l], w_co=w_co[l], w_mlp1=w_mlp1[l], w_mlp2=w_mlp2[l], mem_d=mem_d, dec_d=dec[l],
                     cs_d=cs_d, nsc_d=nsc_d, expo_d=expo_d, biasA_d=biasA_d[l], biasC_d=biasC_d, lmC_d=lmC_d, vcol_d=vcol_d,
                     kaT_d=halos[0].ap, va_d=halos[1].ap, kcT_d=halos[2].ap, vc_d=halos[3].ap, halo_t=tuple(halos) + self.halo_alias,
                     Lall_d=recv_L.rearrange("(j d h p) v -> d j h p v", j=4, d=2, h=4), L_queue="pool",
                     x_out=(y if last else xscr), KmT=KmT, Vm=Vm)
            self.arena_reset()
            self.part_B(a, norm1=False, final_g=(g_final if last else None))
            x_d, x_t = xscr.ap, xscr

    def finish(self):
        nc = self.nc
        P = self.P
        P.finalize()
        esems = {en: self.es.enter_context(nc.semaphore("s_" + en)) for en in ENGS}
        self.cc_sem = self.es.enter_context(nc.semaphore("s_cc"))
        for d in P.dsems:
            if d.count > 0:
                d.handle = self.es.enter_context(nc.semaphore(d.name))
        with nc.Block() as block:
            @block.tensor
            def _(t):
                P.emit_engine("pe", t, esems)

            @block.scalar
            def _(a):
                P.emit_engine("act", a, esems)

            @block.vector
            def _(v):
                P.emit_engine("dve", v, esems)

            @block.gpsimd
            def _(g):
                P.emit_engine("pool", g, esems)

            @block.sync
            def _(s):
                P.emit_engine("sp", s, esems)
        self.es.close()
        return nc


CST_PF, CST_PB, CST_UF, CST_UB, CST_N1, CST_N2 = 0, 128, 256, 384, 512, 640
CST_M1, CST_M2, CST_ZE, CST_ZB = 768, 769, 770, 778
CST_N = 786


def make_cst():
    c = np.zeros((128, CST_N), np.float32)
    m = np.arange(128)[:, None].astype(np.float32)
    n = np.arange(128)[None, :].astype(np.float32)
    c[:, CST_PF:CST_PF + 128] = np.maximum(n - m, 0)
    c[:, CST_PB:CST_PB + 128] = np.maximum(m - n, 0)
    c[:, CST_UF:CST_UF + 128] = (n >= m)
    c[:, CST_UB:CST_UB + 128] = (m > n)
    c[:, CST_N1:CST_N1 + 128] = n + 1 + 0 * m
    c[:, CST_N2:CST_N2 + 128] = 127 - n + 0 * m
    c[:, CST_M1] = m[:, 0] + 1
    c[:, CST_M2] = 127 - m[:, 0]
    tt = np.arange(8)[None, :].astype(np.float32)
    c[:, CST_ZE:CST_ZE + 8] = 1023 - (tt * 128 + m)
    c[:, CST_ZB:CST_ZB + 8] = tt * 128 + m + 1
    return c


def rot_tables(j):
    d = 128
    inv_freq = (10000.0 ** (-np.arange(0, d, 2, dtype=np.float32) / d)).astype(np.float32)
    pos = (np.arange(T, dtype=np.float32) + np.float32(j * T))
    ang = pos[:, None] * inv_freq[None, :]
    cos, sin = np.cos(ang).astype(np.float32), np.sin(ang).astype(np.float32)
    cs = np.concatenate([cos, sin], axis=1)
    nsc = np.concatenate([-sin, cos], axis=1)
    return np.ascontiguousarray(cs), np.ascontiguousarray(nsc)


_PROG_CACHE = {}


def get_prog(mode, stop=None):
    key = (mode, stop)
    if key not in _PROG_CACHE:
        b = Builder(mode)
        if mode == "FUSED":
            b.build_fused(nlayers=(stop if stop is not None else DEPTH))
        elif mode == "A":
            b.build_A()
        elif mode == "B":
            b.build_B(stop=stop)
        else:
            b.build_F()
        _PROG_CACHE[key] = b.finish()
        _PROG_INPUTS[id(_PROG_CACHE[key])] = list(b.din.keys())
    return _PROG_CACHE[key]


def t5_bucket(rel):
    nb = 16
    ret = (rel > 0).astype(np.int32) * nb
    n = np.abs(rel)
    max_exact = nb // 2
    large = max_exact + (np.log(np.maximum(n, 1) / max_exact) / np.log(1024 / max_exact) * (nb - max_exact)).astype(np.int32)
    large = np.minimum(large, nb - 1)
    return (ret + np.where(n < max_exact, n, large)).astype(np.int32)


_IDX_CACHE = {}


def na_index(j):
    if ("na", j) not in _IDX_CACHE:
        qg = np.arange(2)[:, None, None, None]
        r = np.arange(8)[None, :, None, None]
        k = np.arange(128)[None, None, :, None]
        q = np.arange(512)[None, None, None, :]
        tk = 1024 * j - 256 + 128 * (4 * qg + r) + k
        tq = 1024 * j + 512 * qg + q + 0 * k
        inseq = (tk >= 0) & (tk < SEQ)
        rk, ck = tk // 64, tk % 64
        rq, cq = tq // 64, tq % 64
        r0 = np.clip(rq - 4, 0, 56)
        c0 = np.clip(cq - 8, 0, 48)
        valid = inseq & (rk >= r0) & (rk < r0 + 8) & (ck >= c0) & (ck < c0 + 16)
        ri = np.clip(rk - rq + 7, 0, 14)
        ci = np.clip(ck - cq + 15, 0, 30)
        _IDX_CACHE[("na", j)] = (ri.reshape(16, 128, 512), ci.reshape(16, 128, 512), valid.reshape(16, 128, 512))
    return _IDX_CACHE[("na", j)]


def dil_index():
    if "dil" not in _IDX_CACHE:
        k = np.arange(128)[:, None]
        c = np.arange(DIL_STRIP)[None, :]
        off = 1408 + k - c
        a = np.abs(off)
        mult = (a <= 64).astype(np.int32) + ((a <= 256) & (off % 4 == 0)) + ((a <= 1024) & (off % 16 == 0))
        bucket = t5_bucket(off)
        lm = np.where(mult > 0, np.log(np.maximum(mult, 1)), 0.0).astype(np.float32)
        _IDX_CACHE["dil"] = (bucket, mult > 0, np.ascontiguousarray(lm))
    return _IDX_CACHE["dil"]


def dil_bias(t5):
    bucket, dvalid, lmC = dil_index()
    biasC = np.where(dvalid[None], np.transpose(t5[bucket], (2, 0, 1)), np.float32(NEG)).astype(np.float32)
    return np.ascontiguousarray(biasC), lmC


def halo(arr, axis, start, length):
    n = arr.shape[axis]
    lo, hi = max(start, 0), min(start + length, n)
    shp = list(arr.shape)
    shp[axis] = length
    out = np.zeros(shp, arr.dtype)
    sl_o = [slice(None)] * arr.ndim
    sl_i = [slice(None)] * arr.ndim
    sl_o[axis] = slice(lo - start, hi - start)
    sl_i[axis] = slice(lo, hi)
    out[tuple(sl_o)] = arr[tuple(sl_i)]
    return out


def run_B(x_chunks, resA, inputs, l, stop=None):
    nc = get_prog("B", stop)
    cst = make_cst()
    ident = np.eye(128, dtype=np.float32)
    biasC, lmC = dil_bias(inputs["t5_bias"])
    W = {k: np.ascontiguousarray(inputs[k][l]) for k in ("w_in", "w_branch", "w_out", "w_cq", "w_ckv", "w_co", "w_mlp1", "w_mlp2",
                                                          "norm_mix_g", "norm_cross_g", "norm_mem_g", "norm_mlp_g")}
    rpb = inputs["na_rpb"][l]
    maps = []
    for c in range(NCORES):
        b, j = c // 4, c % 4
        grp = [resA[b * 4 + jj] for jj in range(4)]
        kaT = np.concatenate([g["kaT"] for g in grp], axis=2)
        kcT = np.concatenate([g["kcT"] for g in grp], axis=2)
        va = np.concatenate([g["va"] for g in grp], axis=0)
        vc = np.concatenate([g["vc"] for g in grp], axis=0)
        Lall = np.ascontiguousarray(np.stack([g["L"] for g in grp], axis=1))
        ri, ci, valid = na_index(j)
        biasA = np.where(valid[None], rpb[:, ri, ci], np.float32(NEG)).astype(np.float32)
        cs, nsc = rot_tables(j)
        expo = np.zeros((128, 16), np.float32)
        for jj in range(4):
            if jj < j:
                expo[:, jj] = 1024.0 * (j - 1 - jj)
                expo[:, 4 + jj] = 1.0
            if jj > j:
                expo[:, 8 + jj] = 1024.0 * (jj - j - 1)
                expo[:, 12 + jj] = 1.0
        vcol = np.zeros((128, 24), np.float32)
        tok = 1024 * j - 1024 + 128 * np.arange(24)[None, :] + np.arange(128)[:, None]
        vcol[(tok < 0) | (tok >= SEQ)] = NEG
        maps.append({
            "x": x_chunks[c], "g_mix": W["norm_mix_g"], "g_cross": W["norm_cross_g"], "g_mem": W["norm_mem_g"], "g_mlp": W["norm_mlp_g"],
            "w_in": W["w_in"], "w_branch": W["w_branch"], "w_out": W["w_out"], "w_cq": W["w_cq"], "w_ckv": W["w_ckv"], "w_co": W["w_co"],
            "w_mlp1": W["w_mlp1"], "w_mlp2": W["w_mlp2"], "mem": np.ascontiguousarray(inputs["mem"][b]),
            "ret_decay": np.ascontiguousarray(inputs["ret_decay"][l].reshape(8)), "rot_cs": cs, "rot_nsc": nsc, "expo": expo,
            "biasA": biasA, "biasC": biasC, "lmC": lmC, "vcolC": vcol,
            "kaT_h": halo(kaT, 2, 1024 * j - 256, 1536), "va_h": halo(va, 0, 1024 * j - 256, 1536),
            "kcT_h": halo(kcT, 2, 1024 * j - 1024, 3072), "vc_h": halo(vc, 0, 1024 * j - 1024, 3072),
            "Lall": Lall, "ident": ident, "cst": cst,
        })
    needed = set(nc_input_names(nc))
    maps = [{k: v for k, v in m.items() if k in needed} for m in maps]
    res = run_bass_kernel_spmd(nc, maps, core_ids=list(range(NCORES)))
    return res.results


def nc_input_names(nc):
    return _PROG_INPUTS[id(nc)]


_PROG_INPUTS = {}


def run_A(x_chunks, inputs, l):
    nc = get_prog("A")
    cst = make_cst()
    ident = np.eye(128, dtype=np.float32)
    maps = []
    for c in range(NCORES):
        cs, nsc = rot_tables(c % 4)
        maps.append({"x": x_chunks[c], "g_mix": np.ascontiguousarray(inputs["norm_mix_g"][l]),
                     "w_in": np.ascontiguousarray(inputs["w_in"][l]),
                     "ret_decay": np.ascontiguousarray(inputs["ret_decay"][l].reshape(8)),
                     "rot_cs": cs, "rot_nsc": nsc, "ident": ident, "cst": cst})
    res = run_bass_kernel_spmd(nc, maps, core_ids=list(range(NCORES)))
    return res.results


def run_F(x_chunks, inputs):
    nc = get_prog("F")
    cst = make_cst()
    ident = np.eye(128, dtype=np.float32)
    g = np.ascontiguousarray(inputs["final_norm_g"])
    maps = [{"x": x_chunks[c], "g_final": g, "ident": ident, "cst": cst} for c in range(NCORES)]
    res = run_bass_kernel_spmd(nc, maps, core_ids=list(range(NCORES)))
    return res.results


def fused_maps(inputs, nlayers=DEPTH):
    cst = make_cst()
    ident = np.eye(128, dtype=np.float32)
    biasC, lmC = dil_bias(inputs["t5_bias"])
    shared = {k: np.ascontiguousarray(inputs[k]) for k in ("norm_mix_g", "norm_cross_g", "norm_mem_g", "norm_mlp_g", "final_norm_g")}
    for k in ("w_in", "w_branch", "w_out", "w_cq", "w_ckv", "w_co", "w_mlp1", "w_mlp2"):
        for l in range(nlayers):
            shared["%s_%d" % (k, l)] = np.ascontiguousarray(inputs[k][l])
    shared["ret_decay"] = np.ascontiguousarray(inputs["ret_decay"].reshape(DEPTH, 8))
    rpb = inputs["na_rpb"]
    biasA_j = []
    for j in range(4):
        ri, ci, valid = na_index(j)
        biasA_j.append(np.where(valid[None, None], rpb[:, :, ri, ci], np.float32(NEG)).astype(np.float32))
    x = inputs["x"]
    maps = []
    for c in range(NCORES):
        b, j = c // 4, c % 4
        cs, nsc = rot_tables(j)
        expo = np.zeros((128, 16), np.float32)
        wsel = np.zeros((128, 8), np.float32)
        for jj in range(4):
            if jj < j:
                expo[:, jj] = 1024.0 * (j - 1 - jj)
                expo[:, 4 + jj] = 1.0
            if jj > j:
                expo[:, 8 + jj] = 1024.0 * (jj - j - 1)
                expo[:, 12 + jj] = 1.0
        if j > 0:
            wsel[:, j - 1] = 1.0
        if j < 3:
            wsel[:, 4 + j + 1] = 1.0
        vcol = np.zeros((128, 24), np.float32)
        tok = 1024 * j - 1024 + 128 * np.arange(24)[None, :] + np.arange(128)[:, None]
        vcol[(tok < 0) | (tok >= SEQ)] = NEG
        m = dict(shared)
        m.update({"x": np.ascontiguousarray(x[b, j * T:(j + 1) * T]), "mem": np.ascontiguousarray(inputs["mem"][b]),
                  "rot_cs": cs, "rot_nsc": nsc, "expo": expo, "wsel": wsel, "biasC": biasC, "lmC": lmC,
                  "vcolC": vcol, "ident": ident, "cst": cst})
        for l in range(nlayers):
            m["biasA_%d" % l] = np.ascontiguousarray(biasA_j[j][l])
        maps.append(m)
    return maps


def kernel_fused(inputs, nlayers=None):
    nc = get_prog("FUSED", nlayers)
    maps = fused_maps(inputs, nlayers if nlayers is not None else DEPTH)
    res = run_bass_kernel_spmd(nc, maps, core_ids=list(range(NCORES)))
    out = np.zeros((2, SEQ, D), np.float32)
    for c in range(NCORES):
        out[c // 4, (c % 4) * T:(c % 4 + 1) * T] = np.asarray(res.results[c]["y"])
    return out


def kernel(**inputs):
    inputs = {k: np.asarray(v) for k, v in inputs.items()}
    return kernel_fused(inputs)


def kernel_unfused(**inputs):
    inputs = {k: np.asarray(v) for k, v in inputs.items()}
    x = inputs["x"].astype(np.float32, copy=False)
    xch = [np.ascontiguousarray(x[c // 4, (c % 4) * T:(c % 4 + 1) * T]) for c in range(NCORES)]
    for l in range(DEPTH):
        resA = run_A(xch, inputs, l)
        resA = [{k: np.asarray(v) for k, v in r.items()} for r in resA]
        resB = run_B(xch, resA, inputs, l)
        xch = [np.ascontiguousarray(np.asarray(r["x_out"])) for r in resB]
    resF = run_F(xch, inputs)
    out = np.zeros((2, SEQ, D), np.float32)
    for c in range(NCORES):
        out[c // 4, (c % 4) * T:(c % 4 + 1) * T] = np.asarray(resF[c]["y"])
    return out
```

```python
import numpy as np
import ml_dtypes
from contextlib import ExitStack
import concourse.bass as bass
import concourse.mybir as mybir
from concourse.bass_utils import run_bass_kernel_spmd

F32 = mybir.dt.float32
BF16 = mybir.dt.bfloat16
AF = mybir.ActivationFunctionType
ALU = mybir.AluOpType

NCORES = 8
D = 2048
SEQ = 4096
T = 1024
NT = 8
KC = 16
DEPTH = 4
EPS = 1e-6
NEG = -30000.0
HD = 128
INV_SQRT_HD = HD ** -0.5
C_QA, C_KA, C_VA = 0, 768, 1536
C_QB, C_KB, C_VB, C_GR = 2304, 2816, 3328, 4352
C_QC, C_KC, C_VC = 5376, 6144, 6912
C_SA, C_SB, C_SC = 7680, 9728, 11776
SLOT_ELEMS = 8192
DIL_STRIP = 2944
OVERLAP_CC = True


class Buf:
    __slots__ = ("name", "lastw", "readers", "dsem", "wdma", "persistent")

    def __init__(self, name, dsem=None, persistent=True):
        self.name = name
        self.lastw = []
        self.readers = {}
        self.dsem = dsem
        self.wdma = False
        self.persistent = persistent


class DmaSem:
    def __init__(self, name, step=16):
        self.name = name
        self.count = 0
        self.handle = None
        self.step = step


class Op:
    __slots__ = ("eng", "fn", "waits", "inc", "val", "dsem", "noevent")

    def __init__(self, eng, fn):
        self.eng = eng
        self.fn = fn
        self.waits = []
        self.inc = False
        self.val = None
        self.dsem = None
        self.noevent = False


ENGS = ("pe", "act", "dve", "pool", "sp")


class Prog:
    def __init__(self):
        self.ops = {e: [] for e in ENGS}
        self.dsems = []
        self.bar_events = []
        self.bar_passed = {e: True for e in ENGS}

    def dsem(self, name, step=16):
        d = DmaSem(name, step)
        self.dsems.append(d)
        return d

    def barrier(self):
        ev = []
        for e in ("pe", "act", "dve", "pool"):
            for o in reversed(self.ops[e]):
                if o.dsem is None and not o.noevent:
                    ev.append(o)
                    break
        for d in self.dsems:
            if d.count > 0:
                ev.append((d, d.count))
        self.bar_events = ev
        self.bar_passed = {e: False for e in ENGS}

    def op(self, eng, fn, reads=(), writes=(), dma=False):
        o = Op(eng, fn)
        raw, other = [], []
        if not self.bar_passed[eng] and any(not b.persistent for b in list(reads) + list(writes)):
            self.bar_passed[eng] = True
            for d in self.bar_events:
                if isinstance(d, Op) and d.eng == eng and not dma:
                    continue
                raw.append(d)
        for b in reads:
            raw.extend(b.lastw)
        for b in writes:
            if not (dma and b.wdma and not b.readers):
                other.extend(b.lastw)
            other.extend(b.readers.values())
        for d in raw:
            if isinstance(d, Op):
                if d.eng == eng and eng == "pe":
                    continue
                d.inc = True
            o.waits.append(d)
        for d in other:
            if isinstance(d, Op):
                if d.eng == eng and not dma:
                    continue
                d.inc = True
            o.waits.append(d)
        if dma:
            assert len(writes) == 1
            b = writes[0]
            ds = b.dsem
            assert ds is not None, b.name
            ds.count += ds.step
            o.dsem = ds
            ev = (ds, ds.count)
            if b.wdma and not b.readers:
                b.lastw = b.lastw + [ev]
            else:
                b.lastw = [ev]
            b.readers = {}
            b.wdma = True
            for r in reads:
                r.readers[ds.name] = ev
        else:
            for b in writes:
                b.lastw = [o]
                b.readers = {}
                b.wdma = False
            for r in reads:
                if r not in writes:
                    r.readers[eng] = o
        self.ops[eng].append(o)
        return o

    def finalize(self):
        for e in ENGS:
            c = 0
            for o in self.ops[e]:
                if o.inc and o.dsem is None:
                    c += 1
                    o.val = c

    def emit_engine(self, e, eng, esems):
        waited = {}
        for o in self.ops[e]:
            for d in o.waits:
                if isinstance(d, Op):
                    key, val, sem = d.eng, d.val, esems[d.eng]
                else:
                    key, val, sem = d[0].name, d[1], d[0].handle
                if waited.get(key, -1) >= val:
                    continue
                waited[key] = val
                eng.wait_ge(sem, val)
            ins = o.fn(eng)
            if o.dsem is not None:
                ins.then_inc(o.dsem.handle, o.dsem.step)
            elif o.inc:
                ins.then_inc(esems[e], 1)
        if e == "sp":
            for d in self.dsems:
                if d.count > 0 and waited.get(d.name, -1) < d.count:
                    eng.wait_ge(d.handle, d.count)


def _dt_size(dt):
    return 4 if dt == F32 else 2


class Tn:
    __slots__ = ("ap", "b")

    def __init__(self, ap, b):
        self.ap = ap
        self.b = b

    def __getitem__(self, k):
        return self.ap[k]


class Builder:
    def __init__(self, mode, final_norm=False, dbg=None):
        self.mode = mode
        self.final_norm = final_norm
        self.dbg = dbg
        self.nc = bass.Bass("TRN2", target_bir_lowering=False)
        self.P = Prog()
        self.es = ExitStack()
        self.din = {}
        self.dout = {}
        self.ndsem = 0
        self.dsem_by_name = {}
        self.cc_n = 0
        self.rr = 0
        self.evt = 0

    def dram_in(self, name, shape, dtype=F32):
        ap = self.nc.dram_tensor(name, list(shape), dtype, kind="ExternalInput").ap()
        self.din[name] = ap
        return ap

    def dram_out(self, name, shape, dtype=F32):
        ap = self.nc.dram_tensor(name, list(shape), dtype, kind="ExternalOutput").ap()
        t = Tn(ap, Buf(name, self.new_dsem(name)))
        self.dout[name] = t
        return t

    def new_dsem(self, name):
        if name not in self.dsem_by_name:
            self.ndsem += 1
            self.dsem_by_name[name] = self.P.dsem("d%d_%s" % (self.ndsem, name))
        return self.dsem_by_name[name]

    def sb(self, name, shape, dtype, dma=False, sem=None):
        h = self.es.enter_context(self.nc.sbuf_tensor("sb_" + name, list(shape), dtype))
        ds = sem if sem is not None else (self.new_dsem(name) if dma else None)
        return Tn(h[:] if len(shape) == 2 else h[tuple([slice(None)] * len(shape))], Buf(name, ds, True))

    def arena_reset(self, keep=0):
        self.aoff = keep
        self.P.barrier()

    def ar_at(self, name, shape, dtype, off, dma=False):
        save = self.aoff
        self.aoff = off
        t = self.ar(name, shape, dtype, dma=dma)
        self.aoff = save
        return t

    def ar(self, name, shape, dtype, dma=False, sem=None):
        n = int(np.prod(shape[1:])) * _dt_size(dtype)
        n = (n + 63) // 64 * 64
        assert self.aoff + n <= self.arena_bytes, (name, self.aoff, n)
        ap = self.arena[:, self.aoff // 4:(self.aoff + n) // 4]
        self.aoff += n
        if dtype != F32:
            ap = ap.bitcast(dtype)
        ne = int(np.prod(shape[1:]))
        ap = ap[:, 0:ne]
        if len(shape) == 3:
            ap = ap.rearrange("p (a b) -> p a b", b=shape[2])
        elif len(shape) == 4:
            ap = ap.rearrange("p (a b c) -> p a b c", b=shape[2], c=shape[3])
        ds = sem if sem is not None else (self.new_dsem(name) if dma else None)
        return Tn(ap, Buf(name, ds, False))

    def dma(self, q, out_t, out_ap, in_ap, reads=()):
        self.P.op(q, lambda e: e.dma_start(out=out_ap, in_=in_ap), reads=[r.b for r in reads], writes=[out_t.b], dma=True)

    def mm(self, ps, ps_ap, lhsT, lhsT_ap, rhs, rhs_ap, start, stop, extra_reads=()):
        rd = [lhsT.b, rhs.b] + [r.b for r in extra_reads]
        self.P.op("pe", lambda e: e.matmul(ps_ap, lhsT=lhsT_ap, rhs=rhs_ap, start=start, stop=stop), reads=rd, writes=[ps.b])

    def tr(self, ps, ps_ap, src, src_ap):
        self.P.op("pe", lambda e: e.transpose(out=ps_ap, in_=src_ap, identity=self.ident.ap), reads=[src.b, self.ident.b], writes=[ps.b])

    def act(self, out_t, out_ap, in_t, in_ap, func, reads=(), writes=(), **kw):
        self.P.op("act", lambda e: e.activation(out=out_ap, in_=in_ap, func=func, **kw), reads=[in_t.b] + [r.b for r in reads],
                  writes=[out_t.b] + [w.b for w in writes])

    def vop(self, eng, method, out_t, reads, **kw):
        self.P.op(eng, lambda e: getattr(e, method)(**kw), reads=[r.b for r in reads], writes=[out_t.b])

    def evac(self, out_t, out_ap, ps, ps_ap, scale=None):
        self.evt += 1
        if self.evt % 2 == 0:
            if scale is None:
                self.act(out_t, out_ap, ps, ps_ap, AF.Copy)
            else:
                self.act(out_t, out_ap, ps, ps_ap, AF.Copy, scale=scale)
        else:
            if scale is None:
                self.vop("dve", "tensor_copy", out_t, [ps], out=out_ap, in_=ps_ap)
            else:
                self.vop("dve", "tensor_scalar", out_t, [ps], out=out_ap, in0=ps_ap, scalar1=scale, scalar2=None, op0=ALU.mult)

    def next_acc(self):
        self.rr = (self.rr + 1) % 4
        return self.pacc[self.rr]

    def next_slot(self):
        self.slot_i = (self.slot_i + 1) % len(self.wslots)
        return self.wslots[self.slot_i]

    def load_w(self, slot, w_ap, r0, nrows, c0, ncols, col_off=0, total_cols=None, elem_off=0):
        kc = nrows // 128
        tc_ = total_cols if total_cols is not None else ncols
        assert elem_off + kc * tc_ <= SLOT_ELEMS
        view = slot.ap[:, elem_off:elem_off + kc * tc_].rearrange("p (k c) -> p k c", c=tc_)
        src = w_ap[r0:r0 + nrows, c0:c0 + ncols].rearrange("(k p) c -> p k c", p=128)
        self.dma("pool", slot, view[:, :, col_off:col_off + ncols], src)
        return view

    def setup_common(self):
        nc = self.nc
        self.arena_bytes = 100 * 1024
        self.arena = self.es.enter_context(nc.sbuf_tensor("arena", [128, self.arena_bytes // 4], F32))
        self.aoff = 0
        self.xnT = self.sb("xnT", [128, KC, T], BF16)
        self.wslots = [self.sb("wslot%d" % i, [128, SLOT_ELEMS], BF16, dma=True) for i in range(4)]
        self.slot_i = -1
        self.ident = self.sb("ident", [128, 128], BF16, dma=True)
        self.ones = self.sb("ones", [128, 128], BF16)
        self.cst = self.sb("cst", [128, CST_N], F32, dma=True)
        self.lg = self.sb("lg", [128, 8], F32, dma=True)
        self.small = self.sb("small", [128, 64], F32)
        banks = [self.es.enter_context(nc.psum_tensor("pb%d" % i, [128, 512], F32)) for i in range(8)]
        self.pacc = [Tn(banks[i][:], Buf("pacc%d" % i)) for i in range(4)]
        self.patt = [Tn(banks[4 + i][:], Buf("patt%d" % i)) for i in range(2)]
        self.ptr = [Tn(banks[6 + i][:].bitcast(BF16), Buf("ptr%d" % i)) for i in range(2)]
        self.ptr_i = 0
        d_ident = self.dram_in("ident", [128, 128])
        d_cst = self.dram_in("cst", [128, CST_N])
        self.dma("pool", self.ident, self.ident.ap, d_ident)
        self.dma("sp", self.cst, self.cst.ap, d_cst)
        self.vop("dve", "memset", self.ones, [], ap=self.ones.ap, constant=1.0)

    def next_ptr(self):
        self.ptr_i ^= 1
        return self.ptr[self.ptr_i]

    def emit_norm(self, gain_dram, src_dram=None, src_tiles=None, ntiles=NT, dstT=None, gname="g", src_t=None):
        dstT = dstT or self.xnT
        g_bc = self.ar(gname + "_bc", [128, D], F32, dma=True)
        self.dma("sp", g_bc, g_bc.ap, gain_dram.partition_broadcast(128))
        junk = self.ar(gname + "_junk", [128, D], BF16)
        xnb = [self.ar(gname + "_xnb%d" % i, [128, D], BF16) for i in range(2)]
        st = self.ar(gname + "_st", [128, 4 * ntiles], F32)
        xs = None
        if src_dram is not None:
            xs = [self.ar(gname + "_xs%d" % i, [128, D], F32, dma=True) for i in range(2)]
        for tt in range(ntiles):
            if src_dram is not None:
                xt = xs[tt % 2]
                self.dma("sp", xt, xt.ap, src_dram[tt * 128:(tt + 1) * 128, :], reads=[src_t] if src_t is not None else ())
                x_ap = xt.ap
            else:
                xt = src_tiles[tt]
                x_ap = xt.ap
            c = 4 * tt
            self.act(junk, junk.ap, xt, x_ap, AF.Square, accum_out=st.ap[:, c:c + 1], writes=[st])
            self.vop("dve", "tensor_scalar", st, [st, junk], out=st.ap[:, c + 1:c + 2], in0=st.ap[:, c:c + 1],
                     scalar1=1.0 / D, scalar2=EPS, op0=ALU.mult, op1=ALU.add)
            self.act(st, st.ap[:, c + 2:c + 3], st, st.ap[:, c + 1:c + 2], AF.Sqrt)
            self.vop("dve", "reciprocal", st, [st], out=st.ap[:, c + 3:c + 4], in_=st.ap[:, c + 2:c + 3])
            xb = xnb[tt % 2]
            self.vop("dve", "scalar_tensor_tensor", xb, [xt, st, g_bc], out=xb.ap, in0=x_ap, scalar=st.ap[:, c + 3:c + 4],
                     in1=g_bc.ap, op0=ALU.mult, op1=ALU.mult)
            for rnd in range(2):
                pt = self.next_ptr()
                for i in range(8):
                    kc = rnd * 8 + i
                    self.tr(pt, pt.ap[:, i * 128:(i + 1) * 128], xb, xb.ap[:, kc * 128:(kc + 1) * 128])
                self.evac(dstT, dstT.ap[:, rnd * 8:(rnd + 1) * 8, tt * 128:(tt + 1) * 128],
                          pt, pt.ap.rearrange("p (a b) -> p a b", b=128))

    def emit_decay(self, d_decay):
        self.dma("sp", self.lg, self.lg.ap, d_decay.partition_broadcast(128))
        self.act(self.lg, self.lg.ap, self.lg, self.lg.ap, AF.Exp)
        self.vop("dve", "tensor_scalar", self.lg, [self.lg], out=self.lg.ap, in0=self.lg.ap, scalar1=-1.0, scalar2=None, op0=ALU.mult)

    def exp_scaled(self, out_t, out_ap, in_t, in_ap, col, mul=None):
        self.act(out_t, out_ap, in_t, in_ap, AF.Exp, scale=self.lg.ap[:, col:col + 1], reads=[self.lg])
        if mul is not None:
            self.vop("dve", "tensor_scalar", out_t, [out_t], out=out_ap, in0=out_ap, scalar1=mul, scalar2=None, op0=ALU.mult)

    def build_A(self):
        nc = self.nc
        self.setup_common()
        x_d = self.dram_in("x", [T, D])
        g_d = self.dram_in("g_mix", [D])
        w_in = self.dram_in("w_in", [D, 13824])
        dec_d = self.dram_in("ret_decay", [8])
        cs_d = self.dram_in("rot_cs", [T, 128])
        nsc_d = self.dram_in("rot_nsc", [T, 128])
        o_kaT = self.dram_out("kaT", [6, 128, T], BF16)
        o_kcT = self.dram_out("kcT", [6, 128, T], BF16)
        o_va = self.dram_out("va", [T, 768], BF16)
        o_vc = self.dram_out("vc", [T, 768], BF16)
        o_L = self.dram_out("L", [2, 4, 128, 256], F32)
        self.aoff = 0
        self.part_A(x_d, None, g_d, w_in, dec_d, cs_d, nsc_d, (o_kaT, o_kcT, o_va, o_vc, o_L))

    def part_A(self, x_d, x_t, g_d, w_in, dec_d, cs_d, nsc_d, outs, mid_cb=None):
        o_kaT, o_kcT, o_va, o_vc, o_L = outs
        self.emit_decay(dec_d)
        self.emit_norm(g_d, src_dram=x_d, src_t=x_t)
        self.arena_reset()
        kT = self.ar("kT", [128, 6, T], BF16)
        vtm = self.ar("vtm", [128, NT, 768], BF16)

        def load_blocks(c0):
            blks = []
            for (cb, ncol) in ((0, 512), (512, 256)):
                slot = self.next_slot()
                blks.append((slot, self.load_w(slot, w_in, 0, D, c0 + cb, ncol), cb, ncol))
            return blks

        def proj_k(blks, o_t):
            for (slot, wv, cb, ncol) in blks:
                for m in range(ncol // 128):
                    h = (cb // 128) + m
                    for half in range(2):
                        ps = self.next_acc()
                        for kc in range(KC):
                            self.mm(ps, ps.ap, slot, wv[:, kc, m * 128:(m + 1) * 128], self.xnT,
                                    self.xnT.ap[:, kc, half * 512:(half + 1) * 512], kc == 0, kc == KC - 1)
                        self.evac(kT, kT.ap[:, h, half * 512:(half + 1) * 512], ps, ps.ap)
            self.store_kT(o_t, kT)

        def proj_v(blks, o_t):
            for (slot, wv, cb, ncol) in blks:
                for tt in range(NT):
                    ps = self.next_acc()
                    for kc in range(KC):
                        self.mm(ps, ps.ap[:, 0:ncol], self.xnT, self.xnT.ap[:, kc, tt * 128:(tt + 1) * 128], slot,
                                wv[:, kc, :], kc == 0, kc == KC - 1)
                    self.evac(vtm, vtm.ap[:, tt, cb:cb + ncol], ps, ps.ap[:, 0:ncol])
            self.store_v(o_t, vtm)
        proj_k(load_blocks(C_KA), o_kaT)
        proj_v(load_blocks(C_VA), o_va)
        kc_blks = load_blocks(C_KC)
        vc_blks = load_blocks(C_VC)
        if mid_cb is not None:
            mid_cb(0)
        proj_k(kc_blks, o_kcT)
        proj_v(vc_blks, o_vc)
        rot = self.load_rot(cs_d, nsc_d)
        ZF = self.ar("ZF", [128, NT, 4], F32)
        ZB = self.ar("ZB", [128, NT, 4], F32)
        for h in range(4):
            self.exp_scaled(ZF, ZF.ap[:, :, h], self.cst, self.cst.ap[:, CST_ZE:CST_ZE + 8], h, mul=INV_SQRT_HD)
            self.exp_scaled(ZB, ZB.ap[:, :, h], self.cst, self.cst.ap[:, CST_ZB:CST_ZB + 8], 4 + h, mul=INV_SQRT_HD)
        krot = self.ar("krot", [128, NT, 512], F32)
        vb = self.ar("vb", [128, NT, 1024], BF16)
        kzf = self.ar("kzf", [128, NT, 512], BF16)
        kzb = self.ar("kzb", [128, NT, 512], BF16)
        pre = []
        for c0 in (C_KB, C_VB, C_VB + 512):
            slot = self.next_slot()
            pre.append((slot, self.load_w(slot, w_in, 0, D, c0, 512)))
        if mid_cb is not None:
            mid_cb(1)
        slot, wv = pre[0]
        for tt in range(NT):
            ps = self.next_acc()
            for kc in range(KC):
                self.mm(ps, ps.ap, self.xnT, self.xnT.ap[:, kc, tt * 128:(tt + 1) * 128], slot, wv[:, kc, :], kc == 0, kc == KC - 1)
            self.rotary(krot, krot.ap[:, tt, :], ps, ps.ap, rot, tt, 4)
            for (zt, kz) in ((ZF, kzf), (ZB, kzb)):
                self.vop("dve", "tensor_tensor", kz, [krot, zt], out=kz.ap[:, tt, :].rearrange("p (h d) -> p h d", d=128),
                         in0=krot.ap[:, tt, :].rearrange("p (h d) -> p h d", d=128),
                         in1=zt.ap[:, tt, :].unsqueeze(2).to_broadcast([128, 4, 128]), op=ALU.mult)
        for cb in range(2):
            slot, wv = pre[1 + cb]
            for tt in range(NT):
                ps = self.next_acc()
                for kc in range(KC):
                    self.mm(ps, ps.ap, self.xnT, self.xnT.ap[:, kc, tt * 128:(tt + 1) * 128], slot, wv[:, kc, :], kc == 0, kc == KC - 1)
                self.evac(vb, vb.ap[:, tt, cb * 512:(cb + 1) * 512], ps, ps.ap)
        Ls = self.ar("Ls", [128, 2, 4, 256], F32)
        for di, kz in enumerate((kzf, kzb)):
            for h in range(4):
                ps = self.next_acc()
                for tt in range(NT):
                    self.mm(ps, ps.ap[:, 0:256], kz, kz.ap[:, tt, h * 128:(h + 1) * 128], vb, vb.ap[:, tt, h * 256:(h + 1) * 256],
                            tt == 0, tt == NT - 1)
                self.evac(Ls, Ls.ap[:, di, h, :], ps, ps.ap[:, 0:256])
        self.dma("sp", o_L, o_L.ap.rearrange("d h p v -> p d h v"), Ls.ap, reads=[Ls])

    def load_rot(self, cs_d, nsc_d):
        cs = [self.ar("rot_cs%d" % i, [128, 128], F32, dma=True) for i in range(2)]
        nsc = [self.ar("rot_nsc%d" % i, [128, 128], F32, dma=True) for i in range(2)]
        t1 = self.ar("rot_t1", [128, 512], F32)
        t2 = self.ar("rot_t2", [128, 512], F32)
        q32 = self.ar("rot_q32", [128, 512], F32)
        return (cs, nsc, t1, t2, q32, cs_d, nsc_d)

    def rotary(self, out_t, out_ap, ps, ps_ap, rot, tt, ng):
        csl, nscl, t1, t2, q32, cs_d, nsc_d = rot
        cs, nsc = csl[tt % 2], nscl[tt % 2]
        self.dma("sp", cs, cs.ap, cs_d[tt * 128:(tt + 1) * 128, :])
        self.dma("sp", nsc, nsc.ap, nsc_d[tt * 128:(tt + 1) * 128, :])
        n = ng * 128
        self.act(q32, q32.ap[:, 0:n], ps, ps_ap, AF.Copy)
        q4 = q32.ap[:, 0:n].rearrange("p (g s d) -> p g s d", s=2, d=64)
        shp = [128, ng, 2, 64]
        csb = cs.ap.rearrange("p (s d) -> p s d", d=64).unsqueeze(1).to_broadcast(shp)
        nscb = nsc.ap.rearrange("p (s d) -> p s d", d=64).unsqueeze(1).to_broadcast(shp)
        t1v = t1.ap[:, 0:n].rearrange("p (g s d) -> p g s d", s=2, d=64)
        t2v = t2.ap[:, 0:n].rearrange("p (g s d) -> p g s d", s=2, d=64)
        self.vop("dve", "tensor_tensor", t1, [q32, cs], out=t1v, in0=q4[:, :, 0, :].unsqueeze(2).to_broadcast(shp), in1=csb, op=ALU.mult)
        self.vop("dve", "tensor_tensor", t2, [q32, nsc], out=t2v, in0=q4[:, :, 1, :].unsqueeze(2).to_broadcast(shp), in1=nscb, op=ALU.mult)
        self.vop("dve", "tensor_tensor", out_t, [t1, t2], out=out_ap, in0=t1.ap[:, 0:n], in1=t2.ap[:, 0:n], op=ALU.add)


    def build_B(self, stop=None):
        self.setup_common()
        KB = 1024
        x_d = self.dram_in("x", [T, D])
        g_mix = self.dram_in("g_mix", [D])
        g_cross = self.dram_in("g_cross", [D])
        g_mem = self.dram_in("g_mem", [D])
        g_mlp = self.dram_in("g_mlp", [D])
        w_in = self.dram_in("w_in", [D, 13824])
        w_branch = self.dram_in("w_branch", [2560, D])
        w_out = self.dram_in("w_out", [D, D])
        w_cq = self.dram_in("w_cq", [D, 512])
        w_ckv = self.dram_in("w_ckv", [D, 1024])
        w_co = self.dram_in("w_co", [512, D])
        w_mlp1 = self.dram_in("w_mlp1", [D, 8192])
        w_mlp2 = self.dram_in("w_mlp2", [8192, D])
        mem_d = self.dram_in("mem", [256, D])
        dec_d = self.dram_in("ret_decay", [8])
        cs_d = self.dram_in("rot_cs", [T, 128])
        nsc_d = self.dram_in("rot_nsc", [T, 128])
        expo_d = self.dram_in("expo", [128, 16])
        biasA_d = self.dram_in("biasA", [6, 16, 128, 512])
        biasC_d = self.dram_in("biasC", [6, 128, DIL_STRIP])
        lmC_d = self.dram_in("lmC", [128, DIL_STRIP])
        vcol_d = self.dram_in("vcolC", [128, 24])
        kaT_d = self.dram_in("kaT_h", [6, 128, 1536], BF16)
        va_d = self.dram_in("va_h", [1536, 768], BF16)
        kcT_d = self.dram_in("kcT_h", [6, 128, 3072], BF16)
        vc_d = self.dram_in("vc_h", [3072, 768], BF16)
        Lall_d = self.dram_in("Lall", [2, 4, 4, 128, 256])
        x_out = self.dram_out("x_out", [T, D])
        KmT = self.sb("KmT", [128, 4, 256], BF16)
        Vm = self.sb("Vm", [128, 2, 512], BF16)
        a = dict(x_d=x_d, g_mix=g_mix, g_cross=g_cross, g_mem=g_mem, g_mlp=g_mlp, w_in=w_in, w_branch=w_branch, w_out=w_out, w_cq=w_cq,
                 w_ckv=w_ckv, w_co=w_co, w_mlp1=w_mlp1, w_mlp2=w_mlp2, mem_d=mem_d, dec_d=dec_d, cs_d=cs_d, nsc_d=nsc_d, expo_d=expo_d,
                 biasA_d=biasA_d, biasC_d=biasC_d, lmC_d=lmC_d, vcol_d=vcol_d, kaT_d=kaT_d, va_d=va_d, kcT_d=kcT_d, vc_d=vc_d,
                 Lall_d=Lall_d, x_out=x_out, KmT=KmT, Vm=Vm)
        self.part_B(a, stop=stop)

    def part_B(self, a, stop=None, norm1=True, final_g=None):
        KB = 1024
        (x_d, g_mix, g_cross, g_mem, g_mlp, w_in, w_branch, w_out, w_cq, w_ckv, w_co, w_mlp1, w_mlp2, mem_d, dec_d, cs_d, nsc_d, expo_d,
         biasA_d, biasC_d, lmC_d, vcol_d, kaT_d, va_d, kcT_d, vc_d, Lall_d, x_out, KmT, Vm) = [a[k] for k in (
            "x_d", "g_mix", "g_cross", "g_mem", "g_mlp", "w_in", "w_branch", "w_out", "w_cq", "w_ckv", "w_co", "w_mlp1", "w_mlp2", "mem_d",
            "dec_d", "cs_d", "nsc_d", "expo_d", "biasA_d", "biasC_d", "lmC_d", "vcol_d", "kaT_d", "va_d", "kcT_d", "vc_d", "Lall_d",
            "x_out", "KmT", "Vm")]
        x_t = a.get("x_t")
        halo_t = a.get("halo_t", ())
        lq = a.get("L_queue", "sp")
        xnT = self.xnT
        OT_OFF, MG_OFF, X_BYTES = 0, 68 * KB, 64 * KB

        self.aoff = 0
        self.emit_decay(dec_d)
        mnT = self.ar("mnT", [128, KC, 256], BF16)
        self.emit_norm(g_mem, src_dram=mem_d, ntiles=2, dstT=mnT, gname="gm")
        slot = self.next_slot()
        wv = self.load_w(slot, w_ckv, 0, D, 0, 512)
        for h in range(4):
            ps = self.next_acc()
            for kc in range(KC):
                self.mm(ps, ps.ap[:, 0:256], slot, wv[:, kc, h * 128:(h + 1) * 128], mnT, mnT.ap[:, kc, :], kc == 0, kc == KC - 1)
            self.evac(KmT, KmT.ap[:, h, :], ps, ps.ap[:, 0:256])
        slot = self.next_slot()
        wv = self.load_w(slot, w_ckv, 0, D, 512, 512)
        for t in range(2):
            ps = self.next_acc()
            for kc in range(KC):
                self.mm(ps, ps.ap, mnT, mnT.ap[:, kc, t * 128:(t + 1) * 128], slot, wv[:, kc, :], kc == 0, kc == KC - 1)
            self.evac(Vm, Vm.ap[:, t, :], ps, ps.ap)
        if norm1:
            self.arena_reset()
            self.emit_norm(g_mix, src_dram=x_d, src_t=x_t)
        self.arena_reset()
        oT = self.ar_at("oT", [128, 20, T], BF16, OT_OFF)
        self.aoff = 40 * KB
        self.emit_retention(w_in, cs_d, nsc_d, expo_d, Lall_d, oT, lq=lq)
        self.arena_reset(keep=40 * KB)
        self.emit_attn(w_in, C_QA, kaT_d, va_d, biasA_d, oT, 0, nkt_halo=12, nr=8, name="na", halo_t=halo_t)
        self.arena_reset(keep=40 * KB)
        self.emit_attn(w_in, C_QC, kcT_d, vc_d, biasC_d, oT, 14, nkt_halo=24, nr=20, name="dil", lm_d=lmC_d, vcol_d=vcol_d, halo_t=halo_t)
        if stop == "mix":
            o = self.dram_out("oT_dbg", [20, 128, T], BF16)
            self.dma("sp", o, o.ap.rearrange("c p t -> p c t"), oT.ap, reads=[oT])
            return
        self.arena_reset(keep=40 * KB)
        mergedT = self.ar_at("mergedT", [128, KC, T], BF16, MG_OFF)
        SIG = [self.ar("sig%d" % g, [128, 512], F32) for g in range(3)]
        M = [self.ar("mrg%d" % g, [128, 512], F32) for g in range(3)]
        for fc in range(16):
            sw = self.next_slot()
            wbv = self.load_w(sw, w_branch, 0, 2560, fc * 128, 128)
            gsl = []
            for g in range(2):
                gsl.append((sw, self.load_w(sw, w_in, 0, D, C_SA + g * 2048 + fc * 128, 128, elem_off=2560 + g * 2048)))
            s2 = self.next_slot()
            gsl.append((s2, self.load_w(s2, w_in, 0, D, C_SA + 2 * 2048 + fc * 128, 128)))
            for half in range(2):
                hs = slice(half * 512, (half + 1) * 512)
                for g in range(3):
                    ps = self.next_acc()
                    gs_, gv = gsl[g]
                    for kc in range(KC):
                        self.mm(ps, ps.ap, gs_, gv[:, kc, :], xnT, xnT.ap[:, kc, hs], kc == 0, kc == KC - 1)
                    self.act(SIG[g], SIG[g].ap, ps, ps.ap, AF.Sigmoid)
                for g, (k0, nk) in enumerate(((0, 6), (6, 8), (14, 6))):
                    ps = self.next_acc()
                    for k in range(nk):
                        self.mm(ps, ps.ap, sw, wbv[:, k0 + k, :], oT, oT.ap[:, k0 + k, hs], k == 0, k == nk - 1)
                    self.vop("dve", "tensor_tensor", M[g], [ps, SIG[g]], out=M[g].ap, in0=ps.ap, in1=SIG[g].ap, op=ALU.mult)
                self.vop("dve", "tensor_tensor", M[0], [M[0], M[1]], out=M[0].ap, in0=M[0].ap, in1=M[1].ap, op=ALU.add)
                self.vop("dve", "tensor_tensor", mergedT, [M[0], M[2]], out=mergedT.ap[:, fc, hs], in0=M[0].ap, in1=M[2].ap, op=ALU.add)
        self.arena_reset(keep=X_BYTES)
        xs = []
        for tt in range(NT):
            xt = self.ar_at("x%d" % tt, [128, D], F32, tt * 8 * KB, dma=True)
            self.dma("sp", xt, xt.ap, x_d[tt * 128:(tt + 1) * 128, :], reads=[x_t] if x_t is not None else ())
            xs.append(xt)
        for cb in range(4):
            slot = self.next_slot()
            wv = self.load_w(slot, w_out, 0, D, cb * 512, 512)
            for tt in range(NT):
                ps = self.next_acc()
                for kc in range(KC):
                    self.mm(ps, ps.ap, mergedT, mergedT.ap[:, kc, tt * 128:(tt + 1) * 128], slot, wv[:, kc, :], kc == 0, kc == KC - 1)
                xa = xs[tt].ap[:, cb * 512:(cb + 1) * 512]
                self.vop("dve", "tensor_tensor", xs[tt], [xs[tt], ps], out=xa, in0=xa, in1=ps.ap, op=ALU.add)
        if stop == "x1":
            for tt in range(NT):
                self.dma("sp", x_out, x_out.ap[tt * 128:(tt + 1) * 128, :], xs[tt].ap, reads=[xs[tt]])
            return
        self.arena_reset(keep=X_BYTES)
        self.emit_norm(g_cross, src_tiles=xs, gname="gc")
        self.arena_reset(keep=X_BYTES)
        QxT = self.ar("QxT", [128, T], BF16)
        PT = [self.ar("cPT%d" % i, [128, 512], BF16) for i in range(2)]
        rc = self.ar("crc", [128, 512], F32)
        ocT = self.ar("ocT", [128, 4, T], BF16)
        for h in range(4):
            slot = self.next_slot()
            wv = self.load_w(slot, w_cq, 0, D, h * 128, 128)
            for half in range(2):
                ps = self.next_acc()
                for kc in range(KC):
                    self.mm(ps, ps.ap, slot, wv[:, kc, :], xnT, xnT.ap[:, kc, half * 512:(half + 1) * 512], kc == 0, kc == KC - 1)
                self.evac(QxT, QxT.ap[:, half * 512:(half + 1) * 512], ps, ps.ap, scale=INV_SQRT_HD)
            for half in range(2):
                hs = slice(half * 512, (half + 1) * 512)
                num, den = self.patt
                for kt in range(2):
                    sc = self.next_acc()
                    self.mm(sc, sc.ap, KmT, KmT.ap[:, h, kt * 128:(kt + 1) * 128], QxT, QxT.ap[:, hs], True, True)
                    p_ = PT[kt]
                    self.act(p_, p_.ap, sc, sc.ap, AF.Exp)
                    self.mm(num, num.ap, Vm, Vm.ap[:, kt, h * 128:(h + 1) * 128], p_, p_.ap, kt == 0, kt == 1)
                    self.mm(den, den.ap, self.ones, self.ones.ap, p_, p_.ap, kt == 0, kt == 1)
                self.vop("dve", "reciprocal", rc, [den], out=rc.ap, in_=den.ap)
                self.vop("dve", "tensor_tensor", ocT, [num, rc], out=ocT.ap[:, h, hs], in0=num.ap, in1=rc.ap, op=ALU.mult)
        for cb in range(4):
            slot = self.next_slot()
            wv = self.load_w(slot, w_co, 0, 512, cb * 512, 512)
            for tt in range(NT):
                ps = self.next_acc()
                for k in range(4):
                    self.mm(ps, ps.ap, ocT, ocT.ap[:, k, tt * 128:(tt + 1) * 128], slot, wv[:, k, :], k == 0, k == 3)
                xa = xs[tt].ap[:, cb * 512:(cb + 1) * 512]
                self.vop("dve", "tensor_tensor", xs[tt], [xs[tt], ps], out=xa, in0=xa, in1=ps.ap, op=ALU.add)
        if stop == "x2":
            for tt in range(NT):
                self.dma("sp", x_out, x_out.ap[tt * 128:(tt + 1) * 128, :], xs[tt].ap, reads=[xs[tt]])
            return
        self.arena_reset(keep=X_BYTES)
        self.emit_norm(g_mlp, src_tiles=xs, gname="gl")
        self.arena_reset(keep=X_BYTES)
        hT = self.ar("hT", [128, KC, T], BF16)
        r32 = [self.ar("r32_%d" % i, [128, 512], F32) for i in range(2)]
        ri = 0
        for q in range(4):
            for blk in range(4):
                slot = self.next_slot()
                wv = self.load_w(slot, w_mlp1, 0, D, q * 2048 + blk * 512, 512)
                for m in range(4):
                    hc = blk * 4 + m
                    for half in range(2):
                        ps = self.next_acc()
                        for kc in range(KC):
                            self.mm(ps, ps.ap, slot, wv[:, kc, m * 128:(m + 1) * 128], xnT, xnT.ap[:, kc, half * 512:(half + 1) * 512],
                                    kc == 0, kc == KC - 1)
                        ri ^= 1
                        r_ = r32[ri]
                        self.act(r_, r_.ap, ps, ps.ap, AF.Relu)
                        self.vop("dve", "tensor_tensor", hT, [r_], out=hT.ap[:, hc, half * 512:(half + 1) * 512], in0=r_.ap, in1=r_.ap, op=ALU.mult)
            for cb in range(4):
                slot = self.next_slot()
                wv = self.load_w(slot, w_mlp2, q * 2048, 2048, cb * 512, 512)
                for tt in range(NT):
                    ps = self.next_acc()
                    for k in range(KC):
                        self.mm(ps, ps.ap, hT, hT.ap[:, k, tt * 128:(tt + 1) * 128], slot, wv[:, k, :], k == 0, k == KC - 1)
                    xa = xs[tt].ap[:, cb * 512:(cb + 1) * 512]
                    self.vop("dve", "tensor_tensor", xs[tt], [xs[tt], ps], out=xa, in0=xa, in1=ps.ap, op=ALU.add)
        if final_g is None:
            for tt in range(NT):
                self.dma("sp", x_out, x_out.ap[tt * 128:(tt + 1) * 128, :], xs[tt].ap, reads=[xs[tt]])
            return
        self.arena_reset(keep=X_BYTES)
        g_bc = self.ar("gf_bc", [128, D], F32, dma=True)
        self.dma("sp", g_bc, g_bc.ap, final_g.partition_broadcast(128))
        junk = self.ar("gf_junk", [128, D], BF16)
        st = self.ar("gf_st", [128, 4 * NT], F32)
        ys = [self.ar("gf_ys%d" % i, [128, D], F32) for i in range(2)]
        youts = [Tn(x_out.ap, Buf("y_st%d" % i, self.new_dsem("y_st%d" % i))) for i in range(2)]
        for tt in range(NT):
            xt, yt = xs[tt], ys[tt % 2]
            c = 4 * tt
            self.act(junk, junk.ap, xt, xt.ap, AF.Square, accum_out=st.ap[:, c:c + 1], writes=[st])
            self.vop("dve", "tensor_scalar", st, [st], out=st.ap[:, c + 1:c + 2], in0=st.ap[:, c:c + 1], scalar1=1.0 / D, scalar2=EPS, op0=ALU.mult, op1=ALU.add)
            self.act(st, st.ap[:, c + 2:c + 3], st, st.ap[:, c + 1:c + 2], AF.Sqrt)
            self.vop("dve", "reciprocal", st, [st], out=st.ap[:, c + 3:c + 4], in_=st.ap[:, c + 2:c + 3])
            self.vop("dve", "scalar_tensor_tensor", yt, [xt, st, g_bc], out=yt.ap, in0=xt.ap, scalar=st.ap[:, c + 3:c + 4], in1=g_bc.ap, op0=ALU.mult, op1=ALU.mult)
            self.dma("sp", youts[tt % 2], x_out.ap[tt * 128:(tt + 1) * 128, :], yt.ap, reads=[yt])

    def emit_retention(self, w_in, cs_d, nsc_d, expo_d, Lall_d, oT, lq="sp"):
        cst, small = self.cst, self.small
        rot = self.load_rot(cs_d, nsc_d)
        DT = self.ar("DT", [128, 4, 128], F32)
        XIF = self.ar("XIF", [128, 4, 128], F32)
        ZBB = self.ar("ZBB", [128, 4, 128], F32)
        tmpD = self.ar("tmpD", [128, 128], F32)
        expo = self.ar("expo", [128, 16], F32, dma=True)
        self.dma("sp", expo, expo.ap, expo_d)
        wS = self.ar("wS", [128, 2, 4, 4], F32)
        for h in range(4):
            self.exp_scaled(DT, DT.ap[:, h, :], cst, cst.ap[:, CST_PF:CST_PF + 128], h)
            self.vop("dve", "tensor_tensor", DT, [DT, cst], out=DT.ap[:, h, :], in0=DT.ap[:, h, :], in1=cst.ap[:, CST_UF:CST_UF + 128], op=ALU.mult)
            self.exp_scaled(tmpD, tmpD.ap, cst, cst.ap[:, CST_PB:CST_PB + 128], 4 + h)
            self.vop("dve", "tensor_tensor", tmpD, [tmpD, cst], out=tmpD.ap, in0=tmpD.ap, in1=cst.ap[:, CST_UB:CST_UB + 128], op=ALU.mult)
            self.vop("dve", "tensor_tensor", DT, [DT, tmpD], out=DT.ap[:, h, :], in0=DT.ap[:, h, :], in1=tmpD.ap, op=ALU.add)
            self.vop("dve", "tensor_scalar", DT, [DT], out=DT.ap[:, h, :], in0=DT.ap[:, h, :], scalar1=INV_SQRT_HD, scalar2=None, op0=ALU.mult)
            self.exp_scaled(XIF, XIF.ap[:, h, :], cst, cst.ap[:, CST_N1:CST_N1 + 128], h)
            self.exp_scaled(ZBB, ZBB.ap[:, h, :], cst, cst.ap[:, CST_N2:CST_N2 + 128], 4 + h)
            self.exp_scaled(small, small.ap[:, h:h + 1], cst, cst.ap[:, CST_M2:CST_M2 + 1], h, mul=INV_SQRT_HD)
            self.exp_scaled(small, small.ap[:, 4 + h:5 + h], cst, cst.ap[:, CST_M1:CST_M1 + 1], 4 + h, mul=INV_SQRT_HD)
            for di in range(2):
                self.exp_scaled(wS, wS.ap[:, di, h, :], expo, expo.ap[:, 8 * di:8 * di + 4], 4 * di + h)
                self.vop("dve", "tensor_tensor", wS, [wS, expo], out=wS.ap[:, di, h, :], in0=wS.ap[:, di, h, :],
                         in1=expo.ap[:, 8 * di + 4:8 * di + 8], op=ALU.mult)
        self.act(small, small.ap[:, 8:16], self.lg, self.lg.ap, AF.Exp, scale=128.0)
        QKT = self.ar("QKT", [128, 2, T], BF16)
        Qxf = self.ar("Qxf", [128, NT, 128], BF16)
        Qzb = self.ar("Qzb", [128, NT, 128], BF16)
        Kzf = self.ar("Kzf", [128, NT, 128], BF16)
        Kxb = self.ar("Kxb", [128, NT, 128], BF16)
        vbh = self.ar("vbh", [128, NT, 256], BF16)
        sg = self.ar("sg", [128, NT, 256], BF16)
        rotbf = [self.ar("rotbf%d" % i, [128, 256], BF16) for i in range(2)]
        Sbf = [self.ar("Sbf%d" % d, [128, NT, 256], BF16) for d in range(2)]
        S32 = [[self.ar("S32_%d%d" % (d, i), [128, 256], F32) for i in range(2)] for d in range(2)]
        Lh = [self.ar("Lh%d" % d, [128, 4, 256], F32, dma=True) for d in range(2)]
        AT = [self.ar("AT%d" % i, [128, 128], BF16) for i in range(2)]
        junk = self.ar("rjunk", [128, 256], BF16)
        stt = [self.ar("stt%d" % i, [128, 8], F32) for i in range(2)]
        yn = [self.ar("yn%d" % i, [128, 256], F32) for i in range(2)]
        ob = [self.ar("ob%d" % i, [128, 256], BF16) for i in range(2)]
        xnT = self.xnT
        for h in range(4):
            sA = self.next_slot()
            vA = self.load_w(sA, w_in, 0, D, C_QB + 128 * h, 128, col_off=0, total_cols=512)
            self.load_w(sA, w_in, 0, D, C_KB + 128 * h, 128, col_off=128, total_cols=512)
            self.load_w(sA, w_in, 0, D, C_VB + 256 * h, 256, col_off=256, total_cols=512)
            sB = self.next_slot()
            vB = self.load_w(sB, w_in, 0, D, C_GR + 256 * h, 256)
            for di in range(2):
                self.dma(lq, Lh[di], Lh[di].ap, Lall_d[di, :, h].rearrange("j p v -> p j v"))
            for tt in range(NT):
                ts = slice(tt * 128, (tt + 1) * 128)
                ps = self.next_acc()
                for kc in range(KC):
                    self.mm(ps, ps.ap, xnT, xnT.ap[:, kc, ts], sA, vA[:, kc, :], kc == 0, kc == KC - 1)
                rb = rotbf[tt % 2]
                self.rotary(rb, rb.ap, ps, ps.ap[:, 0:256], rot, tt, 2)
                self.evac(vbh, vbh.ap[:, tt, :], ps, ps.ap[:, 256:512])
                pt = self.next_ptr()
                self.tr(pt, pt.ap[:, 0:128], rb, rb.ap[:, 0:128])
                self.tr(pt, pt.ap[:, 128:256], rb, rb.ap[:, 128:256])
                self.evac(QKT, QKT.ap[:, :, ts], pt, pt.ap[:, 0:256].rearrange("p (a b) -> p a b", b=128))
                self.vop("dve", "tensor_scalar", Kzf, [rb, small], out=Kzf.ap[:, tt, :], in0=rb.ap[:, 128:256], scalar1=small.ap[:, h:h + 1],
                         scalar2=None, op0=ALU.mult)
                self.vop("dve", "tensor_scalar", Kxb, [rb, small], out=Kxb.ap[:, tt, :], in0=rb.ap[:, 128:256], scalar1=small.ap[:, 4 + h:5 + h],
                         scalar2=None, op0=ALU.mult)
                ps2 = self.next_acc()
                for kc in range(KC):
                    self.mm(ps2, ps2.ap[:, 0:256], xnT, xnT.ap[:, kc, ts], sB, vB[:, kc, :], kc == 0, kc == KC - 1)
                self.act(sg, sg.ap[:, tt, :], ps2, ps2.ap[:, 0:256], AF.Silu)
            q3 = QKT.ap[:, 0, :].rearrange("p (c n) -> p c n", n=128)
            self.vop("dve", "tensor_tensor", Qxf, [QKT, XIF], out=Qxf.ap, in0=q3, in1=XIF.ap[:, h, :].unsqueeze(1).to_broadcast([128, NT, 128]), op=ALU.mult)
            self.vop("dve", "tensor_tensor", Qzb, [QKT, ZBB], out=Qzb.ap, in0=q3, in1=ZBB.ap[:, h, :].unsqueeze(1).to_broadcast([128, NT, 128]), op=ALU.mult)
            for di in range(2):
                s0 = S32[di][0]
                self.vop("dve", "tensor_scalar", s0, [Lh[di], wS], out=s0.ap, in0=Lh[di].ap[:, 0, :], scalar1=wS.ap[:, di, h, 0:1], scalar2=None, op0=ALU.mult)
                for j in range(1, 4):
                    self.vop("dve", "scalar_tensor_tensor", s0, [Lh[di], wS, s0], out=s0.ap, in0=Lh[di].ap[:, j, :], scalar=wS.ap[:, di, h, j:j + 1],
                             in1=s0.ap, op0=ALU.mult, op1=ALU.add)
            for di in range(2):
                order = list(range(NT)) if di == 0 else list(range(NT - 1, -1, -1))
                kz = Kzf if di == 0 else Kxb
                gcol = 8 + 4 * di + h
                cur = 0
                self.act(Sbf[di], Sbf[di].ap[:, order[0], :], S32[di][0], S32[di][0].ap, AF.Copy)
                for idx in range(NT - 1):
                    i = order[idx]
                    ps = self.next_acc()
                    self.mm(ps, ps.ap[:, 0:256], kz, kz.ap[:, i, :], vbh, vbh.ap[:, i, :], True, True)
                    nxt = cur ^ 1
                    self.vop("dve", "scalar_tensor_tensor", S32[di][nxt], [S32[di][cur], small, ps], out=S32[di][nxt].ap, in0=S32[di][cur].ap,
                             scalar=small.ap[:, gcol:gcol + 1], in1=ps.ap[:, 0:256], op0=ALU.mult, op1=ALU.add)
                    self.act(Sbf[di], Sbf[di].ap[:, order[idx + 1], :], S32[di][nxt], S32[di][nxt].ap, AF.Copy)
                    cur = nxt
            for i in range(NT):
                ts = slice(i * 128, (i + 1) * 128)
                sc = self.next_acc()
                self.mm(sc, sc.ap[:, 0:128], QKT, QKT.ap[:, 1, ts], QKT, QKT.ap[:, 0, ts], True, True)
                at = AT[i % 2]
                self.vop("dve", "tensor_tensor", at, [sc, DT], out=at.ap, in0=sc.ap[:, 0:128], in1=DT.ap[:, h, :], op=ALU.mult)
                o = self.patt[i % 2]
                oa = o.ap[:, 0:256]
                self.mm(o, oa, at, at.ap, vbh, vbh.ap[:, i, :], True, False)
                self.mm(o, oa, Qxf, Qxf.ap[:, i, :], Sbf[0], Sbf[0].ap[:, i, :], False, False)
                self.mm(o, oa, Qzb, Qzb.ap[:, i, :], Sbf[1], Sbf[1].ap[:, i, :], False, True)
                st = stt[i % 2]
                self.act(junk, junk.ap, o, oa, AF.Copy, accum_out=st.ap[:, 0:1], writes=[st])
                self.act(junk, junk.ap, o, oa, AF.Square, accum_out=st.ap[:, 1:2], writes=[st])
                self.vop("dve", "tensor_scalar", st, [st], out=st.ap[:, 2:4], in0=st.ap[:, 0:2], scalar1=1.0 / 256, scalar2=None, op0=ALU.mult)
                self.vop("dve", "tensor_tensor", st, [st], out=st.ap[:, 4:5], in0=st.ap[:, 2:3], in1=st.ap[:, 2:3], op=ALU.mult)
                self.vop("dve", "tensor_tensor", st, [st], out=st.ap[:, 5:6], in0=st.ap[:, 3:4], in1=st.ap[:, 4:5], op=ALU.subtract)
                self.vop("dve", "tensor_scalar", st, [st], out=st.ap[:, 5:6], in0=st.ap[:, 5:6], scalar1=EPS, scalar2=None, op0=ALU.add)
                self.act(st, st.ap[:, 6:7], st, st.ap[:, 5:6], AF.Sqrt)
                self.vop("dve", "reciprocal", st, [st], out=st.ap[:, 7:8], in_=st.ap[:, 6:7])
                y_ = yn[i % 2]
                self.vop("dve", "tensor_scalar", y_, [o, st], out=y_.ap, in0=oa, scalar1=st.ap[:, 2:3], scalar2=st.ap[:, 7:8],
                         op0=ALU.subtract, op1=ALU.mult)
                ob_ = ob[i % 2]
                self.vop("dve", "tensor_tensor", ob_, [y_, sg], out=ob_.ap, in0=y_.ap, in1=sg.ap[:, i, :], op=ALU.mult)
                pt = self.next_ptr()
                self.tr(pt, pt.ap[:, 0:128], ob_, ob_.ap[:, 0:128])
                self.tr(pt, pt.ap[:, 128:256], ob_, ob_.ap[:, 128:256])
                self.evac(oT, oT.ap[:, 6 + 2 * h:8 + 2 * h, ts], pt, pt.ap[:, 0:256].rearrange("p (a b) -> p a b", b=128))

    def emit_attn(self, w_in, c_q, kT_d, v_d, bias_d, oT, o_chunk0, nkt_halo, nr, name, lm_d=None, vcol_d=None, halo_t=()):
        xnT = self.xnT
        QT = self.ar(name + "QT", [128, T], BF16)
        KT = [self.ar(name + "KT%d" % i, [128, nkt_halo * 128], BF16, dma=True) for i in range(2)]
        V = [self.ar(name + "V%d" % i, [128, nkt_halo, 128], BF16, dma=True) for i in range(2)]
        strip = lm_d is not None
        if strip:
            bias2 = [self.ar(name + "bias%d" % i, [128, DIL_STRIP], BF16, dma=True) for i in range(2)]
        else:
            nb = bias_d.shape[1]
            bias2 = [self.ar(name + "bias%d" % i, [128, nb, 512], BF16, dma=True) for i in range(2)]
        PT = [self.ar(name + "PT%d" % i, [128, 512], BF16) for i in range(3)]
        rc = self.ar(name + "rc", [128, 512], F32)
        lm = vcol = None
        if strip:
            lm = self.ar(name + "lm", [128, DIL_STRIP], BF16, dma=True)
            self.dma("pool", lm, lm.ap, lm_d)
            vcol = self.ar(name + "vcol", [128, 24], F32, dma=True)
            self.dma("sp", vcol, vcol.ap, vcol_d)
        for h in range(6):
            slot = self.next_slot()
            wv = self.load_w(slot, w_in, 0, D, c_q + 128 * h, 128)
            kt_, v_ = KT[h % 2], V[h % 2]
            self.dma("sp", kt_, kt_.ap, kT_d[h], reads=halo_t)
            self.dma("sp", v_, v_.ap, v_d[:, h * 128:(h + 1) * 128].rearrange("(t p) d -> p t d", p=128), reads=halo_t)
            bias = bias2[h % 2]
            if strip:
                self.dma("pool", bias, bias.ap, bias_d[h])
                self.vop("dve", "tensor_tensor", bias, [bias, lm], out=bias.ap, in0=bias.ap, in1=lm.ap, op=ALU.add)
            else:
                self.dma("pool", bias, bias.ap, bias_d[h].rearrange("r k n -> k r n"))
            for half in range(2):
                ps = self.next_acc()
                for kc in range(KC):
                    self.mm(ps, ps.ap, slot, wv[:, kc, :], xnT, xnT.ap[:, kc, half * 512:(half + 1) * 512], kc == 0, kc == KC - 1)
                self.evac(QT, QT.ap[:, half * 512:(half + 1) * 512], ps, ps.ap, scale=INV_SQRT_HD)
            for qg in range(2):
                qs = slice(qg * 512, (qg + 1) * 512)
                num, den = self.patt
                for r in range(nr):
                    kk = 4 * qg + r
                    sc = self.next_acc()
                    self.mm(sc, sc.ap, kt_, kt_.ap[:, kk * 128:(kk + 1) * 128], QT, QT.ap[:, qs], True, False)
                    if strip:
                        b_ap = bias.ap[:, 128 * (19 - r):128 * (19 - r) + 512]
                    else:
                        b_ap = bias.ap[:, qg * nr + r, :]
                    self.mm(sc, sc.ap, self.ident, self.ident.ap, bias, b_ap, False, True)
                    p_ = PT[r % 3]
                    if vcol is None:
                        self.act(p_, p_.ap, sc, sc.ap, AF.Exp)
                    else:
                        self.act(p_, p_.ap, sc, sc.ap, AF.Exp, bias=vcol.ap[:, kk:kk + 1], reads=[vcol])
                    self.mm(num, num.ap, v_, v_.ap[:, kk, :], p_, p_.ap, r == 0, r == nr - 1)
                    self.mm(den, den.ap, self.ones, self.ones.ap, p_, p_.ap, r == 0, r == nr - 1)
                self.vop("dve", "reciprocal", rc, [den], out=rc.ap, in_=den.ap)
                self.vop("dve", "tensor_tensor", oT, [num, rc], out=oT.ap[:, o_chunk0 + h, qs], in0=num.ap, in1=rc.ap, op=ALU.mult)

    def build_F(self):
        self.setup_common()
        x_d = self.dram_in("x", [T, D])
        g_d = self.dram_in("g_final", [D])
        y = self.dram_out("y", [T, D])
        self.aoff = 0
        g_bc = self.ar("g_bc", [128, D], F32, dma=True)
        self.dma("sp", g_bc, g_bc.ap, g_d.partition_broadcast(128))
        junk = self.ar("junk", [128, D], BF16)
        st = self.ar("st", [128, 4 * NT], F32)
        xs = [self.ar("xs%d" % i, [128, D], F32, dma=True) for i in range(2)]
        ys = [self.ar("ys%d" % i, [128, D], F32) for i in range(2)]
        for tt in range(NT):
            xt, yt = xs[tt % 2], ys[tt % 2]
            c = 4 * tt
            self.dma("sp", xt, xt.ap, x_d[tt * 128:(tt + 1) * 128, :])
            self.act(junk, junk.ap, xt, xt.ap, AF.Square, accum_out=st.ap[:, c:c + 1], writes=[st])
            self.vop("dve", "tensor_scalar", st, [st], out=st.ap[:, c + 1:c + 2], in0=st.ap[:, c:c + 1], scalar1=1.0 / D, scalar2=EPS, op0=ALU.mult, op1=ALU.add)
            self.act(st, st.ap[:, c + 2:c + 3], st, st.ap[:, c + 1:c + 2], AF.Sqrt)
            self.vop("dve", "reciprocal", st, [st], out=st.ap[:, c + 3:c + 4], in_=st.ap[:, c + 2:c + 3])
            self.vop("dve", "scalar_tensor_tensor", yt, [xt, st, g_bc], out=yt.ap, in0=xt.ap, scalar=st.ap[:, c + 3:c + 4], in1=g_bc.ap, op0=ALU.mult, op1=ALU.mult)
            self.dma("sp", y, y.ap[tt * 128:(tt + 1) * 128, :], yt.ap, reads=[yt])


    def internal(self, name, shape, dtype):
        ap = self.nc.dram_tensor(name, list(shape), dtype).ap()
        return Tn(ap, Buf(name, self.new_dsem(name)))

    def collective(self, src_ts, send_ap, recv_ap):
        self.cc_n += 1
        n = self.cc_n

        def fn(g):
            g.collective_compute("AllGather", ALU.bypass, replica_groups=[[0, 1, 2, 3], [4, 5, 6, 7]],
                                 ins=[send_ap.opt()], outs=[recv_ap.opt()], dma_qos="P3").then_inc(self.cc_sem, 1)
            return g.wait_ge(self.cc_sem, n)
        o = self.P.op("pool", fn, reads=[t.b for t in src_ts], writes=[])
        o.noevent = True

    def store_kT(self, o, kT):
        if isinstance(o, Tn):
            self.dma("sp", o, o.ap.rearrange("h p t -> p h t"), kT.ap, reads=[kT])
        else:
            self.dma("sp", o[0], o[0].ap.rearrange("(h p) t -> p h t", p=128), kT.ap[:, 0:4, :], reads=[kT])
            self.dma("sp", o[1], o[1].ap.rearrange("(h p) t -> p h t", p=128), kT.ap[:, 4:6, :], reads=[kT])

    def store_v(self, o, vtm):
        if isinstance(o, Tn):
            self.dma("sp", o, o.ap.rearrange("(t p) c -> p t c", p=128), vtm.ap, reads=[vtm])
        else:
            self.dma("sp", o[0], o[0].ap.rearrange("(t p) c -> p t c", p=128), vtm.ap[:, :, 0:384], reads=[vtm])
            self.dma("sp", o[1], o[1].ap.rearrange("(t p) c -> p t c", p=128), vtm.ap[:, :, 384:768], reads=[vtm])

    def assemble_halos(self, l, pieces):
        kaT_h = self.internal("kaT_h%d" % l, [6, 128, 1536], BF16)
        va_h = self.internal("va_h%d" % l, [1536, 768], BF16)
        kcT_h = self.internal("kcT_h%d" % l, [6, 128, 3072], BF16)
        vc_h = self.internal("vc_h%d" % l, [3072, 768], BF16)
        NE = 4096
        Rs = [[self.ar("hR%d_%d" % (st, i), [128, NE], BF16, dma=True) for i in range(4)] for st in range(2)]
        accs = [self.ar("hacc%d" % st, [128, NE], BF16) for st in range(2)]
        wsel = self.wsel
        specs = (("ka", "k", 256, kaT_h), ("kc", "k", 1024, kcT_h), ("va", "v", 256, va_h), ("vc", "v", 1024, vc_h))
        step = 0
        alias = []
        for (nm, kind, hw, out) in specs:
            (s0, r0), (s1, r1) = pieces[nm]
            outs2 = [Tn(out.ap, Buf(nm + "_hs%d" % st, self.new_dsem(nm + "_hs%d" % st))) for st in range(2)]
            alias.extend(outs2)
            if kind == "k":
                self.dma("sp", out, out.ap[0:4, :, hw:hw + T], s0.ap.rearrange("(h p) t -> h p t", p=128), reads=[s0])
                self.dma("sp", out, out.ap[4:6, :, hw:hw + T], s1.ap.rearrange("(h p) t -> h p t", p=128), reads=[s1])
            else:
                self.dma("sp", out, out.ap[hw:hw + T, 0:384], s0.ap, reads=[s0])
                self.dma("sp", out, out.ap[hw:hw + T, 384:768], s1.ap, reads=[s1])
            for side in range(2):
                ranks = (0, 1, 2) if side == 0 else (1, 2, 3)
                for pi, rp in enumerate((r0, r1)):
                    R, acc = Rs[step % 2], accs[step % 2]
                    out_s = outs2[step % 2]
                    step += 1
                    if kind == "k":
                        nh, h0 = (4, 0) if pi == 0 else (2, 4)
                        n = nh * hw
                    else:
                        c0 = 384 * pi
                        n = (hw // 128) * 384
                    for r in ranks:
                        if kind == "k":
                            full = rp[r * nh * 128:(r + 1) * nh * 128, :].rearrange("(h p) t -> p h t", p=128)
                            src = full[:, :, T - hw:T] if side == 0 else full[:, :, 0:hw]
                            dst = R[r].ap[:, 0:n].rearrange("p (h t) -> p h t", t=hw)
                        else:
                            full = rp[r * T:(r + 1) * T, :]
                            src = (full[T - hw:T, :] if side == 0 else full[0:hw, :]).rearrange("(t p) c -> p t c", p=128)
                            dst = R[r].ap[:, 0:n].rearrange("p (t c) -> p t c", c=384)
                        self.dma("pool", R[r], dst, src)
                    c = 4 * side
                    ra = ranks[0]
                    self.vop("dve", "tensor_scalar", acc, [R[ra], wsel], out=acc.ap[:, 0:n], in0=R[ra].ap[:, 0:n], scalar1=wsel.ap[:, c + ra:c + ra + 1],
                             scalar2=None, op0=ALU.mult)
                    for r in ranks[1:]:
                        self.vop("dve", "scalar_tensor_tensor", acc, [R[r], wsel, acc], out=acc.ap[:, 0:n], in0=R[r].ap[:, 0:n],
                                 scalar=wsel.ap[:, c + r:c + r + 1], in1=acc.ap[:, 0:n], op0=ALU.mult, op1=ALU.add)
                    if kind == "k":
                        reg = out.ap[h0:h0 + nh, :, 0:hw] if side == 0 else out.ap[h0:h0 + nh, :, hw + T:hw + T + hw]
                        self.dma("sp", out_s, reg.rearrange("h p t -> p h t"), acc.ap[:, 0:n].rearrange("p (h t) -> p h t", t=hw), reads=[acc])
                    else:
                        reg = out.ap[0:hw, c0:c0 + 384] if side == 0 else out.ap[hw + T:hw + T + hw, c0:c0 + 384]
                        self.dma("sp", out_s, reg.rearrange("(t p) c -> p t c", p=128), acc.ap[:, 0:n].rearrange("p (t c) -> p t c", c=384), reads=[acc])
        self.halo_alias = tuple(alias)
        return kaT_h, va_h, kcT_h, vc_h

    def build_fused(self, nlayers=DEPTH):
        nc = self.nc
        self.setup_common()
        x_in = self.dram_in("x", [T, D])
        mem_d = self.dram_in("mem", [256, D])
        g_mix = self.dram_in("norm_mix_g", [DEPTH, D])
        g_cross = self.dram_in("norm_cross_g", [DEPTH, D])
        g_mem = self.dram_in("norm_mem_g", [DEPTH, D])
        g_mlp = self.dram_in("norm_mlp_g", [DEPTH, D])
        g_final = self.dram_in("final_norm_g", [D])
        w_in = [self.dram_in("w_in_%d" % l, [D, 13824]) for l in range(nlayers)]
        w_branch = [self.dram_in("w_branch_%d" % l, [2560, D]) for l in range(nlayers)]
        w_out = [self.dram_in("w_out_%d" % l, [D, D]) for l in range(nlayers)]
        w_cq = [self.dram_in("w_cq_%d" % l, [D, 512]) for l in range(nlayers)]
        w_ckv = [self.dram_in("w_ckv_%d" % l, [D, 1024]) for l in range(nlayers)]
        w_co = [self.dram_in("w_co_%d" % l, [512, D]) for l in range(nlayers)]
        w_mlp1 = [self.dram_in("w_mlp1_%d" % l, [D, 8192]) for l in range(nlayers)]
        w_mlp2 = [self.dram_in("w_mlp2_%d" % l, [8192, D]) for l in range(nlayers)]
        biasA_d = [self.dram_in("biasA_%d" % l, [6, 16, 128, 512]) for l in range(nlayers)]
        dec = self.dram_in("ret_decay", [DEPTH, 8])
        cs_d = self.dram_in("rot_cs", [T, 128])
        nsc_d = self.dram_in("rot_nsc", [T, 128])
        expo_d = self.dram_in("expo", [128, 16])
        wsel_d = self.dram_in("wsel", [128, 8])
        biasC_d = self.dram_in("biasC", [6, 128, DIL_STRIP])
        lmC_d = self.dram_in("lmC", [128, DIL_STRIP])
        vcol_d = self.dram_in("vcolC", [128, 24])
        y = self.dram_out("y", [T, D])
        KmT = self.sb("KmT", [128, 4, 256], BF16)
        Vm = self.sb("Vm", [128, 2, 512], BF16)
        self.wsel = self.sb("wsel", [128, 8], F32, dma=True)
        self.dma("sp", self.wsel, self.wsel.ap, wsel_d)
        xscr = self.internal("xscr", [T, D], F32)
        x_d, x_t = x_in, None
        for l in range(nlayers):
            pieces = {}
            for nm, shapes in (("ka", ((512, T), (256, T))), ("kc", ((512, T), (256, T))), ("va", ((T, 384), (T, 384))), ("vc", ((T, 384), (T, 384)))):
                pp = []
                for i, (rws, cls) in enumerate(shapes):
                    sname = "s_%s%d" % (nm, i)
                    sap = nc.dram_tensor("%s_%d" % (sname, l), [rws, cls], BF16).ap()
                    rap = nc.dram_tensor("r_%s%d_%d" % (nm, i, l), [4 * rws, cls], BF16).ap()
                    pp.append((Tn(sap, Buf(sname, self.new_dsem(sname))), rap))
                pieces[nm] = tuple(pp)
            send_L = nc.dram_tensor("send_L%d" % l, [1024, 256], F32).ap()
            recv_L = nc.dram_tensor("recv_L%d" % l, [4096, 256], F32).ap()
            o_L = Tn(send_L.rearrange("(d h p) v -> d h p v", d=2, h=4), Buf("s_L", self.new_dsem("s_L")))
            self.arena_reset()
            outs = ((pieces["ka"][0][0], pieces["ka"][1][0]), (pieces["kc"][0][0], pieces["kc"][1][0]),
                    (pieces["va"][0][0], pieces["va"][1][0]), (pieces["vc"][0][0], pieces["vc"][1][0]), o_L)
            def mid_cb(stage, pieces=pieces):
                for nm in (("ka", "va") if stage == 0 else ("kc", "vc")):
                    for (st_, rap) in pieces[nm]:
                        self.collective([st_], st_.ap, rap)
            if OVERLAP_CC:
                self.part_A(x_d, x_t, g_mix[l], w_in[l], dec[l], cs_d, nsc_d, outs, mid_cb=mid_cb)
            else:
                self.part_A(x_d, x_t, g_mix[l], w_in[l], dec[l], cs_d, nsc_d, outs)
                mid_cb(0)
                mid_cb(1)
            self.collective([o_L], send_L, recv_L)
            self.arena_reset()
            halos = self.assemble_halos(l, pieces)
            last = (l == nlayers - 1)
            a = dict(x_d=x_d, x_t=x_t, g_mix=g_mix[l], g_cross=g_cross[l], g_mem=g_mem[l], g_mlp=g_mlp[l], w_in=w_in[l], w_branch=w_branch[l],
                     w_out=w_out[l], w_cq=w_cq[l], w_ckv=w_ckv[l], w_co=w_co[l], w_mlp1=w_mlp1[l], w_mlp2=w_mlp2[l], mem_d=mem_d, dec_d=dec[l],
                     cs_d=cs_d, nsc_d=nsc_d, expo_d=expo_d, biasA_d=biasA_d[l], biasC_d=biasC_d, lmC_d=lmC_d, vcol_d=vcol_d,
                     kaT_d=halos[0].ap, va_d=halos[1].ap, kcT_d=halos[2].ap, vc_d=halos[3].ap, halo_t=tuple(halos) + self.halo_alias,
                     Lall_d=recv_L.rearrange("(j d h p) v -> d j h p v", j=4, d=2, h=4), L_queue="pool",
                     x_out=(y if last else xscr), KmT=KmT, Vm=Vm)
            self.arena_reset()
            self.part_B(a, norm1=False, final_g=(g_final if last else None))
            x_d, x_t = xscr.ap, xscr

    def finish(self):
        nc = self.nc
        P = self.P
        P.finalize()
        esems = {en: self.es.enter_context(nc.semaphore("s_" + en)) for en in ENGS}
        self.cc_sem = self.es.enter_context(nc.semaphore("s_cc"))
        for d in P.dsems:
            if d.count > 0:
                d.handle = self.es.enter_context(nc.semaphore(d.name))
        with nc.Block() as block:
            @block.tensor
            def _(t):
                P.emit_engine("pe", t, esems)

            @block.scalar
            def _(a):
                P.emit_engine("act", a, esems)

            @block.vector
            def _(v):
                P.emit_engine("dve", v, esems)

            @block.gpsimd
            def _(g):
                P.emit_engine("pool", g, esems)

            @block.sync
            def _(s):
                P.emit_engine("sp", s, esems)
        self.es.close()
        return nc


CST_PF, CST_PB, CST_UF, CST_UB, CST_N1, CST_N2 = 0, 128, 256, 384, 512, 640
CST_M1, CST_M2, CST_ZE, CST_ZB = 768, 769, 770, 778
CST_N = 786


def make_cst():
    c = np.zeros((128, CST_N), np.float32)
    m = np.arange(128)[:, None].astype(np.float32)
    n = np.arange(128)[None, :].astype(np.float32)
    c[:, CST_PF:CST_PF + 128] = np.maximum(n - m, 0)
    c[:, CST_PB:CST_PB + 128] = np.maximum(m - n, 0)
    c[:, CST_UF:CST_UF + 128] = (n >= m)
    c[:, CST_UB:CST_UB + 128] = (m > n)
    c[:, CST_N1:CST_N1 + 128] = n + 1 + 0 * m
    c[:, CST_N2:CST_N2 + 128] = 127 - n + 0 * m
    c[:, CST_M1] = m[:, 0] + 1
    c[:, CST_M2] = 127 - m[:, 0]
    tt = np.arange(8)[None, :].astype(np.float32)
    c[:, CST_ZE:CST_ZE + 8] = 1023 - (tt * 128 + m)
    c[:, CST_ZB:CST_ZB + 8] = tt * 128 + m + 1
    return c


def rot_tables(j):
    d = 128
    inv_freq = (10000.0 ** (-np.arange(0, d, 2, dtype=np.float32) / d)).astype(np.float32)
    pos = (np.arange(T, dtype=np.float32) + np.float32(j * T))
    ang = pos[:, None] * inv_freq[None, :]
    cos, sin = np.cos(ang).astype(np.float32), np.sin(ang).astype(np.float32)
    cs = np.concatenate([cos, sin], axis=1)
    nsc = np.concatenate([-sin, cos], axis=1)
    return np.ascontiguousarray(cs), np.ascontiguousarray(nsc)


_PROG_CACHE = {}


def get_prog(mode, stop=None):
    key = (mode, stop)
    if key not in _PROG_CACHE:
        b = Builder(mode)
        if mode == "FUSED":
            b.build_fused(nlayers=(stop if stop is not None else DEPTH))
        elif mode == "A":
            b.build_A()
        elif mode == "B":
            b.build_B(stop=stop)
        else:
            b.build_F()
        _PROG_CACHE[key] = b.finish()
        _PROG_INPUTS[id(_PROG_CACHE[key])] = list(b.din.keys())
    return _PROG_CACHE[key]


def t5_bucket(rel):
    nb = 16
    ret = (rel > 0).astype(np.int32) * nb
    n = np.abs(rel)
    max_exact = nb // 2
    large = max_exact + (np.log(np.maximum(n, 1) / max_exact) / np.log(1024 / max_exact) * (nb - max_exact)).astype(np.int32)
    large = np.minimum(large, nb - 1)
    return (ret + np.where(n < max_exact, n, large)).astype(np.int32)


_IDX_CACHE = {}


def na_index(j):
    if ("na", j) not in _IDX_CACHE:
        qg = np.arange(2)[:, None, None, None]
        r = np.arange(8)[None, :, None, None]
        k = np.arange(128)[None, None, :, None]
        q = np.arange(512)[None, None, None, :]
        tk = 1024 * j - 256 + 128 * (4 * qg + r) + k
        tq = 1024 * j + 512 * qg + q + 0 * k
        inseq = (tk >= 0) & (tk < SEQ)
        rk, ck = tk // 64, tk % 64
        rq, cq = tq // 64, tq % 64
        r0 = np.clip(rq - 4, 0, 56)
        c0 = np.clip(cq - 8, 0, 48)
        valid = inseq & (rk >= r0) & (rk < r0 + 8) & (ck >= c0) & (ck < c0 + 16)
        ri = np.clip(rk - rq + 7, 0, 14)
        ci = np.clip(ck - cq + 15, 0, 30)
        _IDX_CACHE[("na", j)] = (ri.reshape(16, 128, 512), ci.reshape(16, 128, 512), valid.reshape(16, 128, 512))
    return _IDX_CACHE[("na", j)]


def dil_index():
    if "dil" not in _IDX_CACHE:
        k = np.arange(128)[:, None]
        c = np.arange(DIL_STRIP)[None, :]
        off = 1408 + k - c
        a = np.abs(off)
        mult = (a <= 64).astype(np.int32) + ((a <= 256) & (off % 4 == 0)) + ((a <= 1024) & (off % 16 == 0))
        bucket = t5_bucket(off)
        lm = np.where(mult > 0, np.log(np.maximum(mult, 1)), 0.0).astype(np.float32)
        _IDX_CACHE["dil"] = (bucket, mult > 0, np.ascontiguousarray(lm))
    return _IDX_CACHE["dil"]


def dil_bias(t5):
    bucket, dvalid, lmC = dil_index()
    biasC = np.where(dvalid[None], np.transpose(t5[bucket], (2, 0, 1)), np.float32(NEG)).astype(np.float32)
    return np.ascontiguousarray(biasC), lmC


def halo(arr, axis, start, length):
    n = arr.shape[axis]
    lo, hi = max(start, 0), min(start + length, n)
    shp = list(arr.shape)
    shp[axis] = length
    out = np.zeros(shp, arr.dtype)
    sl_o = [slice(None)] * arr.ndim
    sl_i = [slice(None)] * arr.ndim
    sl_o[axis] = slice(lo - start, hi - start)
    sl_i[axis] = slice(lo, hi)
    out[tuple(sl_o)] = arr[tuple(sl_i)]
    return out


def run_B(x_chunks, resA, inputs, l, stop=None):
    nc = get_prog("B", stop)
    cst = make_cst()
    ident = np.eye(128, dtype=np.float32)
    biasC, lmC = dil_bias(inputs["t5_bias"])
    W = {k: np.ascontiguousarray(inputs[k][l]) for k in ("w_in", "w_branch", "w_out", "w_cq", "w_ckv", "w_co", "w_mlp1", "w_mlp2",
                                                          "norm_mix_g", "norm_cross_g", "norm_mem_g", "norm_mlp_g")}
    rpb = inputs["na_rpb"][l]
    maps = []
    for c in range(NCORES):
        b, j = c // 4, c % 4
        grp = [resA[b * 4 + jj] for jj in range(4)]
        kaT = np.concatenate([g["kaT"] for g in grp], axis=2)
        kcT = np.concatenate([g["kcT"] for g in grp], axis=2)
        va = np.concatenate([g["va"] for g in grp], axis=0)
        vc = np.concatenate([g["vc"] for g in grp], axis=0)
        Lall = np.ascontiguousarray(np.stack([g["L"] for g in grp], axis=1))
        ri, ci, valid = na_index(j)
        biasA = np.where(valid[None], rpb[:, ri, ci], np.float32(NEG)).astype(np.float32)
        cs, nsc = rot_tables(j)
        expo = np.zeros((128, 16), np.float32)
        for jj in range(4):
            if jj < j:
                expo[:, jj] = 1024.0 * (j - 1 - jj)
                expo[:, 4 + jj] = 1.0
            if jj > j:
                expo[:, 8 + jj] = 1024.0 * (jj - j - 1)
                expo[:, 12 + jj] = 1.0
        vcol = np.zeros((128, 24), np.float32)
        tok = 1024 * j - 1024 + 128 * np.arange(24)[None, :] + np.arange(128)[:, None]
        vcol[(tok < 0) | (tok >= SEQ)] = NEG
        maps.append({
            "x": x_chunks[c], "g_mix": W["norm_mix_g"], "g_cross": W["norm_cross_g"], "g_mem": W["norm_mem_g"], "g_mlp": W["norm_mlp_g"],
            "w_in": W["w_in"], "w_branch": W["w_branch"], "w_out": W["w_out"], "w_cq": W["w_cq"], "w_ckv": W["w_ckv"], "w_co": W["w_co"],
            "w_mlp1": W["w_mlp1"], "w_mlp2": W["w_mlp2"], "mem": np.ascontiguousarray(inputs["mem"][b]),
            "ret_decay": np.ascontiguousarray(inputs["ret_decay"][l].reshape(8)), "rot_cs": cs, "rot_nsc": nsc, "expo": expo,
            "biasA": biasA, "biasC": biasC, "lmC": lmC, "vcolC": vcol,
            "kaT_h": halo(kaT, 2, 1024 * j - 256, 1536), "va_h": halo(va, 0, 1024 * j - 256, 1536),
            "kcT_h": halo(kcT, 2, 1024 * j - 1024, 3072), "vc_h": halo(vc, 0, 1024 * j - 1024, 3072),
            "Lall": Lall, "ident": ident, "cst": cst,
        })
    needed = set(nc_input_names(nc))
    maps = [{k: v for k, v in m.items() if k in needed} for m in maps]
    res = run_bass_kernel_spmd(nc, maps, core_ids=list(range(NCORES)))
    return res.results


def nc_input_names(nc):
    return _PROG_INPUTS[id(nc)]


_PROG_INPUTS = {}


def run_A(x_chunks, inputs, l):
    nc = get_prog("A")
    cst = make_cst()
    ident = np.eye(128, dtype=np.float32)
    maps = []
    for c in range(NCORES):
        cs, nsc = rot_tables(c % 4)
        maps.append({"x": x_chunks[c], "g_mix": np.ascontiguousarray(inputs["norm_mix_g"][l]),
                     "w_in": np.ascontiguousarray(inputs["w_in"][l]),
                     "ret_decay": np.ascontiguousarray(inputs["ret_decay"][l].reshape(8)),
                     "rot_cs": cs, "rot_nsc": nsc, "ident": ident, "cst": cst})
    res = run_bass_kernel_spmd(nc, maps, core_ids=list(range(NCORES)))
    return res.results


def run_F(x_chunks, inputs):
    nc = get_prog("F")
    cst = make_cst()
    ident = np.eye(128, dtype=np.float32)
    g = np.ascontiguousarray(inputs["final_norm_g"])
    maps = [{"x": x_chunks[c], "g_final": g, "ident": ident, "cst": cst} for c in range(NCORES)]
    res = run_bass_kernel_spmd(nc, maps, core_ids=list(range(NCORES)))
    return res.results


def fused_maps(inputs, nlayers=DEPTH):
    cst = make_cst()
    ident = np.eye(128, dtype=np.float32)
    biasC, lmC = dil_bias(inputs["t5_bias"])
    shared = {k: np.ascontiguousarray(inputs[k]) for k in ("norm_mix_g", "norm_cross_g", "norm_mem_g", "norm_mlp_g", "final_norm_g")}
    for k in ("w_in", "w_branch", "w_out", "w_cq", "w_ckv", "w_co", "w_mlp1", "w_mlp2"):
        for l in range(nlayers):
            shared["%s_%d" % (k, l)] = np.ascontiguousarray(inputs[k][l])
    shared["ret_decay"] = np.ascontiguousarray(inputs["ret_decay"].reshape(DEPTH, 8))
    rpb = inputs["na_rpb"]
    biasA_j = []
    for j in range(4):
        ri, ci, valid = na_index(j)
        biasA_j.append(np.where(valid[None, None], rpb[:, :, ri, ci], np.float32(NEG)).astype(np.float32))
    x = inputs["x"]
    maps = []
    for c in range(NCORES):
        b, j = c // 4, c % 4
        cs, nsc = rot_tables(j)
        expo = np.zeros((128, 16), np.float32)
        wsel = np.zeros((128, 8), np.float32)
        for jj in range(4):
            if jj < j:
                expo[:, jj] = 1024.0 * (j - 1 - jj)
                expo[:, 4 + jj] = 1.0
            if jj > j:
                expo[:, 8 + jj] = 1024.0 * (jj - j - 1)
                expo[:, 12 + jj] = 1.0
        if j > 0:
            wsel[:, j - 1] = 1.0
        if j < 3:
            wsel[:, 4 + j + 1] = 1.0
        vcol = np.zeros((128, 24), np.float32)
        tok = 1024 * j - 1024 + 128 * np.arange(24)[None, :] + np.arange(128)[:, None]
        vcol[(tok < 0) | (tok >= SEQ)] = NEG
        m = dict(shared)
        m.update({"x": np.ascontiguousarray(x[b, j * T:(j + 1) * T]), "mem": np.ascontiguousarray(inputs["mem"][b]),
                  "rot_cs": cs, "rot_nsc": nsc, "expo": expo, "wsel": wsel, "biasC": biasC, "lmC": lmC,
                  "vcolC": vcol, "ident": ident, "cst": cst})
        for l in range(nlayers):
            m["biasA_%d" % l] = np.ascontiguousarray(biasA_j[j][l])
        maps.append(m)
    return maps


def kernel_fused(inputs, nlayers=None):
    nc = get_prog("FUSED", nlayers)
    maps = fused_maps(inputs, nlayers if nlayers is not None else DEPTH)
    res = run_bass_kernel_spmd(nc, maps, core_ids=list(range(NCORES)))
    out = np.zeros((2, SEQ, D), np.float32)
    for c in range(NCORES):
        out[c // 4, (c % 4) * T:(c % 4 + 1) * T] = np.asarray(res.results[c]["y"])
    return out


def kernel(**inputs):
    inputs = {k: np.asarray(v) for k, v in inputs.items()}
    return kernel_fused(inputs)


def kernel_unfused(**inputs):
    inputs = {k: np.asarray(v) for k, v in inputs.items()}
    x = inputs["x"].astype(np.float32, copy=False)
    xch = [np.ascontiguousarray(x[c // 4, (c % 4) * T:(c % 4 + 1) * T]) for c in range(NCORES)]
    for l in range(DEPTH):
        resA = run_A(xch, inputs, l)
        resA = [{k: np.asarray(v) for k, v in r.items()} for r in resA]
        resB = run_B(xch, resA, inputs, l)
        xch = [np.ascontiguousarray(np.asarray(r["x_out"])) for r in resB]
    resF = run_F(xch, inputs)
    out = np.zeros((2, SEQ, D), np.float32)
    for c in range(NCORES):
        out[c // 4, (c % 4) * T:(c % 4 + 1) * T] = np.asarray(resF[c]["y"])
    return out
```
